# Optimizing a Trainium2 kernel written in Bass

```python
import numpy as np
import jax
import jax.numpy as jnp
from jax import lax

D_MODEL = 1024
BATCH = 8
SEQ = 2048
DEPTH = 1

RMS_EPS = 1e-6
N_MOD = 6
ROPE_THETA = 500000.0
NEG_INF = -1e30

GLA_HEADS = 4
GLA_DK = D_MODEL // (2 * GLA_HEADS)
GLA_DV = D_MODEL // GLA_HEADS
GLA_RANK = 16
GLA_TAU = 16.0
GLA_CHUNK = 64

NSA_HEADS = 16
NSA_HD = D_MODEL // NSA_HEADS
NSA_GROUPS = 2
NSA_HPG = NSA_HEADS // NSA_GROUPS
ROT_DIM = NSA_HD // 4
CMP_BLOCK = 32
CMP_STRIDE = 16
SEL_BLOCK = 64
SEL_TOPK = 16
WINDOW = 512
ATT_QBLOCK = 128
SEL_QBLOCK = 32
FORCED_SCORE = 1e6

PEER_HEADS = 8
PEER_NKEYS = 128
PEER_NEXPERTS = PEER_NKEYS * PEER_NKEYS
PEER_QDIM = 256
PEER_TOPK = 16
PEER_TOKEN_BLOCK = 128

GLA_QK = GLA_HEADS * GLA_DK
GLA_V = GLA_HEADS * GLA_DV
NSA_Q = NSA_HEADS * NSA_HD
NSA_KV = NSA_GROUPS * NSA_HD
IN_SIZES = (GLA_QK, GLA_QK, GLA_V, GLA_V, GLA_RANK,
            NSA_Q, NSA_KV, NSA_KV, NSA_KV, NSA_KV, NSA_KV, NSA_KV, 3 * NSA_HEADS,
            D_MODEL, D_MODEL)
IN_COLS = sum(IN_SIZES)
IN_OFFSETS = tuple(int(o) for o in np.cumsum(IN_SIZES)[:-1])

kernel_name = 'hybrid_gla_nsa_peer_adaln_block'


def rmsnorm(x, g):
    xf = x.astype(jnp.float32)
    y = xf * lax.rsqrt(jnp.mean(xf * xf, axis=-1, keepdims=True) + RMS_EPS)
    return (y * g.astype(jnp.float32)).astype(x.dtype)


def masked_softmax(s, mask):
    return jax.nn.softmax(jnp.where(mask, s.astype(jnp.float32), NEG_INF), axis=-1)


def rope_partial(t, positions):
    half = ROT_DIM // 2
    inv_freq = jnp.asarray(ROPE_THETA ** (-np.arange(half, dtype=np.float32) * 2.0 / ROT_DIM), dtype=jnp.float32)
    ang = positions.astype(jnp.float32)[..., None] * inv_freq
    cos = jnp.cos(ang)[:, :, None, :]
    sin = jnp.sin(ang)[:, :, None, :]
    tf = t.astype(jnp.float32)
    x1 = tf[..., :half]
    x2 = tf[..., half:ROT_DIM]
    out = jnp.concatenate([x1 * cos - x2 * sin, x2 * cos + x1 * sin, tf[..., ROT_DIM:]], axis=-1)
    return out.astype(t.dtype)


def gla_mixer(q, k, v, r, lr, wa2, ba2, norm_g):
    B, S = q.shape[0], q.shape[1]
    H, DK, DV, C = GLA_HEADS, GLA_DK, GLA_DV, GLA_CHUNK
    nc = S // C
    log_a = jax.nn.log_sigmoid((lr @ wa2 + ba2).astype(jnp.float32)) / GLA_TAU

    def chunks(t, d):
        return t.astype(jnp.float32).reshape(B, nc, C, H, d).transpose(1, 0, 3, 2, 4)

    qs = chunks(q, DK) * (DK ** -0.5)
    kcs = chunks(k, DK)
    vcs = chunks(v, DV)
    acs = chunks(log_a, DK)
    causal = jnp.asarray(np.tril(np.ones((C, C), dtype=bool)))[:, :, None]

    def step(state, xs):
        qc, kc, vc, ac = xs
        b = jnp.cumsum(ac, axis=2)
        diff = b[:, :, :, None, :] - b[:, :, None, :, :]
        decay = jnp.exp(jnp.where(causal, diff, -jnp.inf))
        scores = jnp.einsum('bhtd,bhsd,bhtsd->bhts', qc, kc, decay)
        o = (jnp.einsum('bhts,bhsv->bhtv', scores, vc)
             + jnp.einsum('bhtd,bhdv->bhtv', qc * jnp.exp(b), state))
        b_last = b[:, :, -1:, :]
        state = (jnp.exp(b_last[:, :, 0, :, None]) * state
                 + jnp.einsum('bhsd,bhsv->bhdv', kc * jnp.exp(b_last - b), vc))
        return state, o

    state0 = jnp.zeros((B, H, DK, DV), jnp.float32)
    _, o = lax.scan(step, state0, (qs, kcs, vcs, acs))
    o = o.transpose(1, 0, 3, 2, 4).reshape(B, S, H, DV)
    o = rmsnorm(o, norm_g) * jax.nn.silu(r.astype(jnp.float32).reshape(B, S, H, DV))
    return o.reshape(B, S, H * DV).astype(q.dtype)


def nsa_compressed(q, kc, vc, pe_k, pe_v, ck_w1, ck_w2, cv_w1, cv_w2):
    B, S = q.shape[0], q.shape[1]
    nc = (S - CMP_BLOCK) // CMP_STRIDE + 1
    idx = np.arange(nc)[:, None] * CMP_STRIDE + np.arange(CMP_BLOCK)[None, :]

    def compress(t, pe, w1, w2):
        blk = t[:, idx] + pe[:, None, :]
        blk = blk.transpose(0, 1, 3, 2, 4).reshape(B, nc, NSA_GROUPS, CMP_BLOCK * NSA_HD)
        return jax.nn.gelu(blk @ w1, approximate=False) @ w2

    k_cmp = compress(kc, pe_k, ck_w1, ck_w2)
    v_cmp = compress(vc, pe_v, cv_w1, cv_w2)
    t = np.arange(S)
    blk_end = np.arange(nc) * CMP_STRIDE + CMP_BLOCK - 1
    valid = blk_end[None, :] <= t[:, None]
    has_any = jnp.asarray(valid.any(axis=-1).astype(np.float32))[:, None]
    valid = jnp.asarray(valid)
    s = jnp.einsum('bsghd,bngd->bghsn', q, k_cmp)
    p = masked_softmax(s, valid) * has_any
    o = jnp.einsum('bghsn,bngd->bsghd', p.astype(v_cmp.dtype), v_cmp)
    return o, p.sum(axis=2)


def nsa_select_blocks(imp, S):
    nc = imp.shape[-1]
    ns = S // SEL_BLOCK
    i = np.arange(nc)[:, None]
    j = np.arange(ns)[None, :]
    overlap = ((i * CMP_STRIDE < (j + 1) * SEL_BLOCK) & (i * CMP_STRIDE + CMP_BLOCK > j * SEL_BLOCK))
    score = jnp.einsum('bgsn,nj->bgsj', imp, jnp.asarray(overlap.astype(np.float32)))
    cur = (np.arange(S) // SEL_BLOCK)[:, None]
    forced = jnp.asarray((j == 0) | (j == cur) | (j == cur - 1))
    valid = jnp.asarray(j <= cur)
    score = jnp.where(forced, FORCED_SCORE, jnp.where(valid, score, -1.0))
    _, sel = lax.top_k(score, min(SEL_TOPK, ns))
    return sel


def nsa_selected(q, ks, vs, sel):
    B, S = q.shape[0], q.shape[1]
    G, HPG, HD, L = NSA_GROUPS, NSA_HPG, NSA_HD, SEL_BLOCK
    ns = S // L
    n = sel.shape[-1]
    nq = S // SEL_QBLOCK
    ks_blk = ks.reshape(B, ns, L, G, HD).transpose(0, 3, 1, 2, 4)
    vs_blk = vs.reshape(B, ns, L, G, HD).transpose(0, 3, 1, 2, 4)
    q_ch = q.reshape(B, nq, SEL_QBLOCK, G, HPG, HD).transpose(1, 0, 2, 3, 4, 5)
    sel_ch = sel.reshape(B, G, nq, SEL_QBLOCK, n).transpose(2, 0, 1, 3, 4)
    pos_ch = jnp.arange(S).reshape(nq, SEL_QBLOCK)
    bi = jnp.arange(B)[:, None, None, None]
    gi = jnp.arange(G)[None, :, None, None]

    def body(args):
        qc, ic, tc = args
        kg = ks_blk[bi, gi, ic]
        vg = vs_blk[bi, gi, ic]
        kpos = ic[..., None] * L + jnp.arange(L)
        mask = (kpos <= tc[None, None, :, None, None]).reshape(B, G, 1, SEL_QBLOCK, n * L)
        s = jnp.einsum('bqghd,bgqnld->bghqnl', qc, kg).reshape(B, G, HPG, SEL_QBLOCK, n * L)
        p = masked_softmax(s, mask)
        return jnp.einsum('bghqm,bgqmd->bqghd', p.astype(vg.dtype), vg.reshape(B, G, SEL_QBLOCK, n * L, HD))

    out = lax.map(body, (q_ch, sel_ch, pos_ch))
    return out.transpose(1, 0, 2, 3, 4, 5).reshape(B, S, G, HPG, HD)


def nsa_window(q, kw, vw):
    B, S = q.shape[0], q.shape[1]
    G, HPG, HD, QB = NSA_GROUPS, NSA_HPG, NSA_HD, ATT_QBLOCK
    nb = S // QB
    span = QB + WINDOW
    pad = ((0, 0), (WINDOW, 0), (0, 0), (0, 0))
    kp = jnp.pad(kw, pad)
    vp = jnp.pad(vw, pad)
    q_bl = q.reshape(B, nb, QB, G, HPG, HD).transpose(1, 0, 2, 3, 4, 5)

    def body(args):
        qb, i = args
        start = i * QB
        kb = lax.dynamic_slice_in_dim(kp, start, span, axis=1)
        vb = lax.dynamic_slice_in_dim(vp, start, span, axis=1)
        qpos = start + jnp.arange(QB)
        kpos = start - WINDOW + jnp.arange(span)
        dist = qpos[:, None] - kpos[None, :]
        mask = (dist >= 0) & (dist < WINDOW) & (kpos[None, :] >= 0)
        s = jnp.einsum('bqghd,bkgd->bghqk', qb, kb)
        p = masked_softmax(s, mask)
        return jnp.einsum('bghqk,bkgd->bqghd', p.astype(vb.dtype), vb)

    out = lax.map(body, (q_bl, jnp.arange(nb)))
    return out.transpose(1, 0, 2, 3, 4, 5).reshape(B, S, G, HPG, HD)


def nsa_mixer(q, kc, vc, ks, vs, kw, vw, gates, positions, pe_k, pe_v, ck_w1, ck_w2, cv_w1, cv_w2):
    B, S = q.shape[0], q.shape[1]
    G, HPG, HD = NSA_GROUPS, NSA_HPG, NSA_HD
    q = (rope_partial(q.reshape(B, S, NSA_HEADS, HD), positions) * (HD ** -0.5)).reshape(B, S, G, HPG, HD)
    kc = rope_partial(kc.reshape(B, S, G, HD), positions)
    ks = rope_partial(ks.reshape(B, S, G, HD), positions)
    kw = rope_partial(kw.reshape(B, S, G, HD), positions)
    vc = vc.reshape(B, S, G, HD)
    vs = vs.reshape(B, S, G, HD)
    vw = vw.reshape(B, S, G, HD)
    o_cmp, imp = nsa_compressed(q, kc, vc, pe_k, pe_v, ck_w1, ck_w2, cv_w1, cv_w2)
    sel = nsa_select_blocks(imp, S)
    o_sel = nsa_selected(q, ks, vs, sel)
    o_win = nsa_window(q, kw, vw)
    g = jax.nn.sigmoid(gates.astype(jnp.float32)).reshape(B, S, 3, G, HPG)[..., None]
    o = g[:, :, 0] * o_cmp + g[:, :, 1] * o_sel + g[:, :, 2] * o_win
    return o.reshape(B, S, NSA_HEADS * HD).astype(q.dtype)


def hybrid_mixer(h, positions, w_in, gla_wa2, gla_ba2, gla_norm_g, nsa_pe_k, nsa_pe_v,
                 nsa_ck_w1, nsa_ck_w2, nsa_cv_w1, nsa_cv_w2, w_branch_a, w_branch_b, w_out):
    z = h @ w_in
    (g_q, g_k, g_v, g_r, g_lr, n_q, n_kc, n_vc, n_ks, n_vs, n_kw, n_vw, n_gate,
     merge_a, merge_b) = jnp.split(z, IN_OFFSETS, axis=-1)
    y_a = gla_mixer(g_q, g_k, g_v, g_r, g_lr, gla_wa2, gla_ba2, gla_norm_g)
    y_b = nsa_mixer(n_q, n_kc, n_vc, n_ks, n_vs, n_kw, n_vw, n_gate, positions,
                    nsa_pe_k, nsa_pe_v, nsa_ck_w1, nsa_ck_w2, nsa_cv_w1, nsa_cv_w2)
    m = jax.nn.sigmoid(merge_a) * (y_a @ w_branch_a) + jax.nn.sigmoid(merge_b) * (y_b @ w_branch_b)
    return m @ w_out


def peer_ffn(h, wq, k1, k2, u, v):
    B, S, D = h.shape
    T = B * S
    half = PEER_QDIM // 2
    hf = h.reshape(T, D)
    q = (hf @ wq).reshape(T, PEER_HEADS, PEER_QDIM)
    s1 = jnp.einsum('thd,hnd->thn', q[..., :half], k1).astype(jnp.float32)
    s2 = jnp.einsum('thd,hnd->thn', q[..., half:], k2).astype(jnp.float32)
    v1, i1 = lax.top_k(s1, PEER_TOPK)
    v2, i2 = lax.top_k(s2, PEER_TOPK)
    cand = (v1[..., :, None] + v2[..., None, :]).reshape(T, PEER_HEADS, PEER_TOPK * PEER_TOPK)
    cidx = (i1[..., :, None] * PEER_NKEYS + i2[..., None, :]).reshape(T, PEER_HEADS, PEER_TOPK * PEER_TOPK)
    top_s, pos = lax.top_k(cand, PEER_TOPK)
    eidx = jnp.take_along_axis(cidx, pos, axis=-1)
    gw = jax.nn.softmax(top_s, axis=-1)
    nt = T // PEER_TOKEN_BLOCK

    def body(args):
        hc, ec, gc = args
        a = jax.nn.gelu(jnp.einsum('td,thkd->thk', hc, u[ec]).astype(jnp.float32), approximate=False)
        w = (gc * a).astype(hc.dtype)
        return jnp.einsum('thk,thkd->td', w, v[ec])

    out = lax.map(body, (hf.reshape(nt, PEER_TOKEN_BLOCK, D),
                         eidx.reshape(nt, PEER_TOKEN_BLOCK, PEER_HEADS, PEER_TOPK),
                         gw.reshape(nt, PEER_TOKEN_BLOCK, PEER_HEADS, PEER_TOPK)))
    return out.reshape(B, S, D)


def setup_inputs(seed: int = 0) -> dict:
    key = jax.random.key(seed)
    ks = jax.random.split(key, 32)
    f32 = jnp.float32

    def nrm(k, shape, scale):
        return jax.random.normal(k, shape, f32) * scale

    L, D = DEPTH, D_MODEL
    x = nrm(ks[0], (BATCH, SEQ, D), 1.0)
    c = nrm(ks[1], (BATCH, D), 1.0)
    positions = (jnp.arange(SEQ, dtype=jnp.int32)[None, :]
                 + jax.random.randint(ks[2], (BATCH, 1), 0, 4096, dtype=jnp.int32))
    return {
        'x': x,
        'c': c,
        'positions': positions,
        'ada_w': nrm(ks[3], (L, D, N_MOD * D), 0.5 * D ** -0.5),
        'ada_b': nrm(ks[4], (L, N_MOD * D), 0.02),
        'norm1_g': 1.0 + nrm(ks[5], (L, D), 0.02),
        'norm2_g': 1.0 + nrm(ks[6], (L, D), 0.02),
        'final_g': 1.0 + nrm(ks[7], (D,), 0.02),
        'w_in': nrm(ks[8], (L, D, IN_COLS), D ** -0.5),
        'gla_wa2': nrm(ks[9], (L, GLA_RANK, GLA_QK), GLA_RANK ** -0.5),
        'gla_ba2': nrm(ks[10], (L, GLA_QK), 0.1),
        'gla_norm_g': 1.0 + nrm(ks[11], (L, GLA_DV), 0.02),
        'nsa_pe_k': nrm(ks[12], (L, CMP_BLOCK, NSA_HD), 0.02),
        'nsa_pe_v': nrm(ks[13], (L, CMP_BLOCK, NSA_HD), 0.02),
        'nsa_ck_w1': nrm(ks[14], (L, CMP_BLOCK * NSA_HD, NSA_HD), (CMP_BLOCK * NSA_HD) ** -0.5),
        'nsa_ck_w2': nrm(ks[15], (L, NSA_HD, NSA_HD), NSA_HD ** -0.5),
        'nsa_cv_w1': nrm(ks[16], (L, CMP_BLOCK * NSA_HD, NSA_HD), (CMP_BLOCK * NSA_HD) ** -0.5),
        'nsa_cv_w2': nrm(ks[17], (L, NSA_HD, NSA_HD), NSA_HD ** -0.5),
        'w_branch_a': nrm(ks[18], (L, GLA_V, D), GLA_V ** -0.5),
        'w_branch_b': nrm(ks[19], (L, NSA_Q, D), NSA_Q ** -0.5),
        'w_out': nrm(ks[20], (L, D, D), D ** -0.5),
        'peer_wq': nrm(ks[21], (L, D, PEER_HEADS * PEER_QDIM), D ** -0.5),
        'peer_k1': nrm(ks[22], (L, PEER_HEADS, PEER_NKEYS, PEER_QDIM // 2), (PEER_QDIM // 2) ** -0.5),
        'peer_k2': nrm(ks[23], (L, PEER_HEADS, PEER_NKEYS, PEER_QDIM // 2), (PEER_QDIM // 2) ** -0.5),
        'peer_u': nrm(ks[24], (L, PEER_NEXPERTS, D), D ** -0.5),
        'peer_v': nrm(ks[25], (L, PEER_NEXPERTS, D), 0.1),
    }


def reference(x, c, positions, ada_w, ada_b, norm1_g, norm2_g, final_g, w_in, gla_wa2, gla_ba2,
              gla_norm_g, nsa_pe_k, nsa_pe_v, nsa_ck_w1, nsa_ck_w2, nsa_cv_w1, nsa_cv_w2,
              w_branch_a, w_branch_b, w_out, peer_wq, peer_k1, peer_k2, peer_u, peer_v):
    B, D = c.shape
    for layer in range(DEPTH):
        mod = (jax.nn.silu(c) @ ada_w[layer] + ada_b[layer]).reshape(B, N_MOD, D)
        shift1, scale1, gate1, shift2, scale2, gate2 = [mod[:, i, None, :] for i in range(N_MOD)]
        h = rmsnorm(x, norm1_g[layer]) * (1.0 + scale1) + shift1
        x = x + gate1 * hybrid_mixer(h, positions, w_in[layer], gla_wa2[layer], gla_ba2[layer],
                                     gla_norm_g[layer], nsa_pe_k[layer], nsa_pe_v[layer],
                                     nsa_ck_w1[layer], nsa_ck_w2[layer], nsa_cv_w1[layer], nsa_cv_w2[layer],
                                     w_branch_a[layer], w_branch_b[layer], w_out[layer])
        h = rmsnorm(x, norm2_g[layer]) * (1.0 + scale2) + shift2
        x = x + gate2 * peer_ffn(h, peer_wq[layer], peer_k1[layer], peer_k2[layer], peer_u[layer], peer_v[layer])
    return rmsnorm(x, final_g)
```

```python
import contextlib
import numpy as np
import concourse.bass as bass
import concourse.mybir as mybir
from concourse.alu_op_type import AluOpType as ALU
from concourse.bass_utils import run_bass_kernel_spmd

AF = mybir.ActivationFunctionType
F32 = mybir.dt.float32
BF16 = mybir.dt.bfloat16
I32 = mybir.dt.int32
U32 = mybir.dt.uint32

S = 2048
D = 1024
NT = S // 128
EPS = 1e-6


class StopBuild(Exception):
    pass


def kstop(name):
    import os
    if os.environ.get('KSTOP', '') == name:
        raise StopBuild()


class KB:
    def __init__(self):
        self.nc = bass.Bass("TRN2", target_bir_lowering=False)
        nc = self.nc
        self.es = contextlib.ExitStack()
        self.eng = dict(pe=nc.tensor, act=nc.scalar, dve=nc.vector, pool=nc.gpsimd, sp=nc.sync)
        self.sem = {e: self.es.enter_context(nc.semaphore("s_" + e)) for e in self.eng}
        self.cnt = {e: 0 for e in self.eng}
        self.nslot = 10
        self.dsem = {q: [self.es.enter_context(nc.semaphore("d_%s%d" % (q, i))) for i in range(self.nslot)]
                     for q in ("sp", "pool", "act")}
        self.dval = {q: [0] * self.nslot for q in self.dsem}
        self.dnext = {q: 0 for q in self.dsem}
        self.waited = {}
        self.lastw = {}
        self.readers = {}
        self.ninst = 0

    def _wait(self, e, tok):
        sem, val, prod = tok
        if prod == "pe" and e == "pe":
            return
        key = (e, id(sem))
        if self.waited.get(key, 0) >= val:
            return
        self.eng[e].wait_ge(sem, val)
        self.waited[key] = val

    def _deps(self, e, reads, writes):
        for k in list(reads) + list(writes):
            t = self.lastw.get(k)
            if t is not None:
                self._wait(e, t)
        for k in reads:
            if isinstance(k, tuple) and k[0] == "ps":
                for t in self.readers.get(k, {}).values():
                    if t[2] != e:
                        self._wait(e, t)
        for k in writes:
            for t in self.readers.get(k, {}).values():
                self._wait(e, t)

    def _record(self, tok, reads, writes):
        for k in writes:
            self.lastw[k] = tok
            self.readers[k] = {}
        for k in reads:
            self.readers.setdefault(k, {})[id(tok[0])] = tok

    def op(self, e, fn, reads=(), writes=()):
        self._deps(e, reads, writes)
        inst = fn(self.eng[e])
        self.cnt[e] += 1
        inst.then_inc(self.sem[e], 1)
        tok = (self.sem[e], self.cnt[e], e)
        self._record(tok, reads, writes)
        self.ninst += 1
        return inst

    def dma(self, q, out, in_, reads=(), writes=(), **kw):
        slot = self.dnext[q]
        self.dnext[q] = (slot + 1) % self.nslot
        sem = self.dsem[q][slot]
        if self.dval[q][slot] > 0:
            self._wait(q, (sem, self.dval[q][slot], "dma"))
        self._deps(q, reads, writes)
        inst = self.eng[q].dma_start(out=out, in_=in_, **kw)
        self.dval[q][slot] += 16
        inst.then_inc(sem, 16)
        tok = (sem, self.dval[q][slot], "dma")
        self._record(tok, reads, writes)
        self.ninst += 1
        return tok

    def barrier(self):
        toks = [(self.sem[e], self.cnt[e], e) for e in self.eng if self.cnt[e] > 0]
        for q in self.dsem:
            for i in range(self.nslot):
                if self.dval[q][i] > 0:
                    toks.append((self.dsem[q][i], self.dval[q][i], "dma"))
        for e in self.eng:
            for t in toks:
                self._wait(e, t)
        self.lastw.clear()
        self.readers.clear()

    def final_wait(self, toks):
        for t in toks:
            self._wait("sp", t)


def build(dbg=()):
    kb = KB()
    nc = kb.nc
    es = kb.es
    dbg = set(dbg)

    def din(name, shape, dt=F32):
        return nc.dram_tensor(name, list(shape), dt, kind="ExternalInput").ap()

    def dout(name, shape, dt=F32):
        return nc.dram_tensor(name, list(shape), dt, kind="ExternalOutput").ap()

    uid = [0]

    def sb(stack, name, shape, dt):
        uid[0] += 1
        return stack.enter_context(nc.sbuf_tensor("sb%d_%s" % (uid[0], name), list(shape), dt))

    x_d = din("x", [S, D])
    c_d = din("c_l", [128, 8])
    adaw_d = din("ada_w", [D, 6 * D])
    adab_d = din("ada_b_l", [128, 48])
    g1_d = din("g1_l", [128, 8])
    g2_d = din("g2_l", [128, 8])
    gf_d = din("gf_l", [128, 8])
    ident_d = din("ident", [128, 128])
    out_d = dout("out", [S, D])
    dbg_t = {}
    OUT_TOKS = []

    def dbg_out(name, shape, dt=F32):
        dbg_t[name] = dout("dbg_" + name, shape, dt)
        return dbg_t[name]

    psb = [es.enter_context(nc.psum_tensor("ps%d" % i, [128, 512], F32)) for i in range(8)]
    ps_rr = [0]

    def ps_next():
        i = ps_rr[0]
        ps_rr[0] = (i + 1) % 6
        return i

    ps_acc = [0]

    def ps_next_acc():
        i = ps_acc[0]
        ps_acc[0] = (i + 1) % 2
        return 6 + i

    G = es
    ident_f = sb(G, "ident_f", [128, 128], F32)
    ident_b = sb(G, "ident_b", [128, 128], BF16)
    c_sb = sb(G, "c_sb", [128, 8], F32)
    sc_sb = sb(G, "sc_sb", [128, 8], F32)
    adab_sb = sb(G, "adab_sb", [128, 48], F32)
    mod = sb(G, "mod", [128, 48], F32)
    g1 = sb(G, "g1", [128, 8], F32)
    g2 = sb(G, "g2", [128, 8], F32)
    gf = sb(G, "gf", [128, 8], F32)
    gs1 = sb(G, "gs1", [128, 8], F32)
    gs2 = sb(G, "gs2", [128, 8], F32)
    h1T = sb(G, "h1T", [128, 8, S], BF16)

    kb.dma("sp", ident_f[:], ident_d, writes=["ident_f"])
    kb.dma("sp", c_sb[:], c_d, writes=["c"])
    kb.dma("sp", adab_sb[:], adab_d, writes=["adab"])
    kb.dma("sp", g1[:], g1_d, writes=["g1"])
    kb.dma("sp", g2[:], g2_d, writes=["g2"])
    kb.dma("sp", gf[:], gf_d, writes=["gf"])
    kb.op("dve", lambda e: e.tensor_copy(out=ident_b[:], in_=ident_f[:]), reads=["ident_f"], writes=["ident_b"])

    kb.op("act", lambda e: e.activation(out=sc_sb[:], in_=c_sb[:], func=AF.Silu), reads=["c"], writes=["sc"])
    with contextlib.ExitStack() as P1:
        wbuf = [sb(P1, "adaw%d" % i, [128, 8, 512], F32) for i in range(2)]
        pm = ps_next()
        adaw_v = adaw_d.rearrange("(kc p) n -> p kc n", p=128)
        for blk in range(12):
            wb = wbuf[blk % 2]
            wk = "adaw%d" % (blk % 2)
            kb.dma("sp", wb[:], adaw_v[:, :, blk * 512:(blk + 1) * 512], writes=[wk])
            for mm in range(4):
                m = blk * 4 + mm
                for kc in range(8):
                    kb.op("pe", lambda e, m=m, mm=mm, kc=kc, wb=wb: e.matmul(
                        psb[pm][:, m:m + 1], lhsT=wb[:, kc, mm * 128:(mm + 1) * 128], rhs=sc_sb[:, kc:kc + 1],
                        start=(kc == 0), stop=(kc == 7)), reads=[wk, "sc"], writes=[("ps", pm)])
        kb.op("dve", lambda e: e.tensor_tensor(out=mod[:], in0=psb[pm][:, 0:48], in1=adab_sb[:], op=ALU.add),
              reads=[("ps", pm), "adab"], writes=["mod"])
        kb.op("dve", lambda e: e.scalar_tensor_tensor(out=gs1[:], in0=mod[:, 8:16], scalar=1.0, in1=g1[:],
                                                      op0=ALU.add, op1=ALU.mult), reads=["mod", "g1"], writes=["gs1"])
        kb.op("dve", lambda e: e.scalar_tensor_tensor(out=gs2[:], in0=mod[:, 32:40], scalar=1.0, in1=g2[:],
                                                      op0=ALU.add, op1=ALU.mult), reads=["mod", "g2"], writes=["gs2"])
        kb.barrier()

    def norm_mod_T(stack_name, src_tile_loader, gs, shift_col0, dstT, dst_key):
        with contextlib.ExitStack() as P2:
            xt = [sb(P2, stack_name + "x%d" % i, [128, D], F32) for i in range(2)]
            xn = [sb(P2, stack_name + "xn%d" % i, [128, D], BF16) for i in range(2)]
            junk = sb(P2, stack_name + "junk", [128, D], BF16)
            ss = [sb(P2, stack_name + "ss%d" % i, [128, 4], F32) for i in range(2)]
            for i in range(NT):
                b = i % 2
                kx, kn, ks = stack_name + "x%d" % b, stack_name + "xn%d" % b, stack_name + "ss%d" % b
                src_tile_loader(i, xt[b], kx)
                kb.op("act", lambda e, b=b: e.activation(out=junk[:], in_=xt[b][:], func=AF.Square,
                                                         accum_out=ss[b][:, 0:1]), reads=[kx], writes=[ks, stack_name + "junk"])
                kb.op("dve", lambda e, b=b: e.tensor_scalar(out=ss[b][:, 1:2], in0=ss[b][:, 0:1], scalar1=1.0 / D,
                                                            scalar2=EPS, op0=ALU.mult, op1=ALU.add), reads=[ks], writes=[ks])
                kb.op("act", lambda e, b=b: e.activation(out=ss[b][:, 2:3], in_=ss[b][:, 1:2], func=AF.Sqrt),
                      reads=[ks], writes=[ks])
                kb.op("dve", lambda e, b=b: e.reciprocal(out=ss[b][:, 3:4], in_=ss[b][:, 2:3]), reads=[ks], writes=[ks])
                kb.op("act", lambda e, b=b: e.activation(out=xn[b][:], in_=xt[b][:], func=AF.Identity,
                                                         scale=ss[b][:, 3:4]), reads=[kx, ks], writes=[kn])
                p = ps_next()
                pv = psb[p][:].bitcast(BF16)
                for dc in range(8):
                    kb.op("pe", lambda e, b=b, dc=dc, pv=pv: e.transpose(
                        out=pv[:, dc * 128:(dc + 1) * 128], in_=xn[b][:, dc * 128:(dc + 1) * 128], identity=ident_b[:]),
                        reads=[kn, "ident_b"], writes=[("ps", p)])
                for dc in range(8):
                    eng = "dve" if dc % 2 == 0 else "pool"
                    eng = "dve"
                    kb.op(eng, lambda e, dc=dc, pv=pv, i=i: e.tensor_scalar(
                        out=dstT[:, dc, i * 128:(i + 1) * 128], in0=pv[:, dc * 128:(dc + 1) * 128],
                        scalar1=gs[:, dc:dc + 1], scalar2=mod[:, shift_col0 + dc:shift_col0 + dc + 1],
                        op0=ALU.mult, op1=ALU.add), reads=[("ps", p), "gs1", "gs2", "mod"], writes=[dst_key])
            kb.barrier()

    def load_x(i, dst, key):
        kb.dma("sp", dst[:], x_d[i * 128:(i + 1) * 128, :], writes=[key])

    norm_mod_T("n1", load_x, gs1, 0, h1T, "h1T")

    def stages():
        LT_d = din("LT", [128, 128])
        SU_d = din("SU", [128, 128])
        win_d = din("w_in", [D, 6976])
        wa2_d = din("wa2", [16, 512])
        ba2_d = din("ba2", [1, 512])
        gng_d = din("gng_l", [128, 2])
        wba_d = din("w_branch_a", [D, D])
        LT = sb(G, "LT", [128, 128], F32)
        SU = sb(G, "SU", [128, 128], F32)
        ones_bf = sb(G, "ones_bf", [128, 128], BF16)
        ones_f = sb(G, "ones_f", [128, 128], F32)
        kb.dma("sp", LT[:], LT_d, writes=["LT"])
        kb.dma("sp", SU[:], SU_d, writes=["SU"])
        kb.op("dve", lambda e: e.memset(ones_bf[:], 1.0), writes=["ones_bf"])
        kb.op("dve", lambda e: e.memset(ones_f[:], 1.0), writes=["ones_f"])

        class Rot:
            def __init__(self, stack, name, shape, dt, n=2):
                self.t = [sb(stack, "%s_%d" % (name, i), shape, dt) for i in range(n)]
                self.k = ["%s_%d" % (name, i) for i in range(n)]
                self.i = 0

            def next(self):
                j = self.i
                self.i = (j + 1) % len(self.t)
                return self.t[j], self.k[j]

        def load_w(dst, key, src, ncols):
            kb.dma("pool", dst[:, :, 0:ncols], src.rearrange("(kc p) n -> p kc n", p=128), writes=[key])

        def proj_fm(w, wkey, c0, m, hT, hkeys, t0, nt, ps_ap, pskey):
            for kc in range(8):
                kb.op("pe", lambda e, kc=kc: e.matmul(ps_ap, lhsT=w[:, kc, c0:c0 + m], rhs=hT[:, kc, t0:t0 + nt],
                                                      start=(kc == 0), stop=(kc == 7)),
                      reads=[wkey] + hkeys, writes=[pskey])

        def proj_tm(w, wkey, c0, n, hT, hkeys, i, ps_ap, pskey):
            for kc in range(8):
                kb.op("pe", lambda e, kc=kc: e.matmul(ps_ap, lhsT=hT[:, kc, i * 128:(i + 1) * 128], rhs=w[:, kc, c0:c0 + n],
                                                      start=(kc == 0), stop=(kc == 7)),
                      reads=[wkey] + hkeys, writes=[pskey])

        mT = sb(G, "mT", [128, 8, S], BF16)
        H1K = ["h1T"]
        ev_rr = [0]

        def evac_copy(out_ap, in_ap, reads, writes):
            e = "act" if ev_rr[0] % 2 == 0 else "dve"
            ev_rr[0] += 1
            if e == "act":
                kb.op("act", lambda en: en.copy(out=out_ap, in_=in_ap), reads=reads, writes=writes)
            else:
                kb.op("dve", lambda en: en.tensor_copy(out=out_ap, in_=in_ap), reads=reads, writes=writes)

        with contextlib.ExitStack() as GS:
            yaT = sb(GS, "yaT", [128, 8, S], BF16)
            lrT = sb(GS, "lrT", [16, S], F32)
            wb = Rot(GS, "wb", [128, 8, 512], BF16, 2)
            wa2 = sb(GS, "wa2", [16, 512], F32)
            ba2 = sb(GS, "ba2", [1, 512], F32)
            gng = sb(GS, "gng", [128, 2], F32)
            kb.dma("sp", wa2[:], wa2_d, writes=["wa2"])
            kb.dma("sp", ba2[:], ba2_d, writes=["ba2"])
            kb.dma("sp", gng[:], gng_d, writes=["gng"])
            w, wk = wb.next()
            load_w(w, wk, win_d[:, 3072:3088], 16)
            for tc in range(4):
                p = ps_next()
                proj_fm(w, wk, 0, 16, h1T, H1K, tc * 512, 512, psb[p][0:16, :], ("ps", p))
                kb.op("act", lambda e, p=p, tc=tc: e.copy(out=lrT[:, tc * 512:(tc + 1) * 512], in_=psb[p][0:16, :]),
                      reads=[("ps", p)], writes=["lrT"])
            for blk in range(2):
                w, wk = wb.next()
                load_w(w, wk, win_d[:, 2048 + blk * 512:2048 + (blk + 1) * 512], 512)
                for mt in range(4):
                    for tc in range(4):
                        p = ps_next()
                        proj_fm(w, wk, mt * 128, 128, h1T, H1K, tc * 512, 512, psb[p][:, :], ("ps", p))
                        kb.op("act", lambda e, p=p, tc=tc, ti=blk * 4 + mt: e.activation(
                            out=yaT[:, ti, tc * 512:(tc + 1) * 512], in_=psb[p][:, :], func=AF.Silu),
                            reads=[("ps", p)], writes=[("yaT", blk * 4 + mt)])
            for hp in range(2):
                with contextlib.ExitStack() as HP:
                    qT = sb(HP, "qT", [128, 2, S], BF16)
                    kT = sb(HP, "kT", [128, 2, S], BF16)
                    ktm = sb(HP, "ktm", [128, NT, 256], BF16)
                    vtm = sb(HP, "vtm", [128, NT, 512], BF16)
                    state = sb(HP, "state", [128, 2, 256], F32)
                    state_bf = sb(HP, "state_bf", [128, 2, 256], BF16)
                    r_e1 = Rot(HP, "e1", [128, 256], F32)
                    r_l = Rot(HP, "l", [128, 256], F32)
                    r_er = Rot(HP, "er", [128, 256], F32)
                    r_kh = Rot(HP, "kh", [128, 256], BF16)
                    r_eq = Rot(HP, "eq", [128, 128], F32, 3)
                    r_ek = Rot(HP, "ek", [128, 128], F32, 3)
                    r_qt = Rot(HP, "qt", [128, 128], BF16, 3)
                    r_kt = Rot(HP, "kt", [128, 128], BF16, 3)
                    r_pT = Rot(HP, "pT", [128, 128], BF16, 3)
                    r_sq = Rot(HP, "sq", [128, 256], BF16, 3)
                    r_rt = Rot(HP, "rt", [128, 128], F32, 3)
                    r_rs = Rot(HP, "rs", [128, 128], F32, 3)
                    r_t1 = Rot(HP, "t1", [128, 256], F32, 3)
                    kb.op("dve", lambda e: e.memset(state[:], 0.0), writes=["state0", "state1"])
                    kb.op("dve", lambda e: e.memset(state_bf[:], 0.0), writes=["sbf0", "sbf1"])
                    for which, dst, cbase in (("q", qT, 0), ("k", kT, 512)):
                        w, wk = wb.next()
                        load_w(w, wk, win_d[:, cbase + hp * 256:cbase + (hp + 1) * 256], 256)
                        for hh in range(2):
                            for tc in range(4):
                                p = ps_next()
                                proj_fm(w, wk, hh * 128, 128, h1T, H1K, tc * 512, 512, psb[p][:, :], ("ps", p))
                                evac_copy(dst[:, hh, tc * 512:(tc + 1) * 512], psb[p][:, :], [("ps", p)], [(which + "T", hh)])
                        if which == "k":
                            for i in range(NT):
                                p = ps_next()
                                proj_tm(w, wk, 0, 256, h1T, H1K, i, psb[p][:, 0:256], ("ps", p))
                                evac_copy(ktm[:, i, :], psb[p][:, 0:256], [("ps", p)], [("ktm", i)])
                    w, wk = wb.next()
                    load_w(w, wk, win_d[:, 1024 + hp * 512:1024 + (hp + 1) * 512], 512)
                    for i in range(NT):
                        p = ps_next()
                        proj_tm(w, wk, 0, 512, h1T, H1K, i, psb[p][:, :], ("ps", p))
                        evac_copy(vtm[:, i, :], psb[p][:, :], [("ps", p)], [("vtm", i)])
                    for i in range(NT):
                        ts = slice(i * 128, (i + 1) * 128)
                        p = ps_next()
                        kb.op("pe", lambda e, p=p: e.matmul(psb[p][:, 0:256], lhsT=lrT[:, ts], rhs=wa2[:, hp * 256:(hp + 1) * 256],
                                                            start=True, stop=False), reads=["lrT", "wa2"], writes=[("ps", p)])
                        kb.op("pe", lambda e, p=p: e.matmul(psb[p][:, 0:256], lhsT=ones_f[0:1, 0:128], rhs=ba2[0:1, hp * 256:(hp + 1) * 256],
                                                            start=False, stop=True), reads=["ones_f", "ba2"], writes=[("ps", p)])
                        e1, e1k = r_e1.next()
                        lt, lk = r_l.next()
                        kb.op("act", lambda e, p=p, e1=e1: e.activation(out=e1[:], in_=psb[p][:, 0:256], func=AF.Exp, scale=-1.0),
                              reads=[("ps", p)], writes=[e1k])
                        kb.op("act", lambda e, e1=e1, lt=lt: e.activation(out=lt[:], in_=e1[:], func=AF.Ln, bias=1.0),
                              reads=[e1k], writes=[lk])
                        p = ps_next()
                        kb.op("pe", lambda e, p=p, lt=lt: e.matmul(psb[p][:, 0:256], lhsT=SU[:], rhs=lt[:], start=True, stop=True),
                              reads=["SU", lk], writes=[("ps", p)])
                        er, erk = r_er.next()
                        kb.op("act", lambda e, p=p, er=er: e.activation(out=er[:], in_=psb[p][:, 0:256], func=AF.Exp, scale=-1.0 / 16),
                              reads=[("ps", p)], writes=[erk])
                        kh, khk = r_kh.next()
                        kb.op("dve", lambda e, kh=kh, er=er: e.tensor_tensor(out=kh[:], in0=ktm[:, i, :], in1=er[:], op=ALU.mult),
                              reads=[("ktm", i), erk], writes=[khk])
                        for hh in range(2):
                            h = hp * 2 + hh
                            p = ps_next()
                            kb.op("pe", lambda e, p=p, lt=lt: e.matmul(psb[p][:, 0:128], lhsT=lt[:, hh * 128:(hh + 1) * 128], rhs=LT[:],
                                                                      start=True, stop=True), reads=[lk, "LT"], writes=[("ps", p)])
                            eq, eqk = r_eq.next()
                            ek, ekk = r_ek.next()
                            kb.op("act", lambda e, p=p, eq=eq: e.activation(out=eq[:], in_=psb[p][:, 0:128], func=AF.Exp, scale=-1.0 / 16),
                                  reads=[("ps", p)], writes=[eqk])
                            kb.op("act", lambda e, p=p, ek=ek: e.activation(out=ek[:], in_=psb[p][:, 0:128], func=AF.Exp, scale=1.0 / 16),
                                  reads=[("ps", p)], writes=[ekk])
                            qt, qtk = r_qt.next()
                            kt, ktk = r_kt.next()
                            kb.op("dve", lambda e, qt=qt, eq=eq: e.scalar_tensor_tensor(
                                out=qt[:], in0=qT[:, hh, ts], scalar=float(128 ** -0.5), in1=eq[:], op0=ALU.mult, op1=ALU.mult),
                                reads=[("qT", hh), eqk], writes=[qtk])
                            kb.op("dve", lambda e, kt=kt, ek=ek: e.tensor_tensor(out=kt[:], in0=kT[:, hh, ts], in1=ek[:], op=ALU.mult),
                                  reads=[("kT", hh), ekk], writes=[ktk])
                            p = ps_next()
                            kb.op("pe", lambda e, p=p, kt=kt, qt=qt: e.matmul(psb[p][:, 0:128], lhsT=kt[:], rhs=qt[:], start=True, stop=True),
                                  reads=[ktk, qtk], writes=[("ps", p)])
                            pT, pTk = r_pT.next()
                            kb.op("dve", lambda e, p=p, pT=pT: e.tensor_tensor(out=pT[:], in0=psb[p][:, 0:128], in1=LT[:], op=ALU.mult),
                                  reads=[("ps", p), "LT"], writes=[pTk])
                            po = ps_next()
                            for vh in range(2):
                                kb.op("pe", lambda e, po=po, vh=vh, pT=pT: e.matmul(
                                    psb[po][:, vh * 128:(vh + 1) * 128], lhsT=vtm[:, i, hh * 256 + vh * 128:hh * 256 + (vh + 1) * 128],
                                    rhs=pT[:], start=True, stop=False), reads=[("vtm", i), pTk], writes=[("ps", po)])
                                kb.op("pe", lambda e, po=po, vh=vh, qt=qt: e.matmul(
                                    psb[po][:, vh * 128:(vh + 1) * 128], lhsT=state_bf[:, hh, vh * 128:(vh + 1) * 128],
                                    rhs=qt[:], start=False, stop=True), reads=["sbf%d" % hh, qtk], writes=[("ps", po)])
                            sq, sqk = r_sq.next()
                            kb.op("act", lambda e, po=po, sq=sq: e.activation(out=sq[:], in_=psb[po][:, 0:256], func=AF.Square),
                                  reads=[("ps", po)], writes=[sqk])
                            p2 = ps_next()
                            for vh in range(2):
                                kb.op("pe", lambda e, p2=p2, vh=vh, sq=sq: e.matmul(
                                    psb[p2][:, 0:128], lhsT=ones_bf[:], rhs=sq[:, vh * 128:(vh + 1) * 128],
                                    start=(vh == 0), stop=(vh == 1)), reads=["ones_bf", sqk], writes=[("ps", p2)])
                            rt, rtk = r_rt.next()
                            rs, rsk = r_rs.next()
                            kb.op("act", lambda e, p2=p2, rt=rt: e.activation(out=rt[:], in_=psb[p2][:, 0:128], func=AF.Sqrt,
                                                                              scale=1.0 / 256, bias=EPS), reads=[("ps", p2)], writes=[rtk])
                            kb.op("dve", lambda e, rt=rt, rs=rs: e.reciprocal(out=rs[:], in_=rt[:]), reads=[rtk], writes=[rsk])
                            t1, t1k = r_t1.next()
                            for vh in range(2):
                                kb.op("dve", lambda e, po=po, vh=vh, t1=t1, rs=rs: e.scalar_tensor_tensor(
                                    out=t1[:, vh * 128:(vh + 1) * 128], in0=psb[po][:, vh * 128:(vh + 1) * 128],
                                    scalar=gng[:, vh:vh + 1], in1=rs[:], op0=ALU.mult, op1=ALU.mult),
                                    reads=[("ps", po), "gng", rsk], writes=[t1k])
                            for vh in range(2):
                                kb.op("pool", lambda e, vh=vh, t1=t1, h=h: e.tensor_tensor(
                                    out=yaT[:, h * 2 + vh, ts], in0=t1[:, vh * 128:(vh + 1) * 128], in1=yaT[:, h * 2 + vh, ts],
                                    op=ALU.mult), reads=[t1k, ("yaT", h * 2 + vh)], writes=[("yaT", h * 2 + vh)])
                            pk = ps_next()
                            kb.op("pe", lambda e, pk=pk, kh=kh: e.matmul(psb[pk][:, 0:256], lhsT=kh[:, hh * 128:(hh + 1) * 128],
                                                                        rhs=vtm[:, i, hh * 256:(hh + 1) * 256], start=True, stop=True),
                                  reads=[khk, ("vtm", i)], writes=[("ps", pk)])
                            kb.op("dve", lambda e, pk=pk, eq=eq: e.scalar_tensor_tensor(
                                out=state[:, hh, :], in0=state[:, hh, :], scalar=eq[:, 127:128], in1=psb[pk][:, 0:256],
                                op0=ALU.mult, op1=ALU.add), reads=["state%d" % hh, eqk, ("ps", pk)], writes=["state%d" % hh])
                            kb.op("act", lambda e: e.copy(out=state_bf[:, hh, :], in_=state[:, hh, :]),
                                  reads=["state%d" % hh], writes=["sbf%d" % hh])
                    kb.barrier()
            if "yaT" in dbg:
                kb.dma("sp", dbg_out("yaT", [128, 8, S], BF16), yaT[:], reads=[("yaT", i) for i in range(8)])
            for blk in range(2):
                wm, wmk = wb.next()
                load_w(wm, wmk, win_d[:, 4928 + blk * 512:4928 + (blk + 1) * 512], 512)
                wa, wak = wb.next()
                load_w(wa, wak, wba_d[:, blk * 512:(blk + 1) * 512], 512)
                with contextlib.ExitStack() as BS:
                    r_sg = Rot(BS, "sg", [128, 512], F32, 2)
                    for mt in range(4):
                        for tc in range(4):
                            p1 = ps_next()
                            proj_fm(wm, wmk, mt * 128, 128, h1T, H1K, tc * 512, 512, psb[p1][:, :], ("ps", p1))
                            sg, sgk = r_sg.next()
                            kb.op("act", lambda e, p1=p1, sg=sg: e.activation(out=sg[:], in_=psb[p1][:, :], func=AF.Sigmoid),
                                  reads=[("ps", p1)], writes=[sgk])
                            p2 = ps_next()
                            proj_fm(wa, wak, mt * 128, 128, yaT, [("yaT", j) for j in range(8)], tc * 512, 512, psb[p2][:, :], ("ps", p2))
                            kb.op("dve", lambda e, p2=p2, sg=sg, ti=blk * 4 + mt, tc=tc: e.tensor_tensor(
                                out=mT[:, ti, tc * 512:(tc + 1) * 512], in0=psb[p2][:, :], in1=sg[:], op=ALU.mult),
                                reads=[("ps", p2), sgk], writes=[("mT", blk * 4 + mt)])
                    kb.barrier()
            kb.barrier()
        pos_d = nc.dram_tensor("pos", [1, S], I32, kind="ExternalInput").ap()
        ropec_d = din("rope_c", [16, 4])
        wqrot_d = din("w_qrot", [D, 256])
        wkrot_d = din("w_krot", [D, 96])
        cw1_d = din("cmp_w1", [2, 64, 32 * 64])
        cw2_d = din("cmp_w2", [2, 64, 64])
        cpe_d = din("cmp_peT", [2, 64, 32])
        ovl_d = din("ovl", [127, 32])
        cmpb_d = din("cmpbias", [127, S])
        blk_d = din("blockind", [32, S])
        cb_d = din("causalbias", [128, 128])
        ab_d = din("antibias", [128, 128])
        Msel_d = din("Msel", [128, NT, 32])
        Fsel_d = din("Fsel", [128, NT, 32])
        wbb_d = din("w_branch_b", [D, D])
        NEG = 30000.0

        with contextlib.ExitStack() as NS:
            zeros_bf = sb(NS, "zeros_bf", [128, 512], BF16)
            kb.op("pool", lambda e: e.memset(zeros_bf[:], 0.0), writes=["zeros_bf"])
            cosT = sb(NS, "cosT", [16, S], BF16)
            sinT = sb(NS, "sinT", [16, S], BF16)
            gates = sb(NS, "gates", [128, NT, 48], F32)
            ovl_f = sb(NS, "ovl_f", [127, 32], F32)
            cmpb = sb(NS, "cmpb", [127, S], BF16)
            cb = sb(NS, "cb", [128, 128], BF16)
            ab = sb(NS, "ab", [128, 128], BF16)
            Msel = sb(NS, "Msel", [128, NT, 32], F32)
            Fsel = sb(NS, "Fsel", [128, NT, 32], F32)
            kb.dma("sp", ovl_f[:], ovl_d, writes=["ovl_f"])
            kb.dma("pool", cmpb[:], cmpb_d, writes=["cmpb"])
            kb.dma("pool", cb[:], cb_d, writes=["cb"])
            kb.dma("pool", ab[:], ab_d, writes=["ab"])
            kb.dma("sp", Msel[:], Msel_d, writes=["Msel"])
            kb.dma("sp", Fsel[:], Fsel_d, writes=["Fsel"])
            with contextlib.ExitStack() as RS:
                posi = sb(RS, "posi", [16, S], I32)
                posf = sb(RS, "posf", [16, S], F32)
                ang = sb(RS, "ang", [16, S], F32)
                ki = sb(RS, "ki", [16, S], I32)
                kf = sb(RS, "kf", [16, S], F32)
                rr = sb(RS, "rr", [16, S], F32)
                ropec = sb(RS, "ropec", [16, 4], F32)
                kb.dma("sp", posi[:], pos_d.broadcast_to([16, S]), writes=["posi"])
                kb.dma("sp", ropec[:], ropec_d, writes=["ropec"])
                kb.op("dve", lambda e: e.tensor_copy(out=posf[:], in_=posi[:]), reads=["posi"], writes=["posf"])
                for col, dst, dk in ((1, cosT, "cosT"), (2, sinT, "sinT")):
                    kb.op("dve", lambda e, col=col: e.tensor_scalar(out=ang[:], in0=posf[:], scalar1=ropec[:, 0:1], scalar2=ropec[:, col:col + 1],
                                                                    op0=ALU.mult, op1=ALU.add), reads=["posf", "ropec"], writes=["ang"])
                    kb.op("dve", lambda e: e.tensor_scalar(out=ki[:], in0=ang[:], scalar1=float(1.0 / (2 * np.pi)), scalar2=None,
                                                           op0=ALU.mult), reads=["ang"], writes=["ki"])
                    kb.op("dve", lambda e: e.tensor_copy(out=kf[:], in_=ki[:]), reads=["ki"], writes=["kf"])
                    kb.op("dve", lambda e: e.scalar_tensor_tensor(out=rr[:], in0=kf[:], scalar=float(-2 * np.pi), in1=ang[:],
                                                                  op0=ALU.mult, op1=ALU.add), reads=["kf", "ang"], writes=["rr"])
                    kb.op("dve", lambda e: e.tensor_scalar(out=kf[:], in0=rr[:], scalar1=float(np.pi), scalar2=float(2 * np.pi),
                                                           op0=ALU.is_gt, op1=ALU.mult), reads=["rr"], writes=["kf"])
                    kb.op("dve", lambda e: e.tensor_tensor(out=rr[:], in0=rr[:], in1=kf[:], op=ALU.subtract), reads=["rr", "kf"], writes=["rr"])
                    kb.op("dve", lambda e: e.tensor_scalar(out=kf[:], in0=rr[:], scalar1=float(-np.pi), scalar2=float(2 * np.pi),
                                                           op0=ALU.is_lt, op1=ALU.mult), reads=["rr"], writes=["kf"])
                    kb.op("dve", lambda e: e.tensor_tensor(out=rr[:], in0=rr[:], in1=kf[:], op=ALU.add), reads=["rr", "kf"], writes=["rr"])
                    kb.op("act", lambda e, dst=dst: e.activation(out=dst[:], in_=rr[:], func=AF.Sin), reads=["rr"], writes=[dk])
                kb.barrier()
            kstop("rope")
            wbn = Rot(NS, "wbn", [128, 8, 512], BF16, 2)
            w, wk = wbn.next()
            load_w(w, wk, win_d[:, 4880:4928], 48)
            for i in range(NT):
                p = ps_next()
                proj_tm(w, wk, 0, 48, h1T, H1K, i, psb[p][:, 0:48], ("ps", p))
                kb.op("act", lambda e, p=p, i=i: e.activation(out=gates[:, i, :], in_=psb[p][:, 0:48], func=AF.Sigmoid),
                      reads=[("ps", p)], writes=["gates"])

            kstop("gates")
            for g in range(2):
                with contextlib.ExitStack() as GR:
                    Qa = [sb(GR, "Qa%d" % hh, [96, S], BF16) for hh in range(8)]
                    kcT = sb(GR, "kcT", [64, S], BF16)
                    ksT = sb(GR, "ksT", [96, S], BF16)
                    kwT = sb(GR, "kwT", [64, S], BF16)
                    vs_a = sb(GR, "vs_a", [128, NT, 65], BF16)
                    vw_a = sb(GR, "vw_a", [128, NT, 65], BF16)
                    vcmp = sb(GR, "vcmp", [127, 97], BF16)
                    kcmpT = sb(GR, "kcmpT", [64, 127], BF16)
                    yb = sb(GR, "yb", [128, NT, 512], BF16)
                    score = sb(GR, "score", [128, NT, 32], F32)
                    PJ = contextlib.ExitStack()
                    vcT = sb(PJ, "vcT", [64, S], BF16)
                    cw1 = sb(PJ, "cw1", [64, 2, 32 * 64], BF16)
                    cw2 = sb(PJ, "cw2", [64, 2, 64], BF16)
                    cpe = sb(PJ, "cpe", [64, 2, 32], BF16)
                    peb = sb(PJ, "peb", [64, 2], F32)
                    gsb = sb(PJ, "gsb", [64, 2, 127], BF16)
                    r_t1 = Rot(PJ, "rt1", [16, 512], F32, 1)
                    r_t2 = Rot(PJ, "rt2", [16, 512], F32, 1)
                    wqr = sb(PJ, "wqr", [128, 8, 128], BF16)
                    wkr = sb(PJ, "wkr", [128, 8, 96], BF16)
                    wkv = sb(PJ, "wkv", [128, 8, 384], BF16)
                    for wi in range(6):
                        kb.dma("pool", wkv[:, :, wi * 64:(wi + 1) * 64],
                               win_d[:, 4112 + wi * 128 + g * 64:4112 + wi * 128 + (g + 1) * 64].rearrange("(kc p) n -> p kc n", p=128),
                               writes=["wkv"])
                    kb.op("dve", lambda e: e.memset(score[:], 0.0), writes=["score"])
                    kb.op("pool", lambda e: e.memset(vs_a[:, :, 64:65], 1.0), writes=["vs_a"])
                    kb.op("pool", lambda e: e.memset(vw_a[:, :, 64:65], 1.0), writes=["vw_a"])
                    kb.op("pool", lambda e: e.memset(vcmp[:, 64:65], 1.0), writes=["vcmp"])
                    kb.op("dve", lambda e: e.tensor_copy(out=vcmp[:, 65:97], in_=ovl_f[:]), reads=["ovl_f"], writes=["vcmp"])
                    for hh in range(8):
                        kb.op("pool", lambda e, hh=hh: e.memset(Qa[hh][64:96, :], 0.0), writes=[("Qa", hh)])
                    kb.dma("pool", ksT[64:96, :], blk_d, writes=["ksT"])
                    for kv in range(2):
                        kb.dma("pool", cw1[:, kv, :], cw1_d[kv], writes=["cw1"])
                    kb.dma("pool", cw2[:], cw2_d.rearrange("k d c -> d k c"), writes=["cw2"])
                    kb.dma("pool", cpe[:], cpe_d.rearrange("k d l -> d k l"), writes=["cpe"])
                    load_w(wqr, "wqr", wqrot_d[:, g * 128:(g + 1) * 128], 128)
                    load_w(wkr, "wkr", wkrot_d, 96)

                    def rope_evac(dst, dkey, pm, pr, tc):
                        cs = slice(tc * 512, (tc + 1) * 512)
                        kb.op("act", lambda e: e.copy(out=dst[0:64, cs], in_=psb[pm][0:64, :]), reads=[("ps", pm)], writes=[dkey])
                        import os
                        if os.environ.get("KNOROPE"):
                            return
                        t1, t1k = r_t1.next()
                        t2, t2k = r_t2.next()
                        kb.op("dve", lambda e: e.tensor_tensor(out=t1[:], in0=cosT[:, cs], in1=psb[pm][0:16, :], op=ALU.mult),
                              reads=[("ps", pm), "cosT"], writes=[t1k])
                        kb.op("dve", lambda e: e.tensor_tensor(out=t2[:], in0=sinT[:, cs], in1=psb[pr][0:16, :], op=ALU.mult),
                              reads=[("ps", pr), "sinT"], writes=[t2k])
                        if os.environ.get("KNOPOOL"):
                            return
                        kb.op("pool", lambda e: e.tensor_tensor(out=dst[0:16, cs], in0=t1[:], in1=t2[:], op=ALU.add),
                              reads=[t1k, t2k], writes=[dkey])

                    kstop("grsetup")
                    w, wk = wbn.next()
                    load_w(w, wk, win_d[:, 3088 + g * 512:3088 + (g + 1) * 512], 512)
                    for hh in range(8):
                        for tc in range(4):
                            pm = ps_next()
                            proj_fm(w, wk, hh * 64, 64, h1T, H1K, tc * 512, 512, psb[pm][0:64, :], ("ps", pm))
                            pr = ps_next()
                            proj_fm(wqr, "wqr", hh * 16, 16, h1T, H1K, tc * 512, 512, psb[pr][0:16, :], ("ps", pr))
                            rope_evac(Qa[hh], ("Qa", hh), pm, pr, tc)
                    kstop("qproj")
                    for wi, dst, dkey, ri_ in ((0, kcT, "kcT", 0), (2, ksT, "ksT", 1), (4, kwT, "kwT", 2)):
                        for tc in range(4):
                            pm = ps_next()
                            proj_fm(wkv, "wkv", wi * 64, 64, h1T, H1K, tc * 512, 512, psb[pm][0:64, :], ("ps", pm))
                            pr = ps_next()
                            proj_fm(wkr, "wkr", ri_ * 32 + g * 16, 16, h1T, H1K, tc * 512, 512, psb[pr][0:16, :], ("ps", pr))
                            rope_evac(dst, dkey, pm, pr, tc)
                    for tc in range(4):
                        pm = ps_next()
                        proj_fm(wkv, "wkv", 1 * 64, 64, h1T, H1K, tc * 512, 512, psb[pm][0:64, :], ("ps", pm))
                        kb.op("act", lambda e, pm=pm, tc=tc: e.copy(out=vcT[:, tc * 512:(tc + 1) * 512], in_=psb[pm][0:64, :]),
                              reads=[("ps", pm)], writes=["vcT"])
                    for wi, dst, dkey in ((3, vs_a, "vs_a"), (5, vw_a, "vw_a")):
                        for i in range(NT):
                            pm = ps_next()
                            proj_tm(wkv, "wkv", wi * 64, 64, h1T, H1K, i, psb[pm][:, 0:64], ("ps", pm))
                            evac_copy(dst[:, i, 0:64], psb[pm][:, 0:64], [("ps", pm)], [dkey])
                    kstop("kvproj")
                    for kv, srcT, skey in ((0, kcT, "kcT"), (1, vcT, "vcT")):
                        p = ps_next()
                        for l in range(32):
                            kb.op("pe", lambda e, p=p, l=l, kv=kv: e.matmul(psb[p][0:64, 0:1], lhsT=cw1[:, kv, l * 64:(l + 1) * 64],
                                                                            rhs=cpe[:, kv, l:l + 1], start=(l == 0), stop=(l == 31)),
                                  reads=["cw1", "cpe"], writes=[("ps", p)])
                        kb.op("act", lambda e, p=p, kv=kv: e.copy(out=peb[:, kv:kv + 1], in_=psb[p][0:64, 0:1]),
                              reads=[("ps", p)], writes=["peb"])
                        p = ps_next()
                        for l in range(32):
                            kb.op("pe", lambda e, p=p, l=l, kv=kv, srcT=srcT: e.matmul(
                                psb[p][0:64, 0:127], lhsT=cw1[:, kv, l * 64:(l + 1) * 64], rhs=srcT[0:64, l:l + 16 * 126 + 1:16],
                                start=(l == 0), stop=(l == 31)), reads=["cw1", skey], writes=[("ps", p)])
                        kb.op("act", lambda e, p=p, kv=kv: e.activation(out=gsb[:, kv, :], in_=psb[p][0:64, 0:127], func=AF.Gelu,
                                                                        bias=peb[:, kv:kv + 1]), reads=[("ps", p), "peb"], writes=["gsb"])
                    p = ps_next()
                    kb.op("pe", lambda e, p=p: e.matmul(psb[p][0:64, 0:127], lhsT=cw2[:, 0, :], rhs=gsb[:, 0, :], start=True, stop=True),
                          reads=["cw2", "gsb"], writes=[("ps", p)])
                    kb.op("act", lambda e, p=p: e.copy(out=kcmpT[:], in_=psb[p][0:64, 0:127]), reads=[("ps", p)], writes=["kcmpT"])
                    p = ps_next()
                    kb.op("pe", lambda e, p=p: e.matmul(psb[p][0:127, 0:64], lhsT=gsb[:, 1, :], rhs=cw2[:, 1, :], start=True, stop=True),
                          reads=["cw2", "gsb"], writes=[("ps", p)])
                    kb.op("act", lambda e, p=p: e.copy(out=vcmp[:, 0:64], in_=psb[p][0:127, 0:64]), reads=[("ps", p)], writes=["vcmp"])

                    kb.barrier()
                    PJ.close()
                    AT = contextlib.ExitStack()
                    r_PT = Rot(AT, "PT", [128, 512], BF16, 4)
                    r_ri = Rot(AT, "ri", [128, 8], F32, 3)
                    r_tm = Rot(AT, "otmp", [128, 4, 64], F32, 2)
                    r_sc = Rot(AT, "sct", [128, 4, 32], F32, 2)
                    def attn_chunk(hh, tc, K, kT, kkey, kt_list, rhs_of, vkey, vn, pv_map, br):
                        po = ps_next_acc()
                        kb.op("pe", lambda e: e.matmul(psb[po][:, 0:4 * vn], lhsT=zeros_bf[:, 0:128], rhs=zeros_bf[:, 0:4 * vn],
                                                       start=True, stop=True), reads=["zeros_bf"], writes=[("ps", po)])
                        total = sum(len(x) for x in pv_map)
                        done = [0]

                        def emit_pv(entry):
                            j, PT, PTk, nk = entry
                            for qi in range(4):
                                if j in pv_map[qi]:
                                    done[0] += 1
                                    last = (done[0] == total)
                                    kb.op("pe", lambda e, qi=qi: e.matmul(
                                        psb[po][:, qi * vn:(qi + 1) * vn], lhsT=PT[0:nk, qi * 128:(qi + 1) * 128], rhs=rhs_of(j),
                                        start=False, stop=True, skip_group_check=True), reads=[PTk, vkey], writes=[("ps", po)])
                        pend = []
                        for (j, nk, c0, c1, masks) in kt_list:
                            p = ps_next()
                            kb.op("pe", lambda e, p=p, j=j, nk=nk, c0=c0, c1=c1: e.matmul(
                                psb[p][0:nk, c0:c1], lhsT=kT[0:K, j * 128:j * 128 + nk], rhs=Qa[hh][0:K, tc * 512 + c0:tc * 512 + c1],
                                start=True, stop=(len(masks) == 0)), reads=[kkey, ("Qa", hh)], writes=[("ps", p)])
                            for mi, (col0, width, bt, bkey) in enumerate(masks):
                                kb.op("pe", lambda e, p=p, nk=nk, col0=col0, width=width, bt=bt, mi=mi: e.matmul(
                                    psb[p][0:nk, col0:col0 + width], lhsT=ident_b[0:nk, 0:nk], rhs=bt,
                                    start=False, stop=(mi == len(masks) - 1)), reads=["ident_b", bkey], writes=[("ps", p)])
                            PT, PTk = r_PT.next()
                            kb.op("act", lambda e, p=p, nk=nk, c0=c0, c1=c1, PT=PT: e.activation(
                                out=PT[0:nk, c0:c1], in_=psb[p][0:nk, c0:c1], func=AF.Exp, scale=0.125),
                                reads=[("ps", p)], writes=[PTk])
                            pend.append((j, PT, PTk, nk))
                            if len(pend) > 2:
                                emit_pv(pend.pop(0))
                        while pend:
                            emit_pv(pend.pop(0))
                        O = psb[po][:, 0:4 * vn].rearrange("p (a c) -> p a c", c=vn)
                        ri, rik = r_ri.next()
                        kb.op("dve", lambda e: e.tensor_scalar(out=ri[:, 0:4], in0=O[:, :, 64], scalar1=1e-30, scalar2=None, op0=ALU.add),
                              reads=[("ps", po)], writes=[rik])
                        kb.op("dve", lambda e: e.reciprocal(out=ri[:, 4:8], in_=ri[:, 0:4]), reads=[rik], writes=[rik])
                        if br == 0:
                            sct, sck = r_sc.next()
                            kb.op("dve", lambda e: e.tensor_tensor(out=sct[:], in0=O[:, :, 65:97],
                                                                   in1=ri[:, 4:8].unsqueeze(2).broadcast_to([128, 4, 32]), op=ALU.mult),
                                  reads=[("ps", po), rik], writes=[sck])
                            kb.op("pool", lambda e: e.tensor_tensor(out=score[:, 4 * tc:4 * tc + 4, :], in0=score[:, 4 * tc:4 * tc + 4, :],
                                                                    in1=sct[:], op=ALU.add), reads=[sck, "score"], writes=["score"])
                        gcol = br * 16 + g * 8 + hh
                        kb.op("dve", lambda e: e.tensor_tensor(out=ri[:, 0:4], in0=ri[:, 4:8], in1=gates[:, 4 * tc:4 * tc + 4, gcol], op=ALU.mult),
                              reads=[rik, "gates"], writes=[rik])
                        ydst = yb[:, 4 * tc:4 * tc + 4, hh * 64:(hh + 1) * 64]
                        rgb = ri[:, 0:4].unsqueeze(2).broadcast_to([128, 4, 64])
                        if br == 0:
                            kb.op("dve", lambda e: e.tensor_tensor(out=ydst, in0=O[:, :, 0:64], in1=rgb, op=ALU.mult),
                                  reads=[("ps", po), rik], writes=[("yb", hh, tc)])
                        else:
                            ot, otk = r_tm.next()
                            kb.op("dve", lambda e: e.tensor_tensor(out=ot[:], in0=O[:, :, 0:64], in1=rgb, op=ALU.mult),
                                  reads=[("ps", po), rik], writes=[otk])
                            kb.op("pool", lambda e: e.tensor_tensor(out=ydst, in0=ydst, in1=ot[:], op=ALU.add),
                                  reads=[otk, ("yb", hh, tc)], writes=[("yb", hh, tc)])

                    kstop("cmp")
                    for hh in range(8):
                        for tc in range(4):
                            attn_chunk(hh, tc, 64, kcmpT, "kcmpT",
                                       [(0, 127, 0, 512, [(0, 512, cmpb[:, tc * 512:(tc + 1) * 512], "cmpb")])],
                                       lambda j: vcmp[0:127, 0:97], "vcmp", 97, [[0], [0], [0], [0]], 0)
                    kstop("passA")
                    r_scp = Rot(AT, "scp", [128, 32], F32, 2)
                    r_sc2 = Rot(AT, "sc2", [128, 32], F32, 2)
                    r_m8 = Rot(AT, "m8", [128, 16], F32, 2)
                    r_bt = Rot(AT, "biast", [128, 96], BF16, 2)
                    for bt_, btk_ in zip(r_bt.t, r_bt.k):
                        kb.op("dve", lambda e, bt_=bt_: e.memset(bt_[:], 0.0), writes=[btk_])
                    for i in range(8, NT):
                        scp, scpk = r_scp.next()
                        sc2, sc2k = r_sc2.next()
                        m8, m8k = r_m8.next()
                        bt_, btk_ = r_bt.next()
                        kb.op("dve", lambda e: e.scalar_tensor_tensor(out=scp[:], in0=score[:, i, :], scalar=1.0, in1=Msel[:, i, :],
                                                                      op0=ALU.add, op1=ALU.mult), reads=["score", "Msel"], writes=[scpk])
                        kb.op("dve", lambda e: e.max(out=m8[:, 0:8], in_=scp[:]), reads=[scpk], writes=[m8k])
                        kb.op("dve", lambda e: e.match_replace(out=sc2[:], in_to_replace=m8[:, 0:8], in_values=scp[:], imm_value=-1.0),
                              reads=[scpk, m8k], writes=[sc2k])
                        kb.op("dve", lambda e: e.max(out=m8[:, 8:16], in_=sc2[:]), reads=[sc2k], writes=[m8k])
                        kb.op("dve", lambda e: e.tensor_scalar(out=sc2[:], in0=scp[:], scalar1=m8[:, 12:13], scalar2=None, op0=ALU.is_ge),
                              reads=[scpk, m8k], writes=[sc2k])
                        kb.op("dve", lambda e: e.tensor_tensor(out=sc2[:], in0=sc2[:], in1=Fsel[:, i, :], op=ALU.max),
                              reads=[sc2k, "Fsel"], writes=[sc2k])
                        kb.op("dve", lambda e: e.tensor_scalar(out=bt_[:, 64:96], in0=sc2[:], scalar1=NEG, scalar2=-NEG,
                                                               op0=ALU.mult, op1=ALU.add), reads=[sc2k], writes=[btk_])
                        p = ps_next()
                        pv = psb[p][:].bitcast(BF16)
                        kb.op("pe", lambda e, pv=pv: e.transpose(out=pv[0:96, 0:128], in_=bt_[:, 0:96], identity=ident_b[:]),
                              reads=[btk_, "ident_b"], writes=[("ps", p)])
                        for hh in range(8):
                            evac_copy(Qa[hh][64:96, i * 128:(i + 1) * 128], pv[64:96, 0:128], [("ps", p)], [("Qa", hh)])
                    if "score" in dbg and g == 0:
                        kb.dma("sp", dbg_out("score", [128, NT, 32]), score[:], reads=["score"])
                    if "ycmp" in dbg and g == 0:
                        kb.dma("sp", dbg_out("ycmp", [128, NT, 512], BF16), yb[:], reads=[("yb", hh, tc) for hh in range(8) for tc in range(4)])
                    kstop("passB")
                    for hh in range(8):
                        for tc in range(4):
                            ktl = []
                            for j in range(4 * tc + 4):
                                if j < 4 * tc:
                                    ktl.append((j, 128, 0, 512, []))
                                else:
                                    r = j - 4 * tc
                                    ktl.append((j, 128, 128 * r, 512, [(128 * r, 128, cb[:], "cb")]))
                            import os
                            KBR = os.environ.get("KBR", "12")
                            if "1" in KBR:
                                attn_chunk(hh, tc, 96, ksT, "ksT", ktl, lambda j: vs_a[:, j, 0:65], "vs_a", 65,
                                           [list(range(0, 4 * tc + qi + 1)) for qi in range(4)], 1)
                            ktl = []
                            for j in range(max(0, 4 * tc - 4), 4 * tc + 4):
                                jj = j - 4 * tc
                                c0 = 128 * max(jj, 0)
                                c1 = 128 * (min(jj + 4, 3) + 1)
                                masks = []
                                if jj >= 0:
                                    masks.append((128 * jj, 128, cb[:], "cb"))
                                if jj + 4 <= 3:
                                    masks.append((128 * (jj + 4), 128, ab[:], "ab"))
                                ktl.append((j, 128, c0, c1, masks))
                            if "2" in KBR:
                                attn_chunk(hh, tc, 64, kwT, "kwT", ktl, lambda j: vw_a[:, j, 0:65], "vw_a", 65,
                                           [list(range(max(0, 4 * tc + qi - 4), 4 * tc + qi + 1)) for qi in range(4)], 2)
                    kb.barrier()
                    AT.close()
                    FB = contextlib.ExitStack()
                    ybT = sb(FB, "ybT", [128, 4, S], BF16)
                    r_sg = Rot(FB, "sgb", [128, 512], F32, 2)
                    r_tb = Rot(FB, "tb", [128, 512], F32, 2)
                    kstop("passC")
                    for i in range(NT):
                        p = ps_next()
                        pv = psb[p][:].bitcast(BF16)
                        for m in range(4):
                            kb.op("pe", lambda e, pv=pv, m=m, i=i: e.transpose(out=pv[:, m * 128:(m + 1) * 128], in_=yb[:, i, m * 128:(m + 1) * 128],
                                                                              identity=ident_b[:]),
                                  reads=[("yb", hh, i // 4) for hh in range(8)] + ["ident_b"], writes=[("ps", p)])
                        evac_copy(ybT[:, :, i * 128:(i + 1) * 128], pv[:, 0:512].rearrange("p (m t) -> p m t", t=128),
                                  [("ps", p)], ["ybT"])
                    kstop("fbT")
                    if "ybT" in dbg:
                        kb.dma("sp", dbg_out("ybT%d" % g, [128, 4, S], BF16), ybT[:], reads=["ybT"])
                    for blk in range(2):
                        wm, wmk = wbn.next()
                        load_w(wm, wmk, win_d[:, 5952 + blk * 512:5952 + (blk + 1) * 512], 512)
                        wa, wak = wbn.next()
                        kb.dma("pool", wa[:, 0:4, :], wbb_d[g * 512:(g + 1) * 512, blk * 512:(blk + 1) * 512].rearrange("(kc p) n -> p kc n", p=128),
                               writes=[wak])
                        for mt in range(4):
                            for tc in range(4):
                                p1 = ps_next()
                                proj_fm(wm, wmk, mt * 128, 128, h1T, H1K, tc * 512, 512, psb[p1][:, :], ("ps", p1))
                                sg, sgk = r_sg.next()
                                kb.op("act", lambda e, p1=p1, sg=sg: e.activation(out=sg[:], in_=psb[p1][:, :], func=AF.Sigmoid),
                                      reads=[("ps", p1)], writes=[sgk])
                                p2 = ps_next()
                                for kc in range(4):
                                    kb.op("pe", lambda e, kc=kc, p2=p2: e.matmul(psb[p2][:, :], lhsT=wa[:, kc, mt * 128:(mt + 1) * 128],
                                                                                 rhs=ybT[:, kc, tc * 512:(tc + 1) * 512],
                                                                                 start=(kc == 0), stop=(kc == 3)),
                                          reads=[wak, "ybT"], writes=[("ps", p2)])
                                tb, tbk = r_tb.next()
                                kb.op("dve", lambda e, p2=p2, sg=sg, tb=tb: e.tensor_tensor(out=tb[:], in0=psb[p2][:, :], in1=sg[:], op=ALU.mult),
                                      reads=[("ps", p2), sgk], writes=[tbk])
                                ti = blk * 4 + mt
                                kb.op("pool", lambda e, tb=tb, ti=ti, tc=tc: e.tensor_tensor(
                                    out=mT[:, ti, tc * 512:(tc + 1) * 512], in0=mT[:, ti, tc * 512:(tc + 1) * 512], in1=tb[:], op=ALU.add),
                                    reads=[tbk, ("mT", ti)], writes=[("mT", ti)])
                    kb.barrier()
                    FB.close()
                    kstop("g0")
            kb.barrier()
        if "mT" in dbg:
            kb.dma("sp", dbg_out("mT", [128, 8, S], BF16), mT[:], reads=[("mT", i) for i in range(8)])
        wout_d = din("w_out", [D, D])
        wq_d = din("peer_wq", [D, 2048])
        pk_d = din("peer_k", [2, 8, 128, 128])
        pu_d = din("peer_u", [16384, D])
        pvv_d = din("peer_v", [16384, D])
        mix = h1T[:].rearrange("p a (b c) -> p (a b) c", c=1024)
        h2T = mT
        with contextlib.ExitStack() as PS_:
            G1S = contextlib.ExitStack()
            g1_bc = sb(G1S, "g1_bc", [128, D], F32)

            def build_bc(stack, items):
                r_dg = Rot(stack, "diag", [128, 128], F32, 2)
                for dst, src, dk in items:
                    for dc in range(8):
                        dg, dgk = r_dg.next()
                        kb.op("dve", lambda e, dg=dg, src=src, dc=dc: e.tensor_scalar(out=dg[:], in0=ident_f[:], scalar1=src[:, dc:dc + 1],
                                                                                     scalar2=None, op0=ALU.mult),
                              reads=["ident_f", "mod", "gf"], writes=[dgk])
                        p = ps_next()
                        kb.op("pe", lambda e, p=p, dg=dg: e.matmul(psb[p][:, 0:128], lhsT=ones_f[:, 0:128], rhs=dg[:], start=True, stop=True),
                              reads=["ones_f", dgk], writes=[("ps", p)])
                        evac_copy(dst[:, dc * 128:(dc + 1) * 128], psb[p][:, 0:128], [("ps", p)], [dk])

            with contextlib.ExitStack() as MS:
                build_bc(MS, [(g1_bc, mod[:, 16:24], "g1_bc")])
                wout = sb(MS, "wout", [128, 8, D], BF16)
                for hf in range(2):
                    kb.dma("pool", wout[:, :, hf * 512:(hf + 1) * 512],
                           wout_d[:, hf * 512:(hf + 1) * 512].rearrange("(kc p) n -> p kc n", p=128), writes=["wout"])
                for i in range(NT):
                    for dd in range(2):
                        p = ps_next()
                        for kc in range(8):
                            kb.op("pe", lambda e, p=p, kc=kc, i=i, dd=dd: e.matmul(
                                psb[p][:, :], lhsT=mT[:, kc, i * 128:(i + 1) * 128], rhs=wout[:, kc, dd * 512:(dd + 1) * 512],
                                start=(kc == 0), stop=(kc == 7)), reads=["wout"] + [("mT", j) for j in range(8)], writes=[("ps", p)])
                        evac_copy(mix[:, i, dd * 512:(dd + 1) * 512], psb[p][:, :], [("ps", p)], ["mix"])
                kb.barrier()
            if "mix" in dbg:
                kb.dma("sp", dbg_out("mix", [128, NT, D], BF16), mix, reads=["mix"])
            kstop("mix")
            with contextlib.ExitStack() as N2:
                r_x1 = Rot(N2, "x1tmp", [128, D], F32, 2)

                def load_x1(i, dst, key):
                    kb.dma("sp", dst[:], x_d[i * 128:(i + 1) * 128, :], writes=[key])
                    tmp, tk = r_x1.next()
                    kb.op("dve", lambda e: e.tensor_tensor(out=tmp[:], in0=mix[:, i, :], in1=g1_bc[:], op=ALU.mult),
                          reads=["mix", "g1_bc"], writes=[tk])
                    kb.op("pool", lambda e: e.tensor_tensor(out=dst[:], in0=dst[:], in1=tmp[:], op=ALU.add),
                          reads=[key, tk], writes=[key])

                norm_mod_T("n2", load_x1, gs2, 24, h2T, "h2T")
            if "h2T" in dbg:
                kb.dma("sp", dbg_out("h2T", [128, 8, S], BF16), h2T[:], reads=["h2T"])
            kstop("h2T")
            kb.barrier()
            G1S.close()
            with contextlib.ExitStack() as PE_:
                k1T = sb(PE_, "k1T", [128, 8, 128], BF16)
                k2T = sb(PE_, "k2T", [128, 8, 128], BF16)
                with contextlib.ExitStack() as KS:
                    kst = sb(KS, "kst", [128, 2, 8, 128], BF16)
                    kb.dma("pool", kst[:], pk_d.rearrange("w h n d -> n w h d"), writes=["kst"])
                    for wi, dst, dk in ((0, k1T, "k1T"), (1, k2T, "k2T")):
                        p = ps_next()
                        pv = psb[p][:].bitcast(BF16)
                        for h in range(8):
                            kb.op("pe", lambda e, pv=pv, h=h, wi=wi: e.transpose(out=pv[:, h * 128:(h + 1) * 128], in_=kst[:, wi, h, :],
                                                                              identity=ident_b[:]), reads=["kst", "ident_b"], writes=[("ps", p)])
                        evac_copy(dst[:], pv[:, 0:1024].rearrange("p (h n) -> p h n", n=128), [("ps", p)], [dk])
                    kb.barrier()
                peer_acc = sb(PE_, "peer_acc", [128, 8, D], F32)
                qT = sb(PE_, "qT", [128, 16, 1024], BF16)
                thr_all = sb(PE_, "thr_all", [128, 8, 8], F32)
                nb_all = sb(PE_, "nb_all", [128, 8, 8], F32)
                Dk = sb(PE_, "Dk", [128, 8, 8, 128], BF16)
                kap = sb(PE_, "kap", [128, 8, 8], F32)
                for hf in range(2):
                    T0 = hf * 8
                    tok0 = T0 * 128
                    QS = contextlib.ExitStack()
                    wqb = Rot(QS, "wqb", [128, 8, 512], BF16, 2)
                    for blk in range(4):
                        w, wk = wqb.next()
                        load_w(w, wk, wq_d[:, blk * 512:(blk + 1) * 512], 512)
                        for mt in range(4):
                            for tc in range(2):
                                p = ps_next()
                                proj_fm(w, wk, mt * 128, 128, h2T, ["h2T"], tok0 + tc * 512, 512, psb[p][:, :], ("ps", p))
                                evac_copy(qT[:, blk * 4 + mt, tc * 512:(tc + 1) * 512], psb[p][:, :], [("ps", p)], [("qT", blk * 4 + mt)])
                    kb.barrier()
                    QS.close()
                    with contextlib.ExitStack() as PR:
                        v12 = sb(PR, "v12", [128, 2, 8, 16], F32)
                        cand = sb(PR, "cand", [128, 8, 16, 16], F32)
                        ctop = sb(PR, "ctop", [128, 8, 24], F32)
                        r_mr = Rot(PR, "mr", [128, 256], F32, 2)
                        sm = sb(PR, "sm", [128, 8, 16], F32)
                        zz = sb(PR, "zz", [128, 4, 8], F32)
                        for il in range(8):
                            ts = slice(il * 128, (il + 1) * 128)
                            for wi, kT_ in ((0, k1T), (1, k2T)):
                                for h in range(8):
                                    p = ps_next()
                                    kb.op("pe", lambda e, p=p, h=h, wi=wi, kT_=kT_: e.matmul(
                                        psb[p][:, 0:128], lhsT=qT[:, h * 2 + wi, ts], rhs=kT_[:, h, :], start=True, stop=True),
                                        reads=[("qT", h * 2 + wi), "k1T", "k2T"], writes=[("ps", p)])
                                    mr, mrk = r_mr.next()
                                    kb.op("dve", lambda e, p=p, h=h, wi=wi: e.max(out=v12[:, wi, h, 0:8], in_=psb[p][:, 0:128]),
                                          reads=[("ps", p)], writes=["v12"])
                                    kb.op("dve", lambda e, p=p, h=h, wi=wi, mr=mr: e.match_replace(
                                        out=mr[:, 0:128], in_to_replace=v12[:, wi, h, 0:8], in_values=psb[p][:, 0:128], imm_value=-1e30),
                                        reads=[("ps", p), "v12"], writes=[mrk])
                                    kb.op("dve", lambda e, h=h, wi=wi, mr=mr: e.max(out=v12[:, wi, h, 8:16], in_=mr[:, 0:128]),
                                          reads=[mrk], writes=["v12"])
                            kb.op("dve", lambda e: e.tensor_tensor(
                                out=cand[:], in0=v12[:, 0, :, :].unsqueeze(3).broadcast_to([128, 8, 16, 16]),
                                in1=v12[:, 1, :, :].unsqueeze(2).broadcast_to([128, 8, 16, 16]), op=ALU.add),
                                reads=["v12"], writes=["cand"])
                            for h in range(8):
                                ch = cand[:, h, :, :].rearrange("p a b -> p (a b)")
                                mr, mrk = r_mr.next()
                                mr2, mr2k = r_mr.next()
                                kb.op("dve", lambda e, h=h, ch=ch: e.max(out=ctop[:, h, 0:8], in_=ch), reads=["cand"], writes=["ctop"])
                                kb.op("dve", lambda e, h=h, ch=ch, mr=mr: e.match_replace(out=mr[:], in_to_replace=ctop[:, h, 0:8], in_values=ch,
                                                                                        imm_value=-1e30), reads=["cand", "ctop"], writes=[mrk])
                                kb.op("dve", lambda e, h=h, mr=mr: e.max(out=ctop[:, h, 8:16], in_=mr[:]), reads=[mrk], writes=["ctop"])
                                kb.op("dve", lambda e, h=h, mr=mr, mr2=mr2: e.match_replace(out=mr2[:], in_to_replace=ctop[:, h, 8:16], in_values=mr[:],
                                                                                          imm_value=-1e30), reads=[mrk, "ctop"], writes=[mr2k])
                                kb.op("dve", lambda e, h=h, mr2=mr2: e.max(out=ctop[:, h, 16:24], in_=mr2[:]), reads=[mr2k], writes=["ctop"])
                            kb.op("dve", lambda e: e.tensor_tensor(out=zz[:, 0, :], in0=ctop[:, :, 15], in1=ctop[:, :, 16], op=ALU.add),
                                  reads=["ctop"], writes=["zz"])
                            kb.op("dve", lambda e, il=il: e.tensor_scalar(out=thr_all[:, il, :], in0=zz[:, 0, :], scalar1=0.5, scalar2=None, op0=ALU.mult),
                                  reads=["zz"], writes=["thr_all"])
                            kb.op("dve", lambda e: e.tensor_tensor(out=sm[:], in0=ctop[:, :, 0:16],
                                                                   in1=ctop[:, :, 0:1].broadcast_to([128, 8, 16]), op=ALU.subtract),
                                  reads=["ctop"], writes=["sm"])
                            kb.op("act", lambda e: e.activation(out=sm[:], in_=sm[:], func=AF.Exp), reads=["sm"], writes=["sm"])
                            kb.op("dve", lambda e: e.tensor_reduce(out=zz[:, 1, :], in_=sm[:], axis=mybir.AxisListType.X, op=ALU.add),
                                  reads=["sm"], writes=["zz"])
                            kb.op("act", lambda e: e.activation(out=zz[:, 2, :], in_=zz[:, 1, :], func=AF.Ln), reads=["zz"], writes=["zz"])
                            kb.op("dve", lambda e: e.tensor_tensor(out=zz[:, 3, :], in0=zz[:, 2, :], in1=ctop[:, :, 0], op=ALU.add),
                                  reads=["zz", "ctop"], writes=["zz"])
                            kb.op("dve", lambda e, il=il: e.tensor_scalar(out=nb_all[:, il, :], in0=zz[:, 3, :], scalar1=-1.0, scalar2=None, op0=ALU.mult),
                                  reads=["zz"], writes=["nb_all"])
                            kb.op("dve", lambda e, il=il: e.tensor_tensor(out=kap[:, il, :], in0=thr_all[:, il, :], in1=nb_all[:, il, :], op=ALU.add),
                                  reads=["thr_all", "nb_all"], writes=["kap"])
                            kb.op("act", lambda e, il=il: e.activation(out=kap[:, il, :], in_=kap[:, il, :], func=AF.Exp), reads=["kap"], writes=["kap"])
                            for h in range(8):
                                kb.op("dve", lambda e, il=il, h=h: e.tensor_scalar(out=Dk[:, il, h, :], in0=ident_b[:], scalar1=kap[:, il, h:h + 1],
                                                                                   scalar2=None, op0=ALU.mult),
                                      reads=["ident_b", "kap"], writes=["Dk"])
                        kb.barrier()
                    EL = contextlib.ExitStack()
                    uT = Rot(EL, "uT", [128, 8, 512], BF16, 1)
                    ust = Rot(EL, "ust", [128, D], BF16, 2)
                    vbk = Rot(EL, "vbk", [128, 4, D], BF16, 1)
                    r_s1 = Rot(EL, "s1sb", [128, 8, 4], F32, 2)
                    r_aT4 = Rot(EL, "aT4", [128, 4, 512], BF16, 2)
                    sh_rr = [0]

                    def sh_next():
                        i = sh_rr[0]
                        sh_rr[0] = (i + 1) % 2
                        return 4 + i
                    r_sum = Rot(EL, "sum", [128, 4, 4, 128], BF16, 2)
                    r_E = Rot(EL, "E", [128, 4, 4, 128], BF16, 2)
                    r_G = Rot(EL, "G", [128, 4, 4, 128], BF16, 2)
                    r_wT = Rot(EL, "wT", [128, 4, 128], BF16, 2)
                    kb.op("pool", lambda e: e.memset(peer_acc[:], 0.0), writes=["peer_acc"])
                    import os
                    NBLK = int(os.environ.get("KNBLK", "32"))
                    ut, utk = uT.next()

                    def emit_u_dma(b, c):
                        e0 = b * 512
                        us, usk = ust.next()
                        kb.dma("pool", us[:], pu_d[e0 + c * 128:e0 + (c + 1) * 128, :], writes=[usk])
                        return us, usk

                    def emit_u_tr(c, us, usk):
                        p = sh_next()
                        pv = psb[p][:].bitcast(BF16)
                        for kc in range(8):
                            kb.op("pe", lambda e, pv=pv, kc=kc: e.transpose(out=pv[:, kc * 128:(kc + 1) * 128],
                                                                          in_=us[:, kc * 128:(kc + 1) * 128], identity=ident_b[:]),
                                  reads=[usk, "ident_b"], writes=[("ps", p)])
                        kb.op("act", lambda e: e.copy(out=ut[:, :, c * 128:(c + 1) * 128], in_=pv[:, 0:1024].rearrange("p (k e) -> p k e", e=128)),
                              reads=[("ps", p)], writes=[(utk, c)])

                    def emit_v(b):
                        e0 = b * 512
                        vb, vbk_ = vbk.next()
                        kb.dma("pool", vb[:], pvv_d[e0:e0 + 512, :].rearrange("(c p) d -> p c d", p=128), writes=[vbk_])
                        return vb, vbk_

                    def emit_aT4_chunk_pe(aT, g4, c, bank):
                        aT4, aT4k = aT
                        for kc in range(8):
                            kb.op("pe", lambda e, kc=kc: e.matmul(
                                psb[bank][:, :], lhsT=ut[:, kc, c * 128:(c + 1) * 128],
                                rhs=h2T[:, kc, tok0 + g4 * 512:tok0 + (g4 + 1) * 512], start=(kc == 0), stop=(kc == 7)),
                                reads=[(utk, c), "h2T"], writes=[("ps", bank)])

                        def gelu():
                            kb.op("act", lambda e: e.activation(out=aT4[:, c, :], in_=psb[bank][:, :], func=AF.Gelu),
                                  reads=[("ps", bank)], writes=[(aT4k, c)])
                        return gelu

                    class Job:
                        pass

                    def make_job(b, il, aT4, aT4k, vb, vbk_):
                        J = Job()
                        ts = slice(il * 128, (il + 1) * 128)
                        q = il % 4
                        st = {}

                        def A_pe():
                            p1 = sh_next()
                            st["p1"] = p1
                            for h in range(8):
                                kb.op("pe", lambda e, h=h: e.matmul(psb[p1][:, h * 4:(h + 1) * 4], lhsT=qT[:, h * 2, ts],
                                                                    rhs=k1T[:, h, 4 * b:4 * b + 4], start=True, stop=True),
                                      reads=[("qT", h * 2), "k1T"], writes=[("ps", p1)])
                            p2s = [(il % 2) * 2, (il % 2) * 2 + 1]
                            st["p2s"] = p2s
                            for hg in range(2):
                                for hh in range(4):
                                    h = hg * 4 + hh
                                    kb.op("pe", lambda e, h=h, hh=hh, hg=hg: e.matmul(psb[p2s[hg]][:, hh * 128:(hh + 1) * 128], lhsT=qT[:, h * 2 + 1, ts],
                                                                                    rhs=k2T[:, h, :], start=True, stop=True),
                                          reads=[("qT", h * 2 + 1), "k2T"], writes=[("ps", p2s[hg])])

                        def A_dve():
                            p1 = st["p1"]
                            s1, s1k = r_s1.next()
                            st["s1"] = (s1, s1k)
                            kb.op("dve", lambda e: e.tensor_tensor(
                                out=s1[:], in0=psb[p1][:, 0:32].rearrange("p (h i) -> p h i", i=4),
                                in1=thr_all[:, il, :].unsqueeze(2).broadcast_to([128, 8, 4]), op=ALU.subtract),
                                reads=[("ps", p1), "thr_all"], writes=[s1k])

                        def B1_sums():
                            s1, s1k = st["s1"]
                            sms = []
                            for hg in range(2):
                                p2 = st["p2s"][hg]
                                sm_, smk = r_sum.next()
                                kb.op("dve", lambda e: e.tensor_tensor(
                                    out=sm_[:], in0=s1[:, hg * 4:(hg + 1) * 4, :].unsqueeze(3).broadcast_to([128, 4, 4, 128]),
                                    in1=psb[p2][:, :].rearrange("p (h j) -> p h j", j=128).unsqueeze(2).broadcast_to([128, 4, 4, 128]), op=ALU.add),
                                    reads=[s1k, ("ps", p2)], writes=[smk])
                                sms.append((sm_, smk))
                            Es = []
                            for hg in range(2):
                                sm_, smk = sms[hg]
                                E_, Ek = r_E.next()
                                kb.op("act", lambda e: e.activation(out=E_[:], in_=sm_[:], func=AF.Exp), reads=[smk], writes=[Ek])
                                Es.append((E_, Ek))
                            st["sms"] = sms
                            st["Es"] = Es

                        def B1_stt():
                            Gs = []
                            for hg in range(2):
                                sm_, smk = st["sms"][hg]
                                E_, Ek = st["Es"][hg]
                                G_, Gk = r_G.next()
                                kb.op("dve", lambda e: e.scalar_tensor_tensor(out=G_[:].rearrange("p a b c -> p (a b) c"),
                                                                              in0=sm_[:].rearrange("p a b c -> p (a b) c"), scalar=0.0,
                                                                              in1=E_[:].rearrange("p a b c -> p (a b) c"),
                                                                              op0=ALU.is_gt, op1=ALU.mult),
                                      reads=[smk, Ek], writes=[Gk])
                                Gs.append((G_, Gk))
                            st["Gs"] = Gs

                        def B1_gt():
                            pg = ps_next_acc()
                            st["pg"] = pg
                            for hg in range(2):
                                G_, Gk = st["Gs"][hg]
                                for hh in range(4):
                                    h = hg * 4 + hh
                                    for c in range(4):
                                        kb.op("pe", lambda e, c=c, h=h, hh=hh: e.matmul(psb[pg][:, c * 128:(c + 1) * 128], lhsT=G_[:, hh, c, :],
                                                                                      rhs=Dk[:, il, h, :], start=(h == 0 and c == 0),
                                                                                      stop=True, skip_group_check=True),
                                              reads=[Gk, "Dk"], writes=[("ps", pg)])

                        def B2_wT():
                            pg = st["pg"]
                            wT, wTk = r_wT.next()
                            st["wT"] = (wT, wTk)
                            kb.op("dve", lambda e: e.tensor_tensor(out=wT[:], in0=aT4[:, :, q * 128:(q + 1) * 128],
                                                                   in1=psb[pg][:, :].rearrange("p (c t) -> p c t", t=128),
                                                                   op=ALU.mult), reads=[(aT4k, cc) for cc in range(4)] + [("ps", pg)], writes=[wTk])

                        def B2_peer_acc():
                            wT, wTk = st["wT"]
                            pps = []
                            for dd in range(2):
                                pp = sh_next()
                                pps.append(pp)
                                for c in range(4):
                                    kb.op("pe", lambda e, c=c, dd=dd, pp=pp: e.matmul(psb[pp][:, :], lhsT=wT[:, c, :], rhs=vb[:, c, dd * 512:(dd + 1) * 512],
                                                                                    start=(c == 0), stop=(c == 3)),
                                          reads=[wTk, vbk_], writes=[("ps", pp)])
                            for dd in range(2):
                                pp = pps[dd]
                                kb.op("dve", lambda e, dd=dd, pp=pp: e.tensor_tensor(out=peer_acc[:, il, dd * 512:(dd + 1) * 512],
                                                                                     in0=peer_acc[:, il, dd * 512:(dd + 1) * 512], in1=psb[pp][:, :],
                                                                                     op=ALU.add), reads=[("pacc", il), ("ps", pp), "peer_acc"],
                                      writes=[("pacc", il)])

                        J.A_pe, J.A_dve, J.B1_sums, J.B1_stt, J.B1_gt, J.B2_wT, J.B2_peer_acc = A_pe, A_dve, B1_sums, B1_stt, B1_gt, B2_wT, B2_peer_acc
                        return J

                    for c in range(4):
                        us, usk = emit_u_dma(0, c)
                        emit_u_tr(c, us, usk)
                    vb, vbk_ = emit_v(0)
                    aT_cur = [r_aT4.next(), None]
                    for c in range(4):
                        emit_aT4_chunk_pe(aT_cur[0], 0, c, sh_next())()
                    aT_next = None
                    jobs = {}
                    NSTEP = NBLK * 8
                    u_pend = {}
                    pend_gelu = []
                    for n in range(NSTEP + 3):
                        b, q8 = divmod(n, 8)
                        if n < NSTEP:
                            if q8 == 2:
                                aT_cur[1] = r_aT4.next()
                            if q8 == 0 and b > 0:
                                aT_cur[0] = aT_next
                            aT4, aT4k = aT_cur[q8 // 4]
                            jobs[n] = make_job(b, q8, aT4, aT4k, vb, vbk_)
                        if (n - 1) in jobs:
                            jobs[n - 1].B1_sums()
                        while pend_gelu:
                            pend_gelu.pop(0)()
                        if n < NSTEP and b + 1 < NBLK and 3 <= q8 <= 6:
                            emit_u_tr(q8 - 3, *u_pend[q8 - 3])
                        if (n - 3) in jobs:
                            jobs[n - 3].B2_peer_acc()
                            del jobs[n - 3]
                        if n < NSTEP and q8 == 2 and b > 0:
                            vb, vbk_ = emit_v(b)
                        if n < NSTEP:
                            jobs[n].A_pe()
                            jobs[n].A_dve()
                        if (n - 1) in jobs:
                            jobs[n - 1].B1_stt()
                            jobs[n - 1].B1_gt()
                        if (n - 2) in jobs:
                            jobs[n - 2].B2_wT()
                        if n < NSTEP:
                            bank = 6 + ps_acc[0]
                            if 2 <= q8 <= 5:
                                pend_gelu.append(emit_aT4_chunk_pe(aT_cur[1], 1, q8 - 2, bank))
                            elif b + 1 < NBLK and q8 >= 6:
                                if q8 == 6:
                                    aT_next = r_aT4.next()
                                pend_gelu.append(emit_aT4_chunk_pe(aT_next, 0, q8 - 6, bank))
                            elif b > 0 and q8 <= 1:
                                pend_gelu.append(emit_aT4_chunk_pe(aT_cur[0], 0, q8 + 2, bank))
                            if b + 1 < NBLK:
                                if 1 <= q8 <= 4:
                                    u_pend[q8 - 1] = emit_u_dma(b + 1, q8 - 1)
                    while pend_gelu:
                        pend_gelu.pop(0)()
                    if "peer" in dbg:
                        kb.dma("sp", dbg_out("peer%d" % hf, [128, 8, D]), peer_acc[:], reads=[("pacc", il) for il in range(8)] + ["peer_acc"])
                    kb.barrier()
                    EL.close()
                    with contextlib.ExitStack() as EP:
                        g1_bc = sb(EP, "g1_bc", [128, D], F32)
                        g2_bc = sb(EP, "g2_bc", [128, D], F32)
                        gf_bc = sb(EP, "gf_bc", [128, D], F32)
                        build_bc(EP, [(g1_bc, mod[:, 16:24], "g1_bc"), (g2_bc, mod[:, 40:48], "g2_bc"), (gf_bc, gf[:, 0:8], "gf_bc")])
                        r_x2 = Rot(EP, "x2", [128, D], F32, 2)
                        r_t3 = Rot(EP, "x2u", [128, D], F32, 2)
                        r_t2 = Rot(EP, "x2t", [128, D], F32, 2)
                        r_jk = Rot(EP, "jk", [128, D], BF16, 1)
                        r_st = Rot(EP, "st", [128, 4], F32, 2)
                        for il in range(8):
                            i = T0 + il
                            x2, x2k = r_x2.next()
                            t2, t2k = r_t2.next()
                            st, stk = r_st.next()
                            jk, jkk = r_jk.next()
                            kb.dma("sp", x2[:], x_d[i * 128:(i + 1) * 128, :], writes=[x2k])
                            kb.op("dve", lambda e: e.tensor_tensor(out=t2[:], in0=mix[:, i, :], in1=g1_bc[:], op=ALU.mult),
                                  reads=["mix", "g1_bc"], writes=[t2k])
                            t3, t3k = r_t3.next()
                            kb.op("pool", lambda e: e.tensor_tensor(out=t3[:], in0=peer_acc[:, il, :], in1=g2_bc[:], op=ALU.mult),
                                  reads=[("pacc", il), "peer_acc", "g2_bc"], writes=[t3k])
                            kb.op("pool", lambda e: e.tensor_tensor(out=t2[:], in0=t2[:], in1=t3[:], op=ALU.add),
                                  reads=[t2k, t3k], writes=[t2k])
                            kb.op("pool", lambda e: e.tensor_tensor(out=x2[:], in0=x2[:], in1=t2[:], op=ALU.add), reads=[x2k, t2k], writes=[x2k])
                            kb.op("act", lambda e: e.activation(out=jk[:], in_=x2[:], func=AF.Square, accum_out=st[:, 0:1]),
                                  reads=[x2k], writes=[stk, jkk])
                            kb.op("dve", lambda e: e.tensor_scalar(out=st[:, 1:2], in0=st[:, 0:1], scalar1=1.0 / D, scalar2=EPS,
                                                                   op0=ALU.mult, op1=ALU.add), reads=[stk], writes=[stk])
                            kb.op("act", lambda e: e.activation(out=st[:, 2:3], in_=st[:, 1:2], func=AF.Sqrt), reads=[stk], writes=[stk])
                            kb.op("dve", lambda e: e.reciprocal(out=st[:, 3:4], in_=st[:, 2:3]), reads=[stk], writes=[stk])
                            kb.op("dve", lambda e: e.scalar_tensor_tensor(out=t2[:], in0=x2[:], scalar=st[:, 3:4], in1=gf_bc[:],
                                                                          op0=ALU.mult, op1=ALU.mult), reads=[x2k, stk, "gf_bc"], writes=[t2k])
                            OUT_TOKS.append(kb.dma("sp", out_d[i * 128:(i + 1) * 128, :], t2[:], reads=[t2k]))
                        kb.barrier()
                kb.barrier()

    try:
        stages()
    except StopBuild:
        pass
    toks = []
    if "mod" in dbg:
        toks.append(kb.dma("sp", dbg_out("mod", [128, 48]), mod[:], reads=["mod"]))
    if "h1T" in dbg:
        o = dbg_out("h1T", [128, 8, S], BF16)
        toks.append(kb.dma("sp", o, h1T[:], reads=["h1T"]))
    kb.barrier()
    for e in ("sp",):
        for q in kb.dsem:
            for i in range(kb.nslot):
                if kb.dval[q][i] > 0:
                    kb._wait(e, (kb.dsem[q][i], kb.dval[q][i], "dma"))
    return kb


_CONST_CACHE = {}


def nsa_consts(inputs, b):
    f = np.float32
    m = {}
    m["pos"] = np.ascontiguousarray(np.asarray(inputs["positions"][b], dtype=np.int32).reshape(1, S))
    w_in = np.asarray(inputs["w_in"][0], dtype=f)
    rot = (np.arange(16) + 8) % 16
    m["w_qrot"] = np.ascontiguousarray(np.concatenate([w_in[:, 3088 + hd * 64 + rot] for hd in range(16)], axis=1))
    cols = []
    for base in (4112, 4368, 4624):
        for g in range(2):
            cols.append(w_in[:, base + g * 64 + rot])
    m["w_krot"] = np.ascontiguousarray(np.concatenate(cols, axis=1))
    w1 = np.stack([np.asarray(inputs["nsa_ck_w1"][0], dtype=f), np.asarray(inputs["nsa_cv_w1"][0], dtype=f)])
    m["cmp_w1"] = np.ascontiguousarray(w1.reshape(2, 32, 64, 64).transpose(0, 2, 1, 3).reshape(2, 64, 2048))
    m["cmp_w2"] = np.ascontiguousarray(np.stack([np.asarray(inputs["nsa_ck_w2"][0], dtype=f), np.asarray(inputs["nsa_cv_w2"][0], dtype=f)]))
    pe = np.stack([np.asarray(inputs["nsa_pe_k"][0], dtype=f), np.asarray(inputs["nsa_pe_v"][0], dtype=f)])
    m["cmp_peT"] = np.ascontiguousarray(pe.transpose(0, 2, 1))
    m["w_branch_b"] = np.ascontiguousarray(inputs["w_branch_b"][0], dtype=f)
    if "c" not in _CONST_CACHE:
        c = {}
        inv = (500000.0 ** (-np.arange(8, dtype=np.float32) * 2.0 / 16)).astype(f)
        rc = np.zeros((16, 4), dtype=f)
        rc[:, 0] = np.concatenate([inv, inv])
        rc[:, 1] = np.pi / 2
        rc[0:8, 2] = np.pi
        rc[8:16, 2] = 0.0
        c["rope_c"] = rc
        n = np.arange(127)[:, None]
        j = np.arange(32)[None, :]
        c["ovl"] = ((n * 16 < (j + 1) * 64) & (n * 16 + 32 > j * 64)).astype(f)
        t = np.arange(S)[None, :]
        c["cmpbias"] = np.where(n * 16 + 31 <= t, 0.0, -30000.0).astype(f)
        k = np.arange(S)[None, :]
        c["blockind"] = (k // 64 == np.arange(32)[:, None]).astype(f)
        kk = np.arange(128)[:, None]
        tt = np.arange(128)[None, :]
        c["causalbias"] = np.where(kk > tt, -30000.0, 0.0).astype(f)
        c["antibias"] = np.where(kk <= tt, -30000.0, 0.0).astype(f)
        tok = (np.arange(NT)[None, :, None] * 128 + np.arange(128)[:, None, None])
        cur = tok // 64
        jj = np.arange(32)[None, None, :]
        c["Msel"] = ((jj >= 1) & (jj <= cur - 2)).astype(f)
        c["Fsel"] = ((jj == 0) | (jj == cur) | (jj == cur - 1)).astype(f)
        _CONST_CACHE["c"] = c
    m.update(_CONST_CACHE["c"])
    return m


def host_layout(inputs, b):
    f = np.float32
    m = {}
    m["x"] = np.ascontiguousarray(inputs["x"][b], dtype=f)
    m["c_l"] = np.ascontiguousarray(np.asarray(inputs["c"][b], dtype=f).reshape(8, 128).T)
    m["ada_w"] = np.ascontiguousarray(inputs["ada_w"][0], dtype=f)
    m["ada_b_l"] = np.ascontiguousarray(np.asarray(inputs["ada_b"][0], dtype=f).reshape(48, 128).T)
    m["g1_l"] = np.ascontiguousarray(np.asarray(inputs["norm1_g"][0], dtype=f).reshape(8, 128).T)
    m["g2_l"] = np.ascontiguousarray(np.asarray(inputs["norm2_g"][0], dtype=f).reshape(8, 128).T)
    m["gf_l"] = np.ascontiguousarray(np.asarray(inputs["final_g"], dtype=f).reshape(8, 128).T)
    m["ident"] = np.eye(128, dtype=f)
    m["LT"] = np.triu(np.ones((128, 128), dtype=f))
    m["SU"] = np.tril(np.ones((128, 128), dtype=f), -1)
    m["w_in"] = np.ascontiguousarray(inputs["w_in"][0], dtype=f)
    m["wa2"] = np.ascontiguousarray(inputs["gla_wa2"][0], dtype=f)
    m["ba2"] = np.ascontiguousarray(np.asarray(inputs["gla_ba2"][0], dtype=f).reshape(1, 512))
    m["gng_l"] = np.ascontiguousarray(np.asarray(inputs["gla_norm_g"][0], dtype=f).reshape(2, 128).T)
    m.update(nsa_consts(inputs, b))
    m["w_out"] = np.ascontiguousarray(inputs["w_out"][0], dtype=f)
    m["peer_wq"] = np.ascontiguousarray(inputs["peer_wq"][0], dtype=f)
    m["peer_k"] = np.ascontiguousarray(np.stack([np.asarray(inputs["peer_k1"][0], dtype=f), np.asarray(inputs["peer_k2"][0], dtype=f)]))
    m["peer_u"] = np.ascontiguousarray(inputs["peer_u"][0], dtype=f)
    m["peer_v"] = np.ascontiguousarray(inputs["peer_v"][0], dtype=f)
    m["w_branch_a"] = np.ascontiguousarray(inputs["w_branch_a"][0], dtype=f)
    return m


def run(inputs, dbg=(), ncores=8):
    kb = build(dbg)
    in_maps = [host_layout(inputs, b) for b in range(ncores)]
    res = run_bass_kernel_spmd(kb.nc, in_maps, core_ids=list(range(ncores)))
    return res.results


def kernel(**inputs):
    res = run(inputs)
    return np.stack([np.asarray(r["out"], dtype=np.float32) for r in res], axis=0)
```

```python
import contextlib
import numpy as np
import concourse.bass as bass
import concourse.mybir as mybir
from concourse.alu_op_type import AluOpType as ALU
from concourse.bass_utils import run_bass_kernel_spmd

AF = mybir.ActivationFunctionType
F32 = mybir.dt.float32
BF16 = mybir.dt.bfloat16
I32 = mybir.dt.int32
U32 = mybir.dt.uint32

S = 2048
D = 1024
NT = S // 128
EPS = 1e-6


class StopBuild(Exception):
    pass


def kstop(name):
    import os
    if os.environ.get('KSTOP', '') == name:
        raise StopBuild()


class KB:
    def __init__(self):
        self.nc = bass.Bass("TRN2", target_bir_lowering=False)
        nc = self.nc
        self.es = contextlib.ExitStack()
        self.eng = dict(pe=nc.tensor, act=nc.scalar, dve=nc.vector, pool=nc.gpsimd, sp=nc.sync)
        self.sem = {e: self.es.enter_context(nc.semaphore("s_" + e)) for e in self.eng}
        self.cnt = {e: 0 for e in self.eng}
        self.nslot = 10
        self.dsem = {q: [self.es.enter_context(nc.semaphore("d_%s%d" % (q, i))) for i in range(self.nslot)]
                     for q in ("sp", "pool", "act")}
        self.dval = {q: [0] * self.nslot for q in self.dsem}
        self.dnext = {q: 0 for q in self.dsem}
        self.waited = {}
        self.lastw = {}
        self.readers = {}
        self.ninst = 0

    def _wait(self, e, tok):
        sem, val, prod = tok
        if prod == "pe" and e == "pe":
            return
        key = (e, id(sem))
        if self.waited.get(key, 0) >= val:
            return
        self.eng[e].wait_ge(sem, val)
        self.waited[key] = val

    def _deps(self, e, reads, writes):
        for k in list(reads) + list(writes):
            t = self.lastw.get(k)
            if t is not None:
                self._wait(e, t)
        for k in reads:
            if isinstance(k, tuple) and k[0] == "ps":
                for t in self.readers.get(k, {}).values():
                    if t[2] != e:
                        self._wait(e, t)
        for k in writes:
            for t in self.readers.get(k, {}).values():
                self._wait(e, t)

    def _record(self, tok, reads, writes):
        for k in writes:
            self.lastw[k] = tok
            self.readers[k] = {}
        for k in reads:
            self.readers.setdefault(k, {})[id(tok[0])] = tok

    def op(self, e, fn, reads=(), writes=()):
        self._deps(e, reads, writes)
        inst = fn(self.eng[e])
        self.cnt[e] += 1
        inst.then_inc(self.sem[e], 1)
        tok = (self.sem[e], self.cnt[e], e)
        self._record(tok, reads, writes)
        self.ninst += 1
        return inst

    def dma(self, q, out, in_, reads=(), writes=(), **kw):
        slot = self.dnext[q]
        self.dnext[q] = (slot + 1) % self.nslot
        sem = self.dsem[q][slot]
        if self.dval[q][slot] > 0:
            self._wait(q, (sem, self.dval[q][slot], "dma"))
        self._deps(q, reads, writes)
        inst = self.eng[q].dma_start(out=out, in_=in_, **kw)
        self.dval[q][slot] += 16
        inst.then_inc(sem, 16)
        tok = (sem, self.dval[q][slot], "dma")
        self._record(tok, reads, writes)
        self.ninst += 1
        return tok

    def barrier(self):
        toks = [(self.sem[e], self.cnt[e], e) for e in self.eng if self.cnt[e] > 0]
        for q in self.dsem:
            for i in range(self.nslot):
                if self.dval[q][i] > 0:
                    toks.append((self.dsem[q][i], self.dval[q][i], "dma"))
        for e in self.eng:
            for t in toks:
                self._wait(e, t)
        self.lastw.clear()
        self.readers.clear()

    def final_wait(self, toks):
        for t in toks:
            self._wait("sp", t)


def build(dbg=()):
    kb = KB()
    nc = kb.nc
    es = kb.es
    dbg = set(dbg)

    def din(name, shape, dt=F32):
        return nc.dram_tensor(name, list(shape), dt, kind="ExternalInput").ap()

    def dout(name, shape, dt=F32):
        return nc.dram_tensor(name, list(shape), dt, kind="ExternalOutput").ap()

    uid = [0]

    def sb(stack, name, shape, dt):
        uid[0] += 1
        return stack.enter_context(nc.sbuf_tensor("sb%d_%s" % (uid[0], name), list(shape), dt))

    x_d = din("x", [S, D])
    c_d = din("c_l", [128, 8])
    adaw_d = din("ada_w", [D, 6 * D])
    adab_d = din("ada_b_l", [128, 48])
    g1_d = din("g1_l", [128, 8])
    g2_d = din("g2_l", [128, 8])
    gf_d = din("gf_l", [128, 8])
    ident_d = din("ident", [128, 128])
    out_d = dout("out", [S, D])
    dbg_t = {}
    OUT_TOKS = []

    def dbg_out(name, shape, dt=F32):
        dbg_t[name] = dout("dbg_" + name, shape, dt)
        return dbg_t[name]

    psb = [es.enter_context(nc.psum_tensor("ps%d" % i, [128, 512], F32)) for i in range(8)]
    ps_rr = [0]

    def ps_next():
        i = ps_rr[0]
        ps_rr[0] = (i + 1) % 6
        return i

    ps_acc = [0]

    def ps_next_acc():
        i = ps_acc[0]
        ps_acc[0] = (i + 1) % 2
        return 6 + i

    G = es
    ident_f = sb(G, "ident_f", [128, 128], F32)
    ident_b = sb(G, "ident_b", [128, 128], BF16)
    c_sb = sb(G, "c_sb", [128, 8], F32)
    sc_sb = sb(G, "sc_sb", [128, 8], F32)
    adab_sb = sb(G, "adab_sb", [128, 48], F32)
    mod = sb(G, "mod", [128, 48], F32)
    g1 = sb(G, "g1", [128, 8], F32)
    g2 = sb(G, "g2", [128, 8], F32)
    gf = sb(G, "gf", [128, 8], F32)
    gs1 = sb(G, "gs1", [128, 8], F32)
    gs2 = sb(G, "gs2", [128, 8], F32)
    h1T = sb(G, "h1T", [128, 8, S], BF16)

    kb.dma("sp", ident_f[:], ident_d, writes=["ident_f"])
    kb.dma("sp", c_sb[:], c_d, writes=["c"])
    kb.dma("sp", adab_sb[:], adab_d, writes=["adab"])
    kb.dma("sp", g1[:], g1_d, writes=["g1"])
    kb.dma("sp", g2[:], g2_d, writes=["g2"])
    kb.dma("sp", gf[:], gf_d, writes=["gf"])
    kb.op("dve", lambda e: e.tensor_copy(out=ident_b[:], in_=ident_f[:]), reads=["ident_f"], writes=["ident_b"])

    kb.op("act", lambda e: e.activation(out=sc_sb[:], in_=c_sb[:], func=AF.Silu), reads=["c"], writes=["sc"])
    with contextlib.ExitStack() as P1:
        wbuf = [sb(P1, "adaw%d" % i, [128, 8, 512], F32) for i in range(2)]
        pm = ps_next()
        adaw_v = adaw_d.rearrange("(kc p) n -> p kc n", p=128)
        for blk in range(12):
            wb = wbuf[blk % 2]
            wk = "adaw%d" % (blk % 2)
            kb.dma("sp", wb[:], adaw_v[:, :, blk * 512:(blk + 1) * 512], writes=[wk])
            for mm in range(4):
                m = blk * 4 + mm
                for kc in range(8):
                    kb.op("pe", lambda e, m=m, mm=mm, kc=kc, wb=wb: e.matmul(
                        psb[pm][:, m:m + 1], lhsT=wb[:, kc, mm * 128:(mm + 1) * 128], rhs=sc_sb[:, kc:kc + 1],
                        start=(kc == 0), stop=(kc == 7)), reads=[wk, "sc"], writes=[("ps", pm)])
        kb.op("dve", lambda e: e.tensor_tensor(out=mod[:], in0=psb[pm][:, 0:48], in1=adab_sb[:], op=ALU.add),
              reads=[("ps", pm), "adab"], writes=["mod"])
        kb.op("dve", lambda e: e.scalar_tensor_tensor(out=gs1[:], in0=mod[:, 8:16], scalar=1.0, in1=g1[:],
                                                      op0=ALU.add, op1=ALU.mult), reads=["mod", "g1"], writes=["gs1"])
        kb.op("dve", lambda e: e.scalar_tensor_tensor(out=gs2[:], in0=mod[:, 32:40], scalar=1.0, in1=g2[:],
                                                      op0=ALU.add, op1=ALU.mult), reads=["mod", "g2"], writes=["gs2"])
        kb.barrier()

    def norm_mod_T(stack_name, src_tile_loader, gs, shift_col0, dstT, dst_key):
        with contextlib.ExitStack() as P2:
            xt = [sb(P2, stack_name + "x%d" % i, [128, D], F32) for i in range(2)]
            xn = [sb(P2, stack_name + "xn%d" % i, [128, D], BF16) for i in range(2)]
            junk = sb(P2, stack_name + "junk", [128, D], BF16)
            ss = [sb(P2, stack_name + "ss%d" % i, [128, 4], F32) for i in range(2)]
            for i in range(NT):
                b = i % 2
                kx, kn, ks = stack_name + "x%d" % b, stack_name + "xn%d" % b, stack_name + "ss%d" % b
                src_tile_loader(i, xt[b], kx)
                kb.op("act", lambda e, b=b: e.activation(out=junk[:], in_=xt[b][:], func=AF.Square,
                                                         accum_out=ss[b][:, 0:1]), reads=[kx], writes=[ks, stack_name + "junk"])
                kb.op("dve", lambda e, b=b: e.tensor_scalar(out=ss[b][:, 1:2], in0=ss[b][:, 0:1], scalar1=1.0 / D,
                                                            scalar2=EPS, op0=ALU.mult, op1=ALU.add), reads=[ks], writes=[ks])
                kb.op("act", lambda e, b=b: e.activation(out=ss[b][:, 2:3], in_=ss[b][:, 1:2], func=AF.Sqrt),
                      reads=[ks], writes=[ks])
                kb.op("dve", lambda e, b=b: e.reciprocal(out=ss[b][:, 3:4], in_=ss[b][:, 2:3]), reads=[ks], writes=[ks])
                kb.op("act", lambda e, b=b: e.activation(out=xn[b][:], in_=xt[b][:], func=AF.Identity,
                                                         scale=ss[b][:, 3:4]), reads=[kx, ks], writes=[kn])
                p = ps_next()
                pv = psb[p][:].bitcast(BF16)
                for dc in range(8):
                    kb.op("pe", lambda e, b=b, dc=dc, pv=pv: e.transpose(
                        out=pv[:, dc * 128:(dc + 1) * 128], in_=xn[b][:, dc * 128:(dc + 1) * 128], identity=ident_b[:]),
                        reads=[kn, "ident_b"], writes=[("ps", p)])
                for dc in range(8):
                    eng = "dve" if dc % 2 == 0 else "pool"
                    eng = "dve"
                    kb.op(eng, lambda e, dc=dc, pv=pv, i=i: e.tensor_scalar(
                        out=dstT[:, dc, i * 128:(i + 1) * 128], in0=pv[:, dc * 128:(dc + 1) * 128],
                        scalar1=gs[:, dc:dc + 1], scalar2=mod[:, shift_col0 + dc:shift_col0 + dc + 1],
                        op0=ALU.mult, op1=ALU.add), reads=[("ps", p), "gs1", "gs2", "mod"], writes=[dst_key])
            kb.barrier()

    def load_x(i, dst, key):
        kb.dma("sp", dst[:], x_d[i * 128:(i + 1) * 128, :], writes=[key])

    norm_mod_T("n1", load_x, gs1, 0, h1T, "h1T")

    def stages():
        LT_d = din("LT", [128, 128])
        SU_d = din("SU", [128, 128])
        win_d = din("w_in", [D, 6976])
        wa2_d = din("wa2", [16, 512])
        ba2_d = din("ba2", [1, 512])
        gng_d = din("gng_l", [128, 2])
        wba_d = din("w_branch_a", [D, D])
        LT = sb(G, "LT", [128, 128], F32)
        SU = sb(G, "SU", [128, 128], F32)
        ones_bf = sb(G, "ones_bf", [128, 128], BF16)
        ones_f = sb(G, "ones_f", [128, 128], F32)
        kb.dma("sp", LT[:], LT_d, writes=["LT"])
        kb.dma("sp", SU[:], SU_d, writes=["SU"])
        kb.op("dve", lambda e: e.memset(ones_bf[:], 1.0), writes=["ones_bf"])
        kb.op("dve", lambda e: e.memset(ones_f[:], 1.0), writes=["ones_f"])

        class Rot:
            def __init__(self, stack, name, shape, dt, n=2):
                self.t = [sb(stack, "%s_%d" % (name, i), shape, dt) for i in range(n)]
                self.k = ["%s_%d" % (name, i) for i in range(n)]
                self.i = 0

            def next(self):
                j = self.i
                self.i = (j + 1) % len(self.t)
                return self.t[j], self.k[j]

        def load_w(dst, key, src, ncols):
            kb.dma("pool", dst[:, :, 0:ncols], src.rearrange("(kc p) n -> p kc n", p=128), writes=[key])

        def proj_fm(w, wkey, c0, m, hT, hkeys, t0, nt, ps_ap, pskey):
            for kc in range(8):
                kb.op("pe", lambda e, kc=kc: e.matmul(ps_ap, lhsT=w[:, kc, c0:c0 + m], rhs=hT[:, kc, t0:t0 + nt],
                                                      start=(kc == 0), stop=(kc == 7)),
                      reads=[wkey] + hkeys, writes=[pskey])

        def proj_tm(w, wkey, c0, n, hT, hkeys, i, ps_ap, pskey):
            for kc in range(8):
                kb.op("pe", lambda e, kc=kc: e.matmul(ps_ap, lhsT=hT[:, kc, i * 128:(i + 1) * 128], rhs=w[:, kc, c0:c0 + n],
                                                      start=(kc == 0), stop=(kc == 7)),
                      reads=[wkey] + hkeys, writes=[pskey])

        mT = sb(G, "mT", [128, 8, S], BF16)
        H1K = ["h1T"]
        ev_rr = [0]

        def evac_copy(out_ap, in_ap, reads, writes):
            e = "act" if ev_rr[0] % 2 == 0 else "dve"
            ev_rr[0] += 1
            if e == "act":
                kb.op("act", lambda en: en.copy(out=out_ap, in_=in_ap), reads=reads, writes=writes)
            else:
                kb.op("dve", lambda en: en.tensor_copy(out=out_ap, in_=in_ap), reads=reads, writes=writes)

        with contextlib.ExitStack() as GS:
            yaT = sb(GS, "yaT", [128, 8, S], BF16)
            lrT = sb(GS, "lrT", [16, S], F32)
            wb = Rot(GS, "wb", [128, 8, 512], BF16, 2)
            wa2 = sb(GS, "wa2", [16, 512], F32)
            ba2 = sb(GS, "ba2", [1, 512], F32)
            gng = sb(GS, "gng", [128, 2], F32)
            kb.dma("sp", wa2[:], wa2_d, writes=["wa2"])
            kb.dma("sp", ba2[:], ba2_d, writes=["ba2"])
            kb.dma("sp", gng[:], gng_d, writes=["gng"])
            w, wk = wb.next()
            load_w(w, wk, win_d[:, 3072:3088], 16)
            for tc in range(4):
                p = ps_next()
                proj_fm(w, wk, 0, 16, h1T, H1K, tc * 512, 512, psb[p][0:16, :], ("ps", p))
                kb.op("act", lambda e, p=p, tc=tc: e.copy(out=lrT[:, tc * 512:(tc + 1) * 512], in_=psb[p][0:16, :]),
                      reads=[("ps", p)], writes=["lrT"])
            for blk in range(2):
                w, wk = wb.next()
                load_w(w, wk, win_d[:, 2048 + blk * 512:2048 + (blk + 1) * 512], 512)
                for mt in range(4):
                    for tc in range(4):
                        p = ps_next()
                        proj_fm(w, wk, mt * 128, 128, h1T, H1K, tc * 512, 512, psb[p][:, :], ("ps", p))
                        kb.op("act", lambda e, p=p, tc=tc, ti=blk * 4 + mt: e.activation(
                            out=yaT[:, ti, tc * 512:(tc + 1) * 512], in_=psb[p][:, :], func=AF.Silu),
                            reads=[("ps", p)], writes=[("yaT", blk * 4 + mt)])
            for hp in range(2):
                with contextlib.ExitStack() as HP:
                    qT = sb(HP, "qT", [128, 2, S], BF16)
                    kT = sb(HP, "kT", [128, 2, S], BF16)
                    ktm = sb(HP, "ktm", [128, NT, 256], BF16)
                    vtm = sb(HP, "vtm", [128, NT, 512], BF16)
                    state = sb(HP, "state", [128, 2, 256], F32)
                    state_bf = sb(HP, "state_bf", [128, 2, 256], BF16)
                    r_e1 = Rot(HP, "e1", [128, 256], F32)
                    r_l = Rot(HP, "l", [128, 256], F32)
                    r_er = Rot(HP, "er", [128, 256], F32)
                    r_kh = Rot(HP, "kh", [128, 256], BF16)
                    r_eq = Rot(HP, "eq", [128, 128], F32, 3)
                    r_ek = Rot(HP, "ek", [128, 128], F32, 3)
                    r_qt = Rot(HP, "qt", [128, 128], BF16, 3)
                    r_kt = Rot(HP, "kt", [128, 128], BF16, 3)
                    r_pT = Rot(HP, "pT", [128, 128], BF16, 3)
                    r_sq = Rot(HP, "sq", [128, 256], BF16, 3)
                    r_rt = Rot(HP, "rt", [128, 128], F32, 3)
                    r_rs = Rot(HP, "rs", [128, 128], F32, 3)
                    r_t1 = Rot(HP, "t1", [128, 256], F32, 3)
                    kb.op("dve", lambda e: e.memset(state[:], 0.0), writes=["state0", "state1"])
                    kb.op("dve", lambda e: e.memset(state_bf[:], 0.0), writes=["sbf0", "sbf1"])
                    for which, dst, cbase in (("q", qT, 0), ("k", kT, 512)):
                        w, wk = wb.next()
                        load_w(w, wk, win_d[:, cbase + hp * 256:cbase + (hp + 1) * 256], 256)
                        for hh in range(2):
                            for tc in range(4):
                                p = ps_next()
                                proj_fm(w, wk, hh * 128, 128, h1T, H1K, tc * 512, 512, psb[p][:, :], ("ps", p))
                                evac_copy(dst[:, hh, tc * 512:(tc + 1) * 512], psb[p][:, :], [("ps", p)], [(which + "T", hh)])
                        if which == "k":
                            for i in range(NT):
                                p = ps_next()
                                proj_tm(w, wk, 0, 256, h1T, H1K, i, psb[p][:, 0:256], ("ps", p))
                                evac_copy(ktm[:, i, :], psb[p][:, 0:256], [("ps", p)], [("ktm", i)])
                    w, wk = wb.next()
                    load_w(w, wk, win_d[:, 1024 + hp * 512:1024 + (hp + 1) * 512], 512)
                    for i in range(NT):
                        p = ps_next()
                        proj_tm(w, wk, 0, 512, h1T, H1K, i, psb[p][:, :], ("ps", p))
                        evac_copy(vtm[:, i, :], psb[p][:, :], [("ps", p)], [("vtm", i)])
                    for i in range(NT):
                        ts = slice(i * 128, (i + 1) * 128)
                        p = ps_next()
                        kb.op("pe", lambda e, p=p: e.matmul(psb[p][:, 0:256], lhsT=lrT[:, ts], rhs=wa2[:, hp * 256:(hp + 1) * 256],
                                                            start=True, stop=False), reads=["lrT", "wa2"], writes=[("ps", p)])
                        kb.op("pe", lambda e, p=p: e.matmul(psb[p][:, 0:256], lhsT=ones_f[0:1, 0:128], rhs=ba2[0:1, hp * 256:(hp + 1) * 256],
                                                            start=False, stop=True), reads=["ones_f", "ba2"], writes=[("ps", p)])
                        e1, e1k = r_e1.next()
                        lt, lk = r_l.next()
                        kb.op("act", lambda e, p=p, e1=e1: e.activation(out=e1[:], in_=psb[p][:, 0:256], func=AF.Exp, scale=-1.0),
                              reads=[("ps", p)], writes=[e1k])
                        kb.op("act", lambda e, e1=e1, lt=lt: e.activation(out=lt[:], in_=e1[:], func=AF.Ln, bias=1.0),
                              reads=[e1k], writes=[lk])
                        p = ps_next()
                        kb.op("pe", lambda e, p=p, lt=lt: e.matmul(psb[p][:, 0:256], lhsT=SU[:], rhs=lt[:], start=True, stop=True),
                              reads=["SU", lk], writes=[("ps", p)])
                        er, erk = r_er.next()
                        kb.op("act", lambda e, p=p, er=er: e.activation(out=er[:], in_=psb[p][:, 0:256], func=AF.Exp, scale=-1.0 / 16),
                              reads=[("ps", p)], writes=[erk])
                        kh, khk = r_kh.next()
                        kb.op("dve", lambda e, kh=kh, er=er: e.tensor_tensor(out=kh[:], in0=ktm[:, i, :], in1=er[:], op=ALU.mult),
                              reads=[("ktm", i), erk], writes=[khk])
                        for hh in range(2):
                            h = hp * 2 + hh
                            p = ps_next()
                            kb.op("pe", lambda e, p=p, lt=lt: e.matmul(psb[p][:, 0:128], lhsT=lt[:, hh * 128:(hh + 1) * 128], rhs=LT[:],
                                                                      start=True, stop=True), reads=[lk, "LT"], writes=[("ps", p)])
                            eq, eqk = r_eq.next()
                            ek, ekk = r_ek.next()
                            kb.op("act", lambda e, p=p, eq=eq: e.activation(out=eq[:], in_=psb[p][:, 0:128], func=AF.Exp, scale=-1.0 / 16),
                                  reads=[("ps", p)], writes=[eqk])
                            kb.op("act", lambda e, p=p, ek=ek: e.activation(out=ek[:], in_=psb[p][:, 0:128], func=AF.Exp, scale=1.0 / 16),
                                  reads=[("ps", p)], writes=[ekk])
                            qt, qtk = r_qt.next()
                            kt, ktk = r_kt.next()
                            kb.op("dve", lambda e, qt=qt, eq=eq: e.scalar_tensor_tensor(
                                out=qt[:], in0=qT[:, hh, ts], scalar=float(128 ** -0.5), in1=eq[:], op0=ALU.mult, op1=ALU.mult),
                                reads=[("qT", hh), eqk], writes=[qtk])
                            kb.op("dve", lambda e, kt=kt, ek=ek: e.tensor_tensor(out=kt[:], in0=kT[:, hh, ts], in1=ek[:], op=ALU.mult),
                                  reads=[("kT", hh), ekk], writes=[ktk])
                            p = ps_next()
                            kb.op("pe", lambda e, p=p, kt=kt, qt=qt: e.matmul(psb[p][:, 0:128], lhsT=kt[:], rhs=qt[:], start=True, stop=True),
                                  reads=[ktk, qtk], writes=[("ps", p)])
                            pT, pTk = r_pT.next()
                            kb.op("dve", lambda e, p=p, pT=pT: e.tensor_tensor(out=pT[:], in0=psb[p][:, 0:128], in1=LT[:], op=ALU.mult),
                                  reads=[("ps", p), "LT"], writes=[pTk])
                            po = ps_next()
                            for vh in range(2):
                                kb.op("pe", lambda e, po=po, vh=vh, pT=pT: e.matmul(
                                    psb[po][:, vh * 128:(vh + 1) * 128], lhsT=vtm[:, i, hh * 256 + vh * 128:hh * 256 + (vh + 1) * 128],
                                    rhs=pT[:], start=True, stop=False), reads=[("vtm", i), pTk], writes=[("ps", po)])
                                kb.op("pe", lambda e, po=po, vh=vh, qt=qt: e.matmul(
                                    psb[po][:, vh * 128:(vh + 1) * 128], lhsT=state_bf[:, hh, vh * 128:(vh + 1) * 128],
                                    rhs=qt[:], start=False, stop=True), reads=["sbf%d" % hh, qtk], writes=[("ps", po)])
                            sq, sqk = r_sq.next()
                            kb.op("act", lambda e, po=po, sq=sq: e.activation(out=sq[:], in_=psb[po][:, 0:256], func=AF.Square),
                                  reads=[("ps", po)], writes=[sqk])
                            p2 = ps_next()
                            for vh in range(2):
                                kb.op("pe", lambda e, p2=p2, vh=vh, sq=sq: e.matmul(
                                    psb[p2][:, 0:128], lhsT=ones_bf[:], rhs=sq[:, vh * 128:(vh + 1) * 128],
                                    start=(vh == 0), stop=(vh == 1)), reads=["ones_bf", sqk], writes=[("ps", p2)])
                            rt, rtk = r_rt.next()
                            rs, rsk = r_rs.next()
                            kb.op("act", lambda e, p2=p2, rt=rt: e.activation(out=rt[:], in_=psb[p2][:, 0:128], func=AF.Sqrt,
                                                                              scale=1.0 / 256, bias=EPS), reads=[("ps", p2)], writes=[rtk])
                            kb.op("dve", lambda e, rt=rt, rs=rs: e.reciprocal(out=rs[:], in_=rt[:]), reads=[rtk], writes=[rsk])
                            t1, t1k = r_t1.next()
                            for vh in range(2):
                                kb.op("dve", lambda e, po=po, vh=vh, t1=t1, rs=rs: e.scalar_tensor_tensor(
                                    out=t1[:, vh * 128:(vh + 1) * 128], in0=psb[po][:, vh * 128:(vh + 1) * 128],
                                    scalar=gng[:, vh:vh + 1], in1=rs[:], op0=ALU.mult, op1=ALU.mult),
                                    reads=[("ps", po), "gng", rsk], writes=[t1k])
                            for vh in range(2):
                                kb.op("pool", lambda e, vh=vh, t1=t1, h=h: e.tensor_tensor(
                                    out=yaT[:, h * 2 + vh, ts], in0=t1[:, vh * 128:(vh + 1) * 128], in1=yaT[:, h * 2 + vh, ts],
                                    op=ALU.mult), reads=[t1k, ("yaT", h * 2 + vh)], writes=[("yaT", h * 2 + vh)])
                            pk = ps_next()
                            kb.op("pe", lambda e, pk=pk, kh=kh: e.matmul(psb[pk][:, 0:256], lhsT=kh[:, hh * 128:(hh + 1) * 128],
                                                                        rhs=vtm[:, i, hh * 256:(hh + 1) * 256], start=True, stop=True),
                                  reads=[khk, ("vtm", i)], writes=[("ps", pk)])
                            kb.op("dve", lambda e, pk=pk, eq=eq: e.scalar_tensor_tensor(
                                out=state[:, hh, :], in0=state[:, hh, :], scalar=eq[:, 127:128], in1=psb[pk][:, 0:256],
                                op0=ALU.mult, op1=ALU.add), reads=["state%d" % hh, eqk, ("ps", pk)], writes=["state%d" % hh])
                            kb.op("act", lambda e: e.copy(out=state_bf[:, hh, :], in_=state[:, hh, :]),
                                  reads=["state%d" % hh], writes=["sbf%d" % hh])
                    kb.barrier()
            if "yaT" in dbg:
                kb.dma("sp", dbg_out("yaT", [128, 8, S], BF16), yaT[:], reads=[("yaT", i) for i in range(8)])
            for blk in range(2):
                wm, wmk = wb.next()
                load_w(wm, wmk, win_d[:, 4928 + blk * 512:4928 + (blk + 1) * 512], 512)
                wa, wak = wb.next()
                load_w(wa, wak, wba_d[:, blk * 512:(blk + 1) * 512], 512)
                with contextlib.ExitStack() as BS:
                    r_sg = Rot(BS, "sg", [128, 512], F32, 2)
                    for mt in range(4):
                        for tc in range(4):
                            p1 = ps_next()
                            proj_fm(wm, wmk, mt * 128, 128, h1T, H1K, tc * 512, 512, psb[p1][:, :], ("ps", p1))
                            sg, sgk = r_sg.next()
                            kb.op("act", lambda e, p1=p1, sg=sg: e.activation(out=sg[:], in_=psb[p1][:, :], func=AF.Sigmoid),
                                  reads=[("ps", p1)], writes=[sgk])
                            p2 = ps_next()
                            proj_fm(wa, wak, mt * 128, 128, yaT, [("yaT", j) for j in range(8)], tc * 512, 512, psb[p2][:, :], ("ps", p2))
                            kb.op("dve", lambda e, p2=p2, sg=sg, ti=blk * 4 + mt, tc=tc: e.tensor_tensor(
                                out=mT[:, ti, tc * 512:(tc + 1) * 512], in0=psb[p2][:, :], in1=sg[:], op=ALU.mult),
                                reads=[("ps", p2), sgk], writes=[("mT", blk * 4 + mt)])
                    kb.barrier()
            kb.barrier()
        pos_d = nc.dram_tensor("pos", [1, S], I32, kind="ExternalInput").ap()
        ropec_d = din("rope_c", [16, 4])
        wqrot_d = din("w_qrot", [D, 256])
        wkrot_d = din("w_krot", [D, 96])
        cw1_d = din("cmp_w1", [2, 64, 32 * 64])
        cw2_d = din("cmp_w2", [2, 64, 64])
        cpe_d = din("cmp_peT", [2, 64, 32])
        ovl_d = din("ovl", [127, 32])
        cmpb_d = din("cmpbias", [127, S])
        blk_d = din("blockind", [32, S])
        cb_d = din("causalbias", [128, 128])
        ab_d = din("antibias", [128, 128])
        Msel_d = din("Msel", [128, NT, 32])
        Fsel_d = din("Fsel", [128, NT, 32])
        wbb_d = din("w_branch_b", [D, D])
        NEG = 30000.0

        with contextlib.ExitStack() as NS:
            zeros_bf = sb(NS, "zeros_bf", [128, 512], BF16)
            kb.op("pool", lambda e: e.memset(zeros_bf[:], 0.0), writes=["zeros_bf"])
            cosT = sb(NS, "cosT", [16, S], BF16)
            sinT = sb(NS, "sinT", [16, S], BF16)
            gates = sb(NS, "gates", [128, NT, 48], F32)
            ovl_f = sb(NS, "ovl_f", [127, 32], F32)
            cmpb = sb(NS, "cmpb", [127, S], BF16)
            cb = sb(NS, "cb", [128, 128], BF16)
            ab = sb(NS, "ab", [128, 128], BF16)
            Msel = sb(NS, "Msel", [128, NT, 32], F32)
            Fsel = sb(NS, "Fsel", [128, NT, 32], F32)
            kb.dma("sp", ovl_f[:], ovl_d, writes=["ovl_f"])
            kb.dma("pool", cmpb[:], cmpb_d, writes=["cmpb"])
            kb.dma("pool", cb[:], cb_d, writes=["cb"])
            kb.dma("pool", ab[:], ab_d, writes=["ab"])
            kb.dma("sp", Msel[:], Msel_d, writes=["Msel"])
            kb.dma("sp", Fsel[:], Fsel_d, writes=["Fsel"])
            with contextlib.ExitStack() as RS:
                posi = sb(RS, "posi", [16, S], I32)
                posf = sb(RS, "posf", [16, S], F32)
                ang = sb(RS, "ang", [16, S], F32)
                ki = sb(RS, "ki", [16, S], I32)
                kf = sb(RS, "kf", [16, S], F32)
                rr = sb(RS, "rr", [16, S], F32)
                ropec = sb(RS, "ropec", [16, 4], F32)
                kb.dma("sp", posi[:], pos_d.broadcast_to([16, S]), writes=["posi"])
                kb.dma("sp", ropec[:], ropec_d, writes=["ropec"])
                kb.op("dve", lambda e: e.tensor_copy(out=posf[:], in_=posi[:]), reads=["posi"], writes=["posf"])
                for col, dst, dk in ((1, cosT, "cosT"), (2, sinT, "sinT")):
                    kb.op("dve", lambda e, col=col: e.tensor_scalar(out=ang[:], in0=posf[:], scalar1=ropec[:, 0:1], scalar2=ropec[:, col:col + 1],
                                                                    op0=ALU.mult, op1=ALU.add), reads=["posf", "ropec"], writes=["ang"])
                    kb.op("dve", lambda e: e.tensor_scalar(out=ki[:], in0=ang[:], scalar1=float(1.0 / (2 * np.pi)), scalar2=None,
                                                           op0=ALU.mult), reads=["ang"], writes=["ki"])
                    kb.op("dve", lambda e: e.tensor_copy(out=kf[:], in_=ki[:]), reads=["ki"], writes=["kf"])
                    kb.op("dve", lambda e: e.scalar_tensor_tensor(out=rr[:], in0=kf[:], scalar=float(-2 * np.pi), in1=ang[:],
                                                                  op0=ALU.mult, op1=ALU.add), reads=["kf", "ang"], writes=["rr"])
                    kb.op("dve", lambda e: e.tensor_scalar(out=kf[:], in0=rr[:], scalar1=float(np.pi), scalar2=float(2 * np.pi),
                                                           op0=ALU.is_gt, op1=ALU.mult), reads=["rr"], writes=["kf"])
                    kb.op("dve", lambda e: e.tensor_tensor(out=rr[:], in0=rr[:], in1=kf[:], op=ALU.subtract), reads=["rr", "kf"], writes=["rr"])
                    kb.op("dve", lambda e: e.tensor_scalar(out=kf[:], in0=rr[:], scalar1=float(-np.pi), scalar2=float(2 * np.pi),
                                                           op0=ALU.is_lt, op1=ALU.mult), reads=["rr"], writes=["kf"])
                    kb.op("dve", lambda e: e.tensor_tensor(out=rr[:], in0=rr[:], in1=kf[:], op=ALU.add), reads=["rr", "kf"], writes=["rr"])
                    kb.op("act", lambda e, dst=dst: e.activation(out=dst[:], in_=rr[:], func=AF.Sin), reads=["rr"], writes=[dk])
                kb.barrier()
            kstop("rope")
            wbn = Rot(NS, "wbn", [128, 8, 512], BF16, 2)
            w, wk = wbn.next()
            load_w(w, wk, win_d[:, 4880:4928], 48)
            for i in range(NT):
                p = ps_next()
                proj_tm(w, wk, 0, 48, h1T, H1K, i, psb[p][:, 0:48], ("ps", p))
                kb.op("act", lambda e, p=p, i=i: e.activation(out=gates[:, i, :], in_=psb[p][:, 0:48], func=AF.Sigmoid),
                      reads=[("ps", p)], writes=["gates"])

            kstop("gates")
            for g in range(2):
                with contextlib.ExitStack() as GR:
                    Qa = [sb(GR, "Qa%d" % hh, [96, S], BF16) for hh in range(8)]
                    kcT = sb(GR, "kcT", [64, S], BF16)
                    ksT = sb(GR, "ksT", [96, S], BF16)
                    kwT = sb(GR, "kwT", [64, S], BF16)
                    vs_a = sb(GR, "vs_a", [128, NT, 65], BF16)
                    vw_a = sb(GR, "vw_a", [128, NT, 65], BF16)
                    vcmp = sb(GR, "vcmp", [127, 97], BF16)
                    kcmpT = sb(GR, "kcmpT", [64, 127], BF16)
                    yb = sb(GR, "yb", [128, NT, 512], BF16)
                    score = sb(GR, "score", [128, NT, 32], F32)
                    PJ = contextlib.ExitStack()
                    vcT = sb(PJ, "vcT", [64, S], BF16)
                    cw1 = sb(PJ, "cw1", [64, 2, 32 * 64], BF16)
                    cw2 = sb(PJ, "cw2", [64, 2, 64], BF16)
                    cpe = sb(PJ, "cpe", [64, 2, 32], BF16)
                    peb = sb(PJ, "peb", [64, 2], F32)
                    gsb = sb(PJ, "gsb", [64, 2, 127], BF16)
                    r_t1 = Rot(PJ, "rt1", [16, 512], F32, 1)
                    r_t2 = Rot(PJ, "rt2", [16, 512], F32, 1)
                    wqr = sb(PJ, "wqr", [128, 8, 128], BF16)
                    wkr = sb(PJ, "wkr", [128, 8, 96], BF16)
                    wkv = sb(PJ, "wkv", [128, 8, 384], BF16)
                    for wi in range(6):
                        kb.dma("pool", wkv[:, :, wi * 64:(wi + 1) * 64],
                               win_d[:, 4112 + wi * 128 + g * 64:4112 + wi * 128 + (g + 1) * 64].rearrange("(kc p) n -> p kc n", p=128),
                               writes=["wkv"])
                    kb.op("dve", lambda e: e.memset(score[:], 0.0), writes=["score"])
                    kb.op("pool", lambda e: e.memset(vs_a[:, :, 64:65], 1.0), writes=["vs_a"])
                    kb.op("pool", lambda e: e.memset(vw_a[:, :, 64:65], 1.0), writes=["vw_a"])
                    kb.op("pool", lambda e: e.memset(vcmp[:, 64:65], 1.0), writes=["vcmp"])
                    kb.op("dve", lambda e: e.tensor_copy(out=vcmp[:, 65:97], in_=ovl_f[:]), reads=["ovl_f"], writes=["vcmp"])
                    for hh in range(8):
                        kb.op("pool", lambda e, hh=hh: e.memset(Qa[hh][64:96, :], 0.0), writes=[("Qa", hh)])
                    kb.dma("pool", ksT[64:96, :], blk_d, writes=["ksT"])
                    for kv in range(2):
                        kb.dma("pool", cw1[:, kv, :], cw1_d[kv], writes=["cw1"])
                    kb.dma("pool", cw2[:], cw2_d.rearrange("k d c -> d k c"), writes=["cw2"])
                    kb.dma("pool", cpe[:], cpe_d.rearrange("k d l -> d k l"), writes=["cpe"])
                    load_w(wqr, "wqr", wqrot_d[:, g * 128:(g + 1) * 128], 128)
                    load_w(wkr, "wkr", wkrot_d, 96)

                    def rope_evac(dst, dkey, pm, pr, tc):
                        cs = slice(tc * 512, (tc + 1) * 512)
                        kb.op("act", lambda e: e.copy(out=dst[0:64, cs], in_=psb[pm][0:64, :]), reads=[("ps", pm)], writes=[dkey])
                        import os
                        if os.environ.get("KNOROPE"):
                            return
                        t1, t1k = r_t1.next()
                        t2, t2k = r_t2.next()
                        kb.op("dve", lambda e: e.tensor_tensor(out=t1[:], in0=cosT[:, cs], in1=psb[pm][0:16, :], op=ALU.mult),
                              reads=[("ps", pm), "cosT"], writes=[t1k])
                        kb.op("dve", lambda e: e.tensor_tensor(out=t2[:], in0=sinT[:, cs], in1=psb[pr][0:16, :], op=ALU.mult),
                              reads=[("ps", pr), "sinT"], writes=[t2k])
                        if os.environ.get("KNOPOOL"):
                            return
                        kb.op("pool", lambda e: e.tensor_tensor(out=dst[0:16, cs], in0=t1[:], in1=t2[:], op=ALU.add),
                              reads=[t1k, t2k], writes=[dkey])

                    kstop("grsetup")
                    w, wk = wbn.next()
                    load_w(w, wk, win_d[:, 3088 + g * 512:3088 + (g + 1) * 512], 512)
                    for hh in range(8):
                        for tc in range(4):
                            pm = ps_next()
                            proj_fm(w, wk, hh * 64, 64, h1T, H1K, tc * 512, 512, psb[pm][0:64, :], ("ps", pm))
                            pr = ps_next()
                            proj_fm(wqr, "wqr", hh * 16, 16, h1T, H1K, tc * 512, 512, psb[pr][0:16, :], ("ps", pr))
                            rope_evac(Qa[hh], ("Qa", hh), pm, pr, tc)
                    kstop("qproj")
                    for wi, dst, dkey, ri_ in ((0, kcT, "kcT", 0), (2, ksT, "ksT", 1), (4, kwT, "kwT", 2)):
                        for tc in range(4):
                            pm = ps_next()
                            proj_fm(wkv, "wkv", wi * 64, 64, h1T, H1K, tc * 512, 512, psb[pm][0:64, :], ("ps", pm))
                            pr = ps_next()
                            proj_fm(wkr, "wkr", ri_ * 32 + g * 16, 16, h1T, H1K, tc * 512, 512, psb[pr][0:16, :], ("ps", pr))
                            rope_evac(dst, dkey, pm, pr, tc)
                    for tc in range(4):
                        pm = ps_next()
                        proj_fm(wkv, "wkv", 1 * 64, 64, h1T, H1K, tc * 512, 512, psb[pm][0:64, :], ("ps", pm))
                        kb.op("act", lambda e, pm=pm, tc=tc: e.copy(out=vcT[:, tc * 512:(tc + 1) * 512], in_=psb[pm][0:64, :]),
                              reads=[("ps", pm)], writes=["vcT"])
                    for wi, dst, dkey in ((3, vs_a, "vs_a"), (5, vw_a, "vw_a")):
                        for i in range(NT):
                            pm = ps_next()
                            proj_tm(wkv, "wkv", wi * 64, 64, h1T, H1K, i, psb[pm][:, 0:64], ("ps", pm))
                            evac_copy(dst[:, i, 0:64], psb[pm][:, 0:64], [("ps", pm)], [dkey])
                    kstop("kvproj")
                    for kv, srcT, skey in ((0, kcT, "kcT"), (1, vcT, "vcT")):
                        p = ps_next()
                        for l in range(32):
                            kb.op("pe", lambda e, p=p, l=l, kv=kv: e.matmul(psb[p][0:64, 0:1], lhsT=cw1[:, kv, l * 64:(l + 1) * 64],
                                                                            rhs=cpe[:, kv, l:l + 1], start=(l == 0), stop=(l == 31)),
                                  reads=["cw1", "cpe"], writes=[("ps", p)])
                        kb.op("act", lambda e, p=p, kv=kv: e.copy(out=peb[:, kv:kv + 1], in_=psb[p][0:64, 0:1]),
                              reads=[("ps", p)], writes=["peb"])
                        p = ps_next()
                        for l in range(32):
                            kb.op("pe", lambda e, p=p, l=l, kv=kv, srcT=srcT: e.matmul(
                                psb[p][0:64, 0:127], lhsT=cw1[:, kv, l * 64:(l + 1) * 64], rhs=srcT[0:64, l:l + 16 * 126 + 1:16],
                                start=(l == 0), stop=(l == 31)), reads=["cw1", skey], writes=[("ps", p)])
                        kb.op("act", lambda e, p=p, kv=kv: e.activation(out=gsb[:, kv, :], in_=psb[p][0:64, 0:127], func=AF.Gelu,
                                                                        bias=peb[:, kv:kv + 1]), reads=[("ps", p), "peb"], writes=["gsb"])
                    p = ps_next()
                    kb.op("pe", lambda e, p=p: e.matmul(psb[p][0:64, 0:127], lhsT=cw2[:, 0, :], rhs=gsb[:, 0, :], start=True, stop=True),
                          reads=["cw2", "gsb"], writes=[("ps", p)])
                    kb.op("act", lambda e, p=p: e.copy(out=kcmpT[:], in_=psb[p][0:64, 0:127]), reads=[("ps", p)], writes=["kcmpT"])
                    p = ps_next()
                    kb.op("pe", lambda e, p=p: e.matmul(psb[p][0:127, 0:64], lhsT=gsb[:, 1, :], rhs=cw2[:, 1, :], start=True, stop=True),
                          reads=["cw2", "gsb"], writes=[("ps", p)])
                    kb.op("act", lambda e, p=p: e.copy(out=vcmp[:, 0:64], in_=psb[p][0:127, 0:64]), reads=[("ps", p)], writes=["vcmp"])

                    kb.barrier()
                    PJ.close()
                    AT = contextlib.ExitStack()
                    r_PT = Rot(AT, "PT", [128, 512], BF16, 4)
                    r_ri = Rot(AT, "ri", [128, 8], F32, 3)
                    r_tm = Rot(AT, "otmp", [128, 4, 64], F32, 2)
                    r_sc = Rot(AT, "sct", [128, 4, 32], F32, 2)
                    def attn_chunk(hh, tc, K, kT, kkey, kt_list, rhs_of, vkey, vn, pv_map, br):
                        po = ps_next_acc()
                        kb.op("pe", lambda e: e.matmul(psb[po][:, 0:4 * vn], lhsT=zeros_bf[:, 0:128], rhs=zeros_bf[:, 0:4 * vn],
                                                       start=True, stop=True), reads=["zeros_bf"], writes=[("ps", po)])
                        total = sum(len(x) for x in pv_map)
                        done = [0]

                        def emit_pv(entry):
                            j, PT, PTk, nk = entry
                            for qi in range(4):
                                if j in pv_map[qi]:
                                    done[0] += 1
                                    last = (done[0] == total)
                                    kb.op("pe", lambda e, qi=qi: e.matmul(
                                        psb[po][:, qi * vn:(qi + 1) * vn], lhsT=PT[0:nk, qi * 128:(qi + 1) * 128], rhs=rhs_of(j),
                                        start=False, stop=True, skip_group_check=True), reads=[PTk, vkey], writes=[("ps", po)])
                        pend = []
                        for (j, nk, c0, c1, masks) in kt_list:
                            p = ps_next()
                            kb.op("pe", lambda e, p=p, j=j, nk=nk, c0=c0, c1=c1: e.matmul(
                                psb[p][0:nk, c0:c1], lhsT=kT[0:K, j * 128:j * 128 + nk], rhs=Qa[hh][0:K, tc * 512 + c0:tc * 512 + c1],
                                start=True, stop=(len(masks) == 0)), reads=[kkey, ("Qa", hh)], writes=[("ps", p)])
                            for mi, (col0, width, bt, bkey) in enumerate(masks):
                                kb.op("pe", lambda e, p=p, nk=nk, col0=col0, width=width, bt=bt, mi=mi: e.matmul(
                                    psb[p][0:nk, col0:col0 + width], lhsT=ident_b[0:nk, 0:nk], rhs=bt,
                                    start=False, stop=(mi == len(masks) - 1)), reads=["ident_b", bkey], writes=[("ps", p)])
                            PT, PTk = r_PT.next()
                            kb.op("act", lambda e, p=p, nk=nk, c0=c0, c1=c1, PT=PT: e.activation(
                                out=PT[0:nk, c0:c1], in_=psb[p][0:nk, c0:c1], func=AF.Exp, scale=0.125),
                                reads=[("ps", p)], writes=[PTk])
                            pend.append((j, PT, PTk, nk))
                            if len(pend) > 2:
                                emit_pv(pend.pop(0))
                        while pend:
                            emit_pv(pend.pop(0))
                        O = psb[po][:, 0:4 * vn].rearrange("p (a c) -> p a c", c=vn)
                        ri, rik = r_ri.next()
                        kb.op("dve", lambda e: e.tensor_scalar(out=ri[:, 0:4], in0=O[:, :, 64], scalar1=1e-30, scalar2=None, op0=ALU.add),
                              reads=[("ps", po)], writes=[rik])
                        kb.op("dve", lambda e: e.reciprocal(out=ri[:, 4:8], in_=ri[:, 0:4]), reads=[rik], writes=[rik])
                        if br == 0:
                            sct, sck = r_sc.next()
                            kb.op("dve", lambda e: e.tensor_tensor(out=sct[:], in0=O[:, :, 65:97],
                                                                   in1=ri[:, 4:8].unsqueeze(2).broadcast_to([128, 4, 32]), op=ALU.mult),
                                  reads=[("ps", po), rik], writes=[sck])
                            kb.op("pool", lambda e: e.tensor_tensor(out=score[:, 4 * tc:4 * tc + 4, :], in0=score[:, 4 * tc:4 * tc + 4, :],
                                                                    in1=sct[:], op=ALU.add), reads=[sck, "score"], writes=["score"])
                        gcol = br * 16 + g * 8 + hh
                        kb.op("dve", lambda e: e.tensor_tensor(out=ri[:, 0:4], in0=ri[:, 4:8], in1=gates[:, 4 * tc:4 * tc + 4, gcol], op=ALU.mult),
                              reads=[rik, "gates"], writes=[rik])
                        ydst = yb[:, 4 * tc:4 * tc + 4, hh * 64:(hh + 1) * 64]
                        rgb = ri[:, 0:4].unsqueeze(2).broadcast_to([128, 4, 64])
                        if br == 0:
                            kb.op("dve", lambda e: e.tensor_tensor(out=ydst, in0=O[:, :, 0:64], in1=rgb, op=ALU.mult),
                                  reads=[("ps", po), rik], writes=[("yb", hh, tc)])
                        else:
                            ot, otk = r_tm.next()
                            kb.op("dve", lambda e: e.tensor_tensor(out=ot[:], in0=O[:, :, 0:64], in1=rgb, op=ALU.mult),
                                  reads=[("ps", po), rik], writes=[otk])
                            kb.op("pool", lambda e: e.tensor_tensor(out=ydst, in0=ydst, in1=ot[:], op=ALU.add),
                                  reads=[otk, ("yb", hh, tc)], writes=[("yb", hh, tc)])

                    kstop("cmp")
                    for hh in range(8):
                        for tc in range(4):
                            attn_chunk(hh, tc, 64, kcmpT, "kcmpT",
                                       [(0, 127, 0, 512, [(0, 512, cmpb[:, tc * 512:(tc + 1) * 512], "cmpb")])],
                                       lambda j: vcmp[0:127, 0:97], "vcmp", 97, [[0], [0], [0], [0]], 0)
                    kstop("passA")
                    r_scp = Rot(AT, "scp", [128, 32], F32, 2)
                    r_sc2 = Rot(AT, "sc2", [128, 32], F32, 2)
                    r_m8 = Rot(AT, "m8", [128, 16], F32, 2)
                    r_bt = Rot(AT, "biast", [128, 96], BF16, 2)
                    for bt_, btk_ in zip(r_bt.t, r_bt.k):
                        kb.op("dve", lambda e, bt_=bt_: e.memset(bt_[:], 0.0), writes=[btk_])
                    for i in range(8, NT):
                        scp, scpk = r_scp.next()
                        sc2, sc2k = r_sc2.next()
                        m8, m8k = r_m8.next()
                        bt_, btk_ = r_bt.next()
                        kb.op("dve", lambda e: e.scalar_tensor_tensor(out=scp[:], in0=score[:, i, :], scalar=1.0, in1=Msel[:, i, :],
                                                                      op0=ALU.add, op1=ALU.mult), reads=["score", "Msel"], writes=[scpk])
                        kb.op("dve", lambda e: e.max(out=m8[:, 0:8], in_=scp[:]), reads=[scpk], writes=[m8k])
                        kb.op("dve", lambda e: e.match_replace(out=sc2[:], in_to_replace=m8[:, 0:8], in_values=scp[:], imm_value=-1.0),
                              reads=[scpk, m8k], writes=[sc2k])
                        kb.op("dve", lambda e: e.max(out=m8[:, 8:16], in_=sc2[:]), reads=[sc2k], writes=[m8k])
                        kb.op("dve", lambda e: e.tensor_scalar(out=sc2[:], in0=scp[:], scalar1=m8[:, 12:13], scalar2=None, op0=ALU.is_ge),
                              reads=[scpk, m8k], writes=[sc2k])
                        kb.op("dve", lambda e: e.tensor_tensor(out=sc2[:], in0=sc2[:], in1=Fsel[:, i, :], op=ALU.max),
                              reads=[sc2k, "Fsel"], writes=[sc2k])
                        kb.op("dve", lambda e: e.tensor_scalar(out=bt_[:, 64:96], in0=sc2[:], scalar1=NEG, scalar2=-NEG,
                                                               op0=ALU.mult, op1=ALU.add), reads=[sc2k], writes=[btk_])
                        p = ps_next()
                        pv = psb[p][:].bitcast(BF16)
                        kb.op("pe", lambda e, pv=pv: e.transpose(out=pv[0:96, 0:128], in_=bt_[:, 0:96], identity=ident_b[:]),
                              reads=[btk_, "ident_b"], writes=[("ps", p)])
                        for hh in range(8):
                            evac_copy(Qa[hh][64:96, i * 128:(i + 1) * 128], pv[64:96, 0:128], [("ps", p)], [("Qa", hh)])
                    if "score" in dbg and g == 0:
                        kb.dma("sp", dbg_out("score", [128, NT, 32]), score[:], reads=["score"])
                    if "ycmp" in dbg and g == 0:
                        kb.dma("sp", dbg_out("ycmp", [128, NT, 512], BF16), yb[:], reads=[("yb", hh, tc) for hh in range(8) for tc in range(4)])
                    kstop("passB")
                    for hh in range(8):
                        for tc in range(4):
                            ktl = []
                            for j in range(4 * tc + 4):
                                if j < 4 * tc:
                                    ktl.append((j, 128, 0, 512, []))
                                else:
                                    r = j - 4 * tc
                                    ktl.append((j, 128, 128 * r, 512, [(128 * r, 128, cb[:], "cb")]))
                            import os
                            KBR = os.environ.get("KBR", "12")
                            if "1" in KBR:
                                attn_chunk(hh, tc, 96, ksT, "ksT", ktl, lambda j: vs_a[:, j, 0:65], "vs_a", 65,
                                           [list(range(0, 4 * tc + qi + 1)) for qi in range(4)], 1)
                            ktl = []
                            for j in range(max(0, 4 * tc - 4), 4 * tc + 4):
                                jj = j - 4 * tc
                                c0 = 128 * max(jj, 0)
                                c1 = 128 * (min(jj + 4, 3) + 1)
                                masks = []
                                if jj >= 0:
                                    masks.append((128 * jj, 128, cb[:], "cb"))
                                if jj + 4 <= 3:
                                    masks.append((128 * (jj + 4), 128, ab[:], "ab"))
                                ktl.append((j, 128, c0, c1, masks))
                            if "2" in KBR:
                                attn_chunk(hh, tc, 64, kwT, "kwT", ktl, lambda j: vw_a[:, j, 0:65], "vw_a", 65,
                                           [list(range(max(0, 4 * tc + qi - 4), 4 * tc + qi + 1)) for qi in range(4)], 2)
                    kb.barrier()
                    AT.close()
                    FB = contextlib.ExitStack()
                    ybT = sb(FB, "ybT", [128, 4, S], BF16)
                    r_sg = Rot(FB, "sgb", [128, 512], F32, 2)
                    r_tb = Rot(FB, "tb", [128, 512], F32, 2)
                    kstop("passC")
                    for i in range(NT):
                        p = ps_next()
                        pv = psb[p][:].bitcast(BF16)
                        for m in range(4):
                            kb.op("pe", lambda e, pv=pv, m=m, i=i: e.transpose(out=pv[:, m * 128:(m + 1) * 128], in_=yb[:, i, m * 128:(m + 1) * 128],
                                                                              identity=ident_b[:]),
                                  reads=[("yb", hh, i // 4) for hh in range(8)] + ["ident_b"], writes=[("ps", p)])
                        evac_copy(ybT[:, :, i * 128:(i + 1) * 128], pv[:, 0:512].rearrange("p (m t) -> p m t", t=128),
                                  [("ps", p)], ["ybT"])
                    kstop("fbT")
                    if "ybT" in dbg:
                        kb.dma("sp", dbg_out("ybT%d" % g, [128, 4, S], BF16), ybT[:], reads=["ybT"])
                    for blk in range(2):
                        wm, wmk = wbn.next()
                        load_w(wm, wmk, win_d[:, 5952 + blk * 512:5952 + (blk + 1) * 512], 512)
                        wa, wak = wbn.next()
                        kb.dma("pool", wa[:, 0:4, :], wbb_d[g * 512:(g + 1) * 512, blk * 512:(blk + 1) * 512].rearrange("(kc p) n -> p kc n", p=128),
                               writes=[wak])
                        for mt in range(4):
                            for tc in range(4):
                                p1 = ps_next()
                                proj_fm(wm, wmk, mt * 128, 128, h1T, H1K, tc * 512, 512, psb[p1][:, :], ("ps", p1))
                                sg, sgk = r_sg.next()
                                kb.op("act", lambda e, p1=p1, sg=sg: e.activation(out=sg[:], in_=psb[p1][:, :], func=AF.Sigmoid),
                                      reads=[("ps", p1)], writes=[sgk])
                                p2 = ps_next()
                                for kc in range(4):
                                    kb.op("pe", lambda e, kc=kc, p2=p2: e.matmul(psb[p2][:, :], lhsT=wa[:, kc, mt * 128:(mt + 1) * 128],
                                                                                 rhs=ybT[:, kc, tc * 512:(tc + 1) * 512],
                                                                                 start=(kc == 0), stop=(kc == 3)),
                                          reads=[wak, "ybT"], writes=[("ps", p2)])
                                tb, tbk = r_tb.next()
                                kb.op("dve", lambda e, p2=p2, sg=sg, tb=tb: e.tensor_tensor(out=tb[:], in0=psb[p2][:, :], in1=sg[:], op=ALU.mult),
                                      reads=[("ps", p2), sgk], writes=[tbk])
                                ti = blk * 4 + mt
                                kb.op("pool", lambda e, tb=tb, ti=ti, tc=tc: e.tensor_tensor(
                                    out=mT[:, ti, tc * 512:(tc + 1) * 512], in0=mT[:, ti, tc * 512:(tc + 1) * 512], in1=tb[:], op=ALU.add),
                                    reads=[tbk, ("mT", ti)], writes=[("mT", ti)])
                    kb.barrier()
                    FB.close()
                    kstop("g0")
            kb.barrier()
        if "mT" in dbg:
            kb.dma("sp", dbg_out("mT", [128, 8, S], BF16), mT[:], reads=[("mT", i) for i in range(8)])
        wout_d = din("w_out", [D, D])
        wq_d = din("peer_wq", [D, 2048])
        pk_d = din("peer_k", [2, 8, 128, 128])
        pu_d = din("peer_u", [16384, D])
        pvv_d = din("peer_v", [16384, D])
        mix = h1T[:].rearrange("p a (b c) -> p (a b) c", c=1024)
        h2T = mT
        with contextlib.ExitStack() as PS_:
            G1S = contextlib.ExitStack()
            g1_bc = sb(G1S, "g1_bc", [128, D], F32)

            def build_bc(stack, items):
                r_dg = Rot(stack, "diag", [128, 128], F32, 2)
                for dst, src, dk in items:
                    for dc in range(8):
                        dg, dgk = r_dg.next()
                        kb.op("dve", lambda e, dg=dg, src=src, dc=dc: e.tensor_scalar(out=dg[:], in0=ident_f[:], scalar1=src[:, dc:dc + 1],
                                                                                     scalar2=None, op0=ALU.mult),
                              reads=["ident_f", "mod", "gf"], writes=[dgk])
                        p = ps_next()
                        kb.op("pe", lambda e, p=p, dg=dg: e.matmul(psb[p][:, 0:128], lhsT=ones_f[:, 0:128], rhs=dg[:], start=True, stop=True),
                              reads=["ones_f", dgk], writes=[("ps", p)])
                        evac_copy(dst[:, dc * 128:(dc + 1) * 128], psb[p][:, 0:128], [("ps", p)], [dk])

            with contextlib.ExitStack() as MS:
                build_bc(MS, [(g1_bc, mod[:, 16:24], "g1_bc")])
                wout = sb(MS, "wout", [128, 8, D], BF16)
                for hf in range(2):
                    kb.dma("pool", wout[:, :, hf * 512:(hf + 1) * 512],
                           wout_d[:, hf * 512:(hf + 1) * 512].rearrange("(kc p) n -> p kc n", p=128), writes=["wout"])
                for i in range(NT):
                    for dd in range(2):
                        p = ps_next()
                        for kc in range(8):
                            kb.op("pe", lambda e, p=p, kc=kc, i=i, dd=dd: e.matmul(
                                psb[p][:, :], lhsT=mT[:, kc, i * 128:(i + 1) * 128], rhs=wout[:, kc, dd * 512:(dd + 1) * 512],
                                start=(kc == 0), stop=(kc == 7)), reads=["wout"] + [("mT", j) for j in range(8)], writes=[("ps", p)])
                        evac_copy(mix[:, i, dd * 512:(dd + 1) * 512], psb[p][:, :], [("ps", p)], ["mix"])
                kb.barrier()
            if "mix" in dbg:
                kb.dma("sp", dbg_out("mix", [128, NT, D], BF16), mix, reads=["mix"])
            kstop("mix")
            with contextlib.ExitStack() as N2:
                r_x1 = Rot(N2, "x1tmp", [128, D], F32, 2)

                def load_x1(i, dst, key):
                    kb.dma("sp", dst[:], x_d[i * 128:(i + 1) * 128, :], writes=[key])
                    tmp, tk = r_x1.next()
                    kb.op("dve", lambda e: e.tensor_tensor(out=tmp[:], in0=mix[:, i, :], in1=g1_bc[:], op=ALU.mult),
                          reads=["mix", "g1_bc"], writes=[tk])
                    kb.op("pool", lambda e: e.tensor_tensor(out=dst[:], in0=dst[:], in1=tmp[:], op=ALU.add),
                          reads=[key, tk], writes=[key])

                norm_mod_T("n2", load_x1, gs2, 24, h2T, "h2T")
            if "h2T" in dbg:
                kb.dma("sp", dbg_out("h2T", [128, 8, S], BF16), h2T[:], reads=["h2T"])
            kstop("h2T")
            kb.barrier()
            G1S.close()
            with contextlib.ExitStack() as PE_:
                k1T = sb(PE_, "k1T", [128, 8, 128], BF16)
                k2T = sb(PE_, "k2T", [128, 8, 128], BF16)
                with contextlib.ExitStack() as KS:
                    kst = sb(KS, "kst", [128, 2, 8, 128], BF16)
                    kb.dma("pool", kst[:], pk_d.rearrange("w h n d -> n w h d"), writes=["kst"])
                    for wi, dst, dk in ((0, k1T, "k1T"), (1, k2T, "k2T")):
                        p = ps_next()
                        pv = psb[p][:].bitcast(BF16)
                        for h in range(8):
                            kb.op("pe", lambda e, pv=pv, h=h, wi=wi: e.transpose(out=pv[:, h * 128:(h + 1) * 128], in_=kst[:, wi, h, :],
                                                                              identity=ident_b[:]), reads=["kst", "ident_b"], writes=[("ps", p)])
                        evac_copy(dst[:], pv[:, 0:1024].rearrange("p (h n) -> p h n", n=128), [("ps", p)], [dk])
                    kb.barrier()
                peer_acc = sb(PE_, "peer_acc", [128, 8, D], F32)
                qT = sb(PE_, "qT", [128, 16, 1024], BF16)
                thr_all = sb(PE_, "thr_all", [128, 8, 8], F32)
                nb_all = sb(PE_, "nb_all", [128, 8, 8], F32)
                Dk = sb(PE_, "Dk", [128, 8, 8, 128], BF16)
                kap = sb(PE_, "kap", [128, 8, 8], F32)
                for hf in range(2):
                    T0 = hf * 8
                    tok0 = T0 * 128
                    QS = contextlib.ExitStack()
                    wqb = Rot(QS, "wqb", [128, 8, 512], BF16, 2)
                    for blk in range(4):
                        w, wk = wqb.next()
                        load_w(w, wk, wq_d[:, blk * 512:(blk + 1) * 512], 512)
                        for mt in range(4):
                            for tc in range(2):
                                p = ps_next()
                                proj_fm(w, wk, mt * 128, 128, h2T, ["h2T"], tok0 + tc * 512, 512, psb[p][:, :], ("ps", p))
                                evac_copy(qT[:, blk * 4 + mt, tc * 512:(tc + 1) * 512], psb[p][:, :], [("ps", p)], [("qT", blk * 4 + mt)])
                    kb.barrier()
                    QS.close()
                    with contextlib.ExitStack() as PR:
                        v12 = sb(PR, "v12", [128, 2, 8, 16], F32)
                        cand = sb(PR, "cand", [128, 8, 16, 16], F32)
                        ctop = sb(PR, "ctop", [128, 8, 24], F32)
                        r_mr = Rot(PR, "mr", [128, 256], F32, 2)
                        sm = sb(PR, "sm", [128, 8, 16], F32)
                        zz = sb(PR, "zz", [128, 4, 8], F32)
                        for il in range(8):
                            ts = slice(il * 128, (il + 1) * 128)
                            for wi, kT_ in ((0, k1T), (1, k2T)):
                                for h in range(8):
                                    p = ps_next()
                                    kb.op("pe", lambda e, p=p, h=h, wi=wi, kT_=kT_: e.matmul(
                                        psb[p][:, 0:128], lhsT=qT[:, h * 2 + wi, ts], rhs=kT_[:, h, :], start=True, stop=True),
                                        reads=[("qT", h * 2 + wi), "k1T", "k2T"], writes=[("ps", p)])
                                    mr, mrk = r_mr.next()
                                    kb.op("dve", lambda e, p=p, h=h, wi=wi: e.max(out=v12[:, wi, h, 0:8], in_=psb[p][:, 0:128]),
                                          reads=[("ps", p)], writes=["v12"])
                                    kb.op("dve", lambda e, p=p, h=h, wi=wi, mr=mr: e.match_replace(
                                        out=mr[:, 0:128], in_to_replace=v12[:, wi, h, 0:8], in_values=psb[p][:, 0:128], imm_value=-1e30),
                                        reads=[("ps", p), "v12"], writes=[mrk])
                                    kb.op("dve", lambda e, h=h, wi=wi, mr=mr: e.max(out=v12[:, wi, h, 8:16], in_=mr[:, 0:128]),
                                          reads=[mrk], writes=["v12"])
                            kb.op("dve", lambda e: e.tensor_tensor(
                                out=cand[:], in0=v12[:, 0, :, :].unsqueeze(3).broadcast_to([128, 8, 16, 16]),
                                in1=v12[:, 1, :, :].unsqueeze(2).broadcast_to([128, 8, 16, 16]), op=ALU.add),
                                reads=["v12"], writes=["cand"])
                            for h in range(8):
                                ch = cand[:, h, :, :].rearrange("p a b -> p (a b)")
                                mr, mrk = r_mr.next()
                                mr2, mr2k = r_mr.next()
                                kb.op("dve", lambda e, h=h, ch=ch: e.max(out=ctop[:, h, 0:8], in_=ch), reads=["cand"], writes=["ctop"])
                                kb.op("dve", lambda e, h=h, ch=ch, mr=mr: e.match_replace(out=mr[:], in_to_replace=ctop[:, h, 0:8], in_values=ch,
                                                                                        imm_value=-1e30), reads=["cand", "ctop"], writes=[mrk])
                                kb.op("dve", lambda e, h=h, mr=mr: e.max(out=ctop[:, h, 8:16], in_=mr[:]), reads=[mrk], writes=["ctop"])
                                kb.op("dve", lambda e, h=h, mr=mr, mr2=mr2: e.match_replace(out=mr2[:], in_to_replace=ctop[:, h, 8:16], in_values=mr[:],
                                                                                          imm_value=-1e30), reads=[mrk, "ctop"], writes=[mr2k])
                                kb.op("dve", lambda e, h=h, mr2=mr2: e.max(out=ctop[:, h, 16:24], in_=mr2[:]), reads=[mr2k], writes=["ctop"])
                            kb.op("dve", lambda e: e.tensor_tensor(out=zz[:, 0, :], in0=ctop[:, :, 15], in1=ctop[:, :, 16], op=ALU.add),
                                  reads=["ctop"], writes=["zz"])
                            kb.op("dve", lambda e, il=il: e.tensor_scalar(out=thr_all[:, il, :], in0=zz[:, 0, :], scalar1=0.5, scalar2=None, op0=ALU.mult),
                                  reads=["zz"], writes=["thr_all"])
                            kb.op("dve", lambda e: e.tensor_tensor(out=sm[:], in0=ctop[:, :, 0:16],
                                                                   in1=ctop[:, :, 0:1].broadcast_to([128, 8, 16]), op=ALU.subtract),
                                  reads=["ctop"], writes=["sm"])
                            kb.op("act", lambda e: e.activation(out=sm[:], in_=sm[:], func=AF.Exp), reads=["sm"], writes=["sm"])
                            kb.op("dve", lambda e: e.tensor_reduce(out=zz[:, 1, :], in_=sm[:], axis=mybir.AxisListType.X, op=ALU.add),
                                  reads=["sm"], writes=["zz"])
                            kb.op("act", lambda e: e.activation(out=zz[:, 2, :], in_=zz[:, 1, :], func=AF.Ln), reads=["zz"], writes=["zz"])
                            kb.op("dve", lambda e: e.tensor_tensor(out=zz[:, 3, :], in0=zz[:, 2, :], in1=ctop[:, :, 0], op=ALU.add),
                                  reads=["zz", "ctop"], writes=["zz"])
                            kb.op("dve", lambda e, il=il: e.tensor_scalar(out=nb_all[:, il, :], in0=zz[:, 3, :], scalar1=-1.0, scalar2=None, op0=ALU.mult),
                                  reads=["zz"], writes=["nb_all"])
                            kb.op("dve", lambda e, il=il: e.tensor_tensor(out=kap[:, il, :], in0=thr_all[:, il, :], in1=nb_all[:, il, :], op=ALU.add),
                                  reads=["thr_all", "nb_all"], writes=["kap"])
                            kb.op("act", lambda e, il=il: e.activation(out=kap[:, il, :], in_=kap[:, il, :], func=AF.Exp), reads=["kap"], writes=["kap"])
                            for h in range(8):
                                kb.op("dve", lambda e, il=il, h=h: e.tensor_scalar(out=Dk[:, il, h, :], in0=ident_b[:], scalar1=kap[:, il, h:h + 1],
                                                                                   scalar2=None, op0=ALU.mult),
                                      reads=["ident_b", "kap"], writes=["Dk"])
                        kb.barrier()
                    EL = contextlib.ExitStack()
                    uT = Rot(EL, "uT", [128, 8, 512], BF16, 1)
                    ust = Rot(EL, "ust", [128, D], BF16, 2)
                    vbk = Rot(EL, "vbk", [128, 4, D], BF16, 1)
                    r_s1 = Rot(EL, "s1sb", [128, 8, 4], F32, 2)
                    r_aT4 = Rot(EL, "aT4", [128, 4, 512], BF16, 2)
                    sh_rr = [0]

                    def sh_next():
                        i = sh_rr[0]
                        sh_rr[0] = (i + 1) % 2
                        return 4 + i
                    r_sum = Rot(EL, "sum", [128, 4, 4, 128], BF16, 2)
                    r_E = Rot(EL, "E", [128, 4, 4, 128], BF16, 2)
                    r_G = Rot(EL, "G", [128, 4, 4, 128], BF16, 2)
                    r_wT = Rot(EL, "wT", [128, 4, 128], BF16, 2)
                    kb.op("pool", lambda e: e.memset(peer_acc[:], 0.0), writes=["peer_acc"])
                    import os
                    NBLK = int(os.environ.get("KNBLK", "32"))
                    ut, utk = uT.next()

                    def emit_u_dma(b, c):
                        e0 = b * 512
                        us, usk = ust.next()
                        kb.dma("pool", us[:], pu_d[e0 + c * 128:e0 + (c + 1) * 128, :], writes=[usk])
                        return us, usk

                    def emit_u_tr(c, us, usk):
                        p = sh_next()
                        pv = psb[p][:].bitcast(BF16)
                        for kc in range(8):
                            kb.op("pe", lambda e, pv=pv, kc=kc: e.transpose(out=pv[:, kc * 128:(kc + 1) * 128],
                                                                          in_=us[:, kc * 128:(kc + 1) * 128], identity=ident_b[:]),
                                  reads=[usk, "ident_b"], writes=[("ps", p)])
                        kb.op("act", lambda e: e.copy(out=ut[:, :, c * 128:(c + 1) * 128], in_=pv[:, 0:1024].rearrange("p (k e) -> p k e", e=128)),
                              reads=[("ps", p)], writes=[(utk, c)])

                    def emit_v(b):
                        e0 = b * 512
                        vb, vbk_ = vbk.next()
                        kb.dma("pool", vb[:], pvv_d[e0:e0 + 512, :].rearrange("(c p) d -> p c d", p=128), writes=[vbk_])
                        return vb, vbk_

                    def emit_aT4_chunk_pe(aT, g4, c, bank):
                        aT4, aT4k = aT
                        for kc in range(8):
                            kb.op("pe", lambda e, kc=kc: e.matmul(
                                psb[bank][:, :], lhsT=ut[:, kc, c * 128:(c + 1) * 128],
                                rhs=h2T[:, kc, tok0 + g4 * 512:tok0 + (g4 + 1) * 512], start=(kc == 0), stop=(kc == 7)),
                                reads=[(utk, c), "h2T"], writes=[("ps", bank)])

                        def gelu():
                            kb.op("act", lambda e: e.activation(out=aT4[:, c, :], in_=psb[bank][:, :], func=AF.Gelu),
                                  reads=[("ps", bank)], writes=[(aT4k, c)])
                        return gelu

                    class Job:
                        pass

                    def make_job(b, il, aT4, aT4k, vb, vbk_):
                        J = Job()
                        ts = slice(il * 128, (il + 1) * 128)
                        q = il % 4
                        st = {}

                        def A_pe():
                            p1 = sh_next()
                            st["p1"] = p1
                            for h in range(8):
                                kb.op("pe", lambda e, h=h: e.matmul(psb[p1][:, h * 4:(h + 1) * 4], lhsT=qT[:, h * 2, ts],
                                                                    rhs=k1T[:, h, 4 * b:4 * b + 4], start=True, stop=True),
                                      reads=[("qT", h * 2), "k1T"], writes=[("ps", p1)])
                            p2s = [(il % 2) * 2, (il % 2) * 2 + 1]
                            st["p2s"] = p2s
                            for hg in range(2):
                                for hh in range(4):
                                    h = hg * 4 + hh
                                    kb.op("pe", lambda e, h=h, hh=hh, hg=hg: e.matmul(psb[p2s[hg]][:, hh * 128:(hh + 1) * 128], lhsT=qT[:, h * 2 + 1, ts],
                                                                                    rhs=k2T[:, h, :], start=True, stop=True),
                                          reads=[("qT", h * 2 + 1), "k2T"], writes=[("ps", p2s[hg])])

                        def A_dve():
                            p1 = st["p1"]
                            s1, s1k = r_s1.next()
                            st["s1"] = (s1, s1k)
                            kb.op("dve", lambda e: e.tensor_tensor(
                                out=s1[:], in0=psb[p1][:, 0:32].rearrange("p (h i) -> p h i", i=4),
                                in1=thr_all[:, il, :].unsqueeze(2).broadcast_to([128, 8, 4]), op=ALU.subtract),
                                reads=[("ps", p1), "thr_all"], writes=[s1k])

                        def B1_sums():
                            s1, s1k = st["s1"]
                            sms = []
                            for hg in range(2):
                                p2 = st["p2s"][hg]
                                sm_, smk = r_sum.next()
                                kb.op("dve", lambda e: e.tensor_tensor(
                                    out=sm_[:], in0=s1[:, hg * 4:(hg + 1) * 4, :].unsqueeze(3).broadcast_to([128, 4, 4, 128]),
                                    in1=psb[p2][:, :].rearrange("p (h j) -> p h j", j=128).unsqueeze(2).broadcast_to([128, 4, 4, 128]), op=ALU.add),
                                    reads=[s1k, ("ps", p2)], writes=[smk])
                                sms.append((sm_, smk))
                            Es = []
                            for hg in range(2):
                                sm_, smk = sms[hg]
                                E_, Ek = r_E.next()
                                kb.op("act", lambda e: e.activation(out=E_[:], in_=sm_[:], func=AF.Exp), reads=[smk], writes=[Ek])
                                Es.append((E_, Ek))
                            st["sms"] = sms
                            st["Es"] = Es

                        def B1_stt():
                            Gs = []
                            for hg in range(2):
                                sm_, smk = st["sms"][hg]
                                E_, Ek = st["Es"][hg]
                                G_, Gk = r_G.next()
                                kb.op("dve", lambda e: e.scalar_tensor_tensor(out=G_[:].rearrange("p a b c -> p (a b) c"),
                                                                              in0=sm_[:].rearrange("p a b c -> p (a b) c"), scalar=0.0,
                                                                              in1=E_[:].rearrange("p a b c -> p (a b) c"),
                                                                              op0=ALU.is_gt, op1=ALU.mult),
                                      reads=[smk, Ek], writes=[Gk])
                                Gs.append((G_, Gk))
                            st["Gs"] = Gs

                        def B1_gt():
                            pg = ps_next_acc()
                            st["pg"] = pg
                            for hg in range(2):
                                G_, Gk = st["Gs"][hg]
                                for hh in range(4):
                                    h = hg * 4 + hh
                                    for c in range(4):
                                        kb.op("pe", lambda e, c=c, h=h, hh=hh: e.matmul(psb[pg][:, c * 128:(c + 1) * 128], lhsT=G_[:, hh, c, :],
                                                                                      rhs=Dk[:, il, h, :], start=(h == 0 and c == 0),
                                                                                      stop=True, skip_group_check=True),
                                              reads=[Gk, "Dk"], writes=[("ps", pg)])

                        def B2_wT():
                            pg = st["pg"]
                            wT, wTk = r_wT.next()
                            st["wT"] = (wT, wTk)
                            kb.op("dve", lambda e: e.tensor_tensor(out=wT[:], in0=aT4[:, :, q * 128:(q + 1) * 128],
                                                                   in1=psb[pg][:, :].rearrange("p (c t) -> p c t", t=128),
                                                                   op=ALU.mult), reads=[(aT4k, cc) for cc in range(4)] + [("ps", pg)], writes=[wTk])

                        def B2_peer_acc():
                            wT, wTk = st["wT"]
                            pps = []
                            for dd in range(2):
                                pp = sh_next()
                                pps.append(pp)
                                for c in range(4):
                                    kb.op("pe", lambda e, c=c, dd=dd, pp=pp: e.matmul(psb[pp][:, :], lhsT=wT[:, c, :], rhs=vb[:, c, dd * 512:(dd + 1) * 512],
                                                                                    start=(c == 0), stop=(c == 3)),
                                          reads=[wTk, vbk_], writes=[("ps", pp)])
                            for dd in range(2):
                                pp = pps[dd]
                                kb.op("dve", lambda e, dd=dd, pp=pp: e.tensor_tensor(out=peer_acc[:, il, dd * 512:(dd + 1) * 512],
                                                                                     in0=peer_acc[:, il, dd * 512:(dd + 1) * 512], in1=psb[pp][:, :],
                                                                                     op=ALU.add), reads=[("pacc", il), ("ps", pp), "peer_acc"],
                                      writes=[("pacc", il)])

                        J.A_pe, J.A_dve, J.B1_sums, J.B1_stt, J.B1_gt, J.B2_wT, J.B2_peer_acc = A_pe, A_dve, B1_sums, B1_stt, B1_gt, B2_wT, B2_peer_acc
                        return J

                    for c in range(4):
                        us, usk = emit_u_dma(0, c)
                        emit_u_tr(c, us, usk)
                    vb, vbk_ = emit_v(0)
                    aT_cur = [r_aT4.next(), None]
                    for c in range(4):
                        emit_aT4_chunk_pe(aT_cur[0], 0, c, sh_next())()
                    aT_next = None
                    jobs = {}
                    NSTEP = NBLK * 8
                    u_pend = {}
                    pend_gelu = []
                    for n in range(NSTEP + 3):
                        b, q8 = divmod(n, 8)
                        if n < NSTEP:
                            if q8 == 2:
                                aT_cur[1] = r_aT4.next()
                            if q8 == 0 and b > 0:
                                aT_cur[0] = aT_next
                            aT4, aT4k = aT_cur[q8 // 4]
                            jobs[n] = make_job(b, q8, aT4, aT4k, vb, vbk_)
                        if (n - 1) in jobs:
                            jobs[n - 1].B1_sums()
                        while pend_gelu:
                            pend_gelu.pop(0)()
                        if (n - 3) in jobs:
                            jobs[n - 3].B2_peer_acc()
                            del jobs[n - 3]
                        if n < NSTEP and q8 == 2 and b > 0:
                            vb, vbk_ = emit_v(b)
                        if n < NSTEP:
                            jobs[n].A_pe()
                            jobs[n].A_dve()
                        if (n - 2) in jobs:
                            jobs[n - 2].B2_wT()
                        if n < NSTEP:
                            bank = 6 + (1 - ps_acc[0]) if (n - 1) in jobs else 6 + ps_acc[0]
                            if 2 <= q8 <= 5:
                                pend_gelu.append(emit_aT4_chunk_pe(aT_cur[1], 1, q8 - 2, bank))
                            elif b + 1 < NBLK and q8 >= 6:
                                if q8 == 6:
                                    aT_next = r_aT4.next()
                                pend_gelu.append(emit_aT4_chunk_pe(aT_next, 0, q8 - 6, bank))
                            elif b > 0 and q8 <= 1:
                                pend_gelu.append(emit_aT4_chunk_pe(aT_cur[0], 0, q8 + 2, bank))
                            if b + 1 < NBLK:
                                if 3 <= q8 <= 6:
                                    emit_u_tr(q8 - 3, *u_pend[q8 - 3])
                                if 1 <= q8 <= 4:
                                    u_pend[q8 - 1] = emit_u_dma(b + 1, q8 - 1)
                        if (n - 1) in jobs:
                            jobs[n - 1].B1_stt()
                            jobs[n - 1].B1_gt()
                    while pend_gelu:
                        pend_gelu.pop(0)()
                    if "peer" in dbg:
                        kb.dma("sp", dbg_out("peer%d" % hf, [128, 8, D]), peer_acc[:], reads=[("pacc", il) for il in range(8)] + ["peer_acc"])
                    kb.barrier()
                    EL.close()
                    with contextlib.ExitStack() as EP:
                        g1_bc = sb(EP, "g1_bc", [128, D], F32)
                        g2_bc = sb(EP, "g2_bc", [128, D], F32)
                        gf_bc = sb(EP, "gf_bc", [128, D], F32)
                        build_bc(EP, [(g1_bc, mod[:, 16:24], "g1_bc"), (g2_bc, mod[:, 40:48], "g2_bc"), (gf_bc, gf[:, 0:8], "gf_bc")])
                        r_x2 = Rot(EP, "x2", [128, D], F32, 2)
                        r_t3 = Rot(EP, "x2u", [128, D], F32, 2)
                        r_t2 = Rot(EP, "x2t", [128, D], F32, 2)
                        r_jk = Rot(EP, "jk", [128, D], BF16, 1)
                        r_st = Rot(EP, "st", [128, 4], F32, 2)
                        for il in range(8):
                            i = T0 + il
                            x2, x2k = r_x2.next()
                            t2, t2k = r_t2.next()
                            st, stk = r_st.next()
                            jk, jkk = r_jk.next()
                            kb.dma("sp", x2[:], x_d[i * 128:(i + 1) * 128, :], writes=[x2k])
                            kb.op("dve", lambda e: e.tensor_tensor(out=t2[:], in0=mix[:, i, :], in1=g1_bc[:], op=ALU.mult),
                                  reads=["mix", "g1_bc"], writes=[t2k])
                            t3, t3k = r_t3.next()
                            kb.op("pool", lambda e: e.tensor_tensor(out=t3[:], in0=peer_acc[:, il, :], in1=g2_bc[:], op=ALU.mult),
                                  reads=[("pacc", il), "peer_acc", "g2_bc"], writes=[t3k])
                            kb.op("pool", lambda e: e.tensor_tensor(out=t2[:], in0=t2[:], in1=t3[:], op=ALU.add),
                                  reads=[t2k, t3k], writes=[t2k])
                            kb.op("pool", lambda e: e.tensor_tensor(out=x2[:], in0=x2[:], in1=t2[:], op=ALU.add), reads=[x2k, t2k], writes=[x2k])
                            kb.op("act", lambda e: e.activation(out=jk[:], in_=x2[:], func=AF.Square, accum_out=st[:, 0:1]),
                                  reads=[x2k], writes=[stk, jkk])
                            kb.op("dve", lambda e: e.tensor_scalar(out=st[:, 1:2], in0=st[:, 0:1], scalar1=1.0 / D, scalar2=EPS,
                                                                   op0=ALU.mult, op1=ALU.add), reads=[stk], writes=[stk])
                            kb.op("act", lambda e: e.activation(out=st[:, 2:3], in_=st[:, 1:2], func=AF.Sqrt), reads=[stk], writes=[stk])
                            kb.op("dve", lambda e: e.reciprocal(out=st[:, 3:4], in_=st[:, 2:3]), reads=[stk], writes=[stk])
                            kb.op("dve", lambda e: e.scalar_tensor_tensor(out=t2[:], in0=x2[:], scalar=st[:, 3:4], in1=gf_bc[:],
                                                                          op0=ALU.mult, op1=ALU.mult), reads=[x2k, stk, "gf_bc"], writes=[t2k])
                            OUT_TOKS.append(kb.dma("sp", out_d[i * 128:(i + 1) * 128, :], t2[:], reads=[t2k]))
                        kb.barrier()
                kb.barrier()

    try:
        stages()
    except StopBuild:
        pass
    toks = []
    if "mod" in dbg:
        toks.append(kb.dma("sp", dbg_out("mod", [128, 48]), mod[:], reads=["mod"]))
    if "h1T" in dbg:
        o = dbg_out("h1T", [128, 8, S], BF16)
        toks.append(kb.dma("sp", o, h1T[:], reads=["h1T"]))
    kb.barrier()
    for e in ("sp",):
        for q in kb.dsem:
            for i in range(kb.nslot):
                if kb.dval[q][i] > 0:
                    kb._wait(e, (kb.dsem[q][i], kb.dval[q][i], "dma"))
    return kb


_CONST_CACHE = {}


def nsa_consts(inputs, b):
    f = np.float32
    m = {}
    m["pos"] = np.ascontiguousarray(np.asarray(inputs["positions"][b], dtype=np.int32).reshape(1, S))
    w_in = np.asarray(inputs["w_in"][0], dtype=f)
    rot = (np.arange(16) + 8) % 16
    m["w_qrot"] = np.ascontiguousarray(np.concatenate([w_in[:, 3088 + hd * 64 + rot] for hd in range(16)], axis=1))
    cols = []
    for base in (4112, 4368, 4624):
        for g in range(2):
            cols.append(w_in[:, base + g * 64 + rot])
    m["w_krot"] = np.ascontiguousarray(np.concatenate(cols, axis=1))
    w1 = np.stack([np.asarray(inputs["nsa_ck_w1"][0], dtype=f), np.asarray(inputs["nsa_cv_w1"][0], dtype=f)])
    m["cmp_w1"] = np.ascontiguousarray(w1.reshape(2, 32, 64, 64).transpose(0, 2, 1, 3).reshape(2, 64, 2048))
    m["cmp_w2"] = np.ascontiguousarray(np.stack([np.asarray(inputs["nsa_ck_w2"][0], dtype=f), np.asarray(inputs["nsa_cv_w2"][0], dtype=f)]))
    pe = np.stack([np.asarray(inputs["nsa_pe_k"][0], dtype=f), np.asarray(inputs["nsa_pe_v"][0], dtype=f)])
    m["cmp_peT"] = np.ascontiguousarray(pe.transpose(0, 2, 1))
    m["w_branch_b"] = np.ascontiguousarray(inputs["w_branch_b"][0], dtype=f)
    if "c" not in _CONST_CACHE:
        c = {}
        inv = (500000.0 ** (-np.arange(8, dtype=np.float32) * 2.0 / 16)).astype(f)
        rc = np.zeros((16, 4), dtype=f)
        rc[:, 0] = np.concatenate([inv, inv])
        rc[:, 1] = np.pi / 2
        rc[0:8, 2] = np.pi
        rc[8:16, 2] = 0.0
        c["rope_c"] = rc
        n = np.arange(127)[:, None]
        j = np.arange(32)[None, :]
        c["ovl"] = ((n * 16 < (j + 1) * 64) & (n * 16 + 32 > j * 64)).astype(f)
        t = np.arange(S)[None, :]
        c["cmpbias"] = np.where(n * 16 + 31 <= t, 0.0, -30000.0).astype(f)
        k = np.arange(S)[None, :]
        c["blockind"] = (k // 64 == np.arange(32)[:, None]).astype(f)
        kk = np.arange(128)[:, None]
        tt = np.arange(128)[None, :]
        c["causalbias"] = np.where(kk > tt, -30000.0, 0.0).astype(f)
        c["antibias"] = np.where(kk <= tt, -30000.0, 0.0).astype(f)
        tok = (np.arange(NT)[None, :, None] * 128 + np.arange(128)[:, None, None])
        cur = tok // 64
        jj = np.arange(32)[None, None, :]
        c["Msel"] = ((jj >= 1) & (jj <= cur - 2)).astype(f)
        c["Fsel"] = ((jj == 0) | (jj == cur) | (jj == cur - 1)).astype(f)
        _CONST_CACHE["c"] = c
    m.update(_CONST_CACHE["c"])
    return m


def host_layout(inputs, b):
    f = np.float32
    m = {}
    m["x"] = np.ascontiguousarray(inputs["x"][b], dtype=f)
    m["c_l"] = np.ascontiguousarray(np.asarray(inputs["c"][b], dtype=f).reshape(8, 128).T)
    m["ada_w"] = np.ascontiguousarray(inputs["ada_w"][0], dtype=f)
    m["ada_b_l"] = np.ascontiguousarray(np.asarray(inputs["ada_b"][0], dtype=f).reshape(48, 128).T)
    m["g1_l"] = np.ascontiguousarray(np.asarray(inputs["norm1_g"][0], dtype=f).reshape(8, 128).T)
    m["g2_l"] = np.ascontiguousarray(np.asarray(inputs["norm2_g"][0], dtype=f).reshape(8, 128).T)
    m["gf_l"] = np.ascontiguousarray(np.asarray(inputs["final_g"], dtype=f).reshape(8, 128).T)
    m["ident"] = np.eye(128, dtype=f)
    m["LT"] = np.triu(np.ones((128, 128), dtype=f))
    m["SU"] = np.tril(np.ones((128, 128), dtype=f), -1)
    m["w_in"] = np.ascontiguousarray(inputs["w_in"][0], dtype=f)
    m["wa2"] = np.ascontiguousarray(inputs["gla_wa2"][0], dtype=f)
    m["ba2"] = np.ascontiguousarray(np.asarray(inputs["gla_ba2"][0], dtype=f).reshape(1, 512))
    m["gng_l"] = np.ascontiguousarray(np.asarray(inputs["gla_norm_g"][0], dtype=f).reshape(2, 128).T)
    m.update(nsa_consts(inputs, b))
    m["w_out"] = np.ascontiguousarray(inputs["w_out"][0], dtype=f)
    m["peer_wq"] = np.ascontiguousarray(inputs["peer_wq"][0], dtype=f)
    m["peer_k"] = np.ascontiguousarray(np.stack([np.asarray(inputs["peer_k1"][0], dtype=f), np.asarray(inputs["peer_k2"][0], dtype=f)]))
    m["peer_u"] = np.ascontiguousarray(inputs["peer_u"][0], dtype=f)
    m["peer_v"] = np.ascontiguousarray(inputs["peer_v"][0], dtype=f)
    m["w_branch_a"] = np.ascontiguousarray(inputs["w_branch_a"][0], dtype=f)
    return m


def run(inputs, dbg=(), ncores=8):
    kb = build(dbg)
    in_maps = [host_layout(inputs, b) for b in range(ncores)]
    res = run_bass_kernel_spmd(kb.nc, in_maps, core_ids=list(range(ncores)))
    return res.results


def kernel(**inputs):
    res = run(inputs)
    return np.stack([np.asarray(r["out"], dtype=np.float32) for r in res], axis=0)
```

```python
import contextlib
import numpy as np
import concourse.bass as bass
import concourse.mybir as mybir
from concourse.alu_op_type import AluOpType as ALU
from concourse.bass_utils import run_bass_kernel_spmd

AF = mybir.ActivationFunctionType
F32 = mybir.dt.float32
BF16 = mybir.dt.bfloat16
I32 = mybir.dt.int32
U32 = mybir.dt.uint32

S = 2048
D = 1024
NT = S // 128
EPS = 1e-6


class StopBuild(Exception):
    pass


def kstop(name):
    import os
    if os.environ.get('KSTOP', '') == name:
        raise StopBuild()


class KB:
    def __init__(self):
        self.nc = bass.Bass("TRN2", target_bir_lowering=False)
        nc = self.nc
        self.es = contextlib.ExitStack()
        self.eng = dict(pe=nc.tensor, act=nc.scalar, dve=nc.vector, pool=nc.gpsimd, sp=nc.sync)
        self.sem = {e: self.es.enter_context(nc.semaphore("s_" + e)) for e in self.eng}
        self.cnt = {e: 0 for e in self.eng}
        self.nslot = 10
        self.dsem = {q: [self.es.enter_context(nc.semaphore("d_%s%d" % (q, i))) for i in range(self.nslot)]
                     for q in ("sp", "pool", "act")}
        self.dval = {q: [0] * self.nslot for q in self.dsem}
        self.dnext = {q: 0 for q in self.dsem}
        self.waited = {}
        self.lastw = {}
        self.readers = {}
        self.ninst = 0

    def _wait(self, e, tok):
        sem, val, prod = tok
        if prod == "pe" and e == "pe":
            return
        key = (e, id(sem))
        if self.waited.get(key, 0) >= val:
            return
        self.eng[e].wait_ge(sem, val)
        self.waited[key] = val

    def _deps(self, e, reads, writes):
        for k in list(reads) + list(writes):
            t = self.lastw.get(k)
            if t is not None:
                self._wait(e, t)
        for k in reads:
            if isinstance(k, tuple) and k[0] == "ps":
                for t in self.readers.get(k, {}).values():
                    if t[2] != e:
                        self._wait(e, t)
        for k in writes:
            for t in self.readers.get(k, {}).values():
                self._wait(e, t)

    def _record(self, tok, reads, writes):
        for k in writes:
            self.lastw[k] = tok
            self.readers[k] = {}
        for k in reads:
            self.readers.setdefault(k, {})[id(tok[0])] = tok

    def op(self, e, fn, reads=(), writes=()):
        self._deps(e, reads, writes)
        inst = fn(self.eng[e])
        self.cnt[e] += 1
        inst.then_inc(self.sem[e], 1)
        tok = (self.sem[e], self.cnt[e], e)
        self._record(tok, reads, writes)
        self.ninst += 1
        return inst

    def dma(self, q, out, in_, reads=(), writes=(), **kw):
        slot = self.dnext[q]
        self.dnext[q] = (slot + 1) % self.nslot
        sem = self.dsem[q][slot]
        if self.dval[q][slot] > 0:
            self._wait(q, (sem, self.dval[q][slot], "dma"))
        self._deps(q, reads, writes)
        inst = self.eng[q].dma_start(out=out, in_=in_, **kw)
        self.dval[q][slot] += 16
        inst.then_inc(sem, 16)
        tok = (sem, self.dval[q][slot], "dma")
        self._record(tok, reads, writes)
        self.ninst += 1
        return tok

    def barrier(self):
        toks = [(self.sem[e], self.cnt[e], e) for e in self.eng if self.cnt[e] > 0]
        for q in self.dsem:
            for i in range(self.nslot):
                if self.dval[q][i] > 0:
                    toks.append((self.dsem[q][i], self.dval[q][i], "dma"))
        for e in self.eng:
            for t in toks:
                self._wait(e, t)
        self.lastw.clear()
        self.readers.clear()

    def final_wait(self, toks):
        for t in toks:
            self._wait("sp", t)


def build(dbg=()):
    kb = KB()
    nc = kb.nc
    es = kb.es
    dbg = set(dbg)

    def din(name, shape, dt=F32):
        return nc.dram_tensor(name, list(shape), dt, kind="ExternalInput").ap()

    def dout(name, shape, dt=F32):
        return nc.dram_tensor(name, list(shape), dt, kind="ExternalOutput").ap()

    uid = [0]

    def sb(stack, name, shape, dt):
        uid[0] += 1
        return stack.enter_context(nc.sbuf_tensor("sb%d_%s" % (uid[0], name), list(shape), dt))

    x_d = din("x", [S, D])
    c_d = din("c_l", [128, 8])
    adaw_d = din("ada_w", [D, 6 * D])
    adab_d = din("ada_b_l", [128, 48])
    g1_d = din("g1_l", [128, 8])
    g2_d = din("g2_l", [128, 8])
    gf_d = din("gf_l", [128, 8])
    ident_d = din("ident", [128, 128])
    out_d = dout("out", [S, D])
    dbg_t = {}
    OUT_TOKS = []

    def dbg_out(name, shape, dt=F32):
        dbg_t[name] = dout("dbg_" + name, shape, dt)
        return dbg_t[name]

    psb = [es.enter_context(nc.psum_tensor("ps%d" % i, [128, 512], F32)) for i in range(8)]
    ps_rr = [0]

    def ps_next():
        i = ps_rr[0]
        ps_rr[0] = (i + 1) % 6
        return i

    ps_acc = [0]

    def ps_next_acc():
        i = ps_acc[0]
        ps_acc[0] = (i + 1) % 2
        return 6 + i

    G = es
    ident_f = sb(G, "ident_f", [128, 128], F32)
    ident_b = sb(G, "ident_b", [128, 128], BF16)
    c_sb = sb(G, "c_sb", [128, 8], F32)
    sc_sb = sb(G, "sc_sb", [128, 8], F32)
    adab_sb = sb(G, "adab_sb", [128, 48], F32)
    mod = sb(G, "mod", [128, 48], F32)
    g1 = sb(G, "g1", [128, 8], F32)
    g2 = sb(G, "g2", [128, 8], F32)
    gf = sb(G, "gf", [128, 8], F32)
    gs1 = sb(G, "gs1", [128, 8], F32)
    gs2 = sb(G, "gs2", [128, 8], F32)
    h1T = sb(G, "h1T", [128, 8, S], BF16)

    kb.dma("sp", ident_f[:], ident_d, writes=["ident_f"])
    kb.dma("sp", c_sb[:], c_d, writes=["c"])
    kb.dma("sp", adab_sb[:], adab_d, writes=["adab"])
    kb.dma("sp", g1[:], g1_d, writes=["g1"])
    kb.dma("sp", g2[:], g2_d, writes=["g2"])
    kb.dma("sp", gf[:], gf_d, writes=["gf"])
    kb.op("dve", lambda e: e.tensor_copy(out=ident_b[:], in_=ident_f[:]), reads=["ident_f"], writes=["ident_b"])

    kb.op("act", lambda e: e.activation(out=sc_sb[:], in_=c_sb[:], func=AF.Silu), reads=["c"], writes=["sc"])
    with contextlib.ExitStack() as P1:
        wbuf = [sb(P1, "adaw%d" % i, [128, 8, 512], F32) for i in range(2)]
        pm = ps_next()
        adaw_v = adaw_d.rearrange("(kc p) n -> p kc n", p=128)
        for blk in range(12):
            wb = wbuf[blk % 2]
            wk = "adaw%d" % (blk % 2)
            kb.dma("sp", wb[:], adaw_v[:, :, blk * 512:(blk + 1) * 512], writes=[wk])
            for mm in range(4):
                m = blk * 4 + mm
                for kc in range(8):
                    kb.op("pe", lambda e, m=m, mm=mm, kc=kc, wb=wb: e.matmul(
                        psb[pm][:, m:m + 1], lhsT=wb[:, kc, mm * 128:(mm + 1) * 128], rhs=sc_sb[:, kc:kc + 1],
                        start=(kc == 0), stop=(kc == 7)), reads=[wk, "sc"], writes=[("ps", pm)])
        kb.op("dve", lambda e: e.tensor_tensor(out=mod[:], in0=psb[pm][:, 0:48], in1=adab_sb[:], op=ALU.add),
              reads=[("ps", pm), "adab"], writes=["mod"])
        kb.op("dve", lambda e: e.scalar_tensor_tensor(out=gs1[:], in0=mod[:, 8:16], scalar=1.0, in1=g1[:],
                                                      op0=ALU.add, op1=ALU.mult), reads=["mod", "g1"], writes=["gs1"])
        kb.op("dve", lambda e: e.scalar_tensor_tensor(out=gs2[:], in0=mod[:, 32:40], scalar=1.0, in1=g2[:],
                                                      op0=ALU.add, op1=ALU.mult), reads=["mod", "g2"], writes=["gs2"])
        kb.barrier()

    def norm_mod_T(stack_name, src_tile_loader, gs, shift_col0, dstT, dst_key):
        with contextlib.ExitStack() as P2:
            xt = [sb(P2, stack_name + "x%d" % i, [128, D], F32) for i in range(2)]
            xn = [sb(P2, stack_name + "xn%d" % i, [128, D], BF16) for i in range(2)]
            junk = sb(P2, stack_name + "junk", [128, D], BF16)
            ss = [sb(P2, stack_name + "ss%d" % i, [128, 4], F32) for i in range(2)]
            for i in range(NT):
                b = i % 2
                kx, kn, ks = stack_name + "x%d" % b, stack_name + "xn%d" % b, stack_name + "ss%d" % b
                src_tile_loader(i, xt[b], kx)
                kb.op("act", lambda e, b=b: e.activation(out=junk[:], in_=xt[b][:], func=AF.Square,
                                                         accum_out=ss[b][:, 0:1]), reads=[kx], writes=[ks, stack_name + "junk"])
                kb.op("dve", lambda e, b=b: e.tensor_scalar(out=ss[b][:, 1:2], in0=ss[b][:, 0:1], scalar1=1.0 / D,
                                                            scalar2=EPS, op0=ALU.mult, op1=ALU.add), reads=[ks], writes=[ks])
                kb.op("act", lambda e, b=b: e.activation(out=ss[b][:, 2:3], in_=ss[b][:, 1:2], func=AF.Sqrt),
                      reads=[ks], writes=[ks])
                kb.op("dve", lambda e, b=b: e.reciprocal(out=ss[b][:, 3:4], in_=ss[b][:, 2:3]), reads=[ks], writes=[ks])
                kb.op("act", lambda e, b=b: e.activation(out=xn[b][:], in_=xt[b][:], func=AF.Identity,
                                                         scale=ss[b][:, 3:4]), reads=[kx, ks], writes=[kn])
                p = ps_next()
                pv = psb[p][:].bitcast(BF16)
                for dc in range(8):
                    kb.op("pe", lambda e, b=b, dc=dc, pv=pv: e.transpose(
                        out=pv[:, dc * 128:(dc + 1) * 128], in_=xn[b][:, dc * 128:(dc + 1) * 128], identity=ident_b[:]),
                        reads=[kn, "ident_b"], writes=[("ps", p)])
                for dc in range(8):
                    eng = "dve" if dc % 2 == 0 else "pool"
                    eng = "dve"
                    kb.op(eng, lambda e, dc=dc, pv=pv, i=i: e.tensor_scalar(
                        out=dstT[:, dc, i * 128:(i + 1) * 128], in0=pv[:, dc * 128:(dc + 1) * 128],
                        scalar1=gs[:, dc:dc + 1], scalar2=mod[:, shift_col0 + dc:shift_col0 + dc + 1],
                        op0=ALU.mult, op1=ALU.add), reads=[("ps", p), "gs1", "gs2", "mod"], writes=[dst_key])
            kb.barrier()

    def load_x(i, dst, key):
        kb.dma("sp", dst[:], x_d[i * 128:(i + 1) * 128, :], writes=[key])

    norm_mod_T("n1", load_x, gs1, 0, h1T, "h1T")

    def stages():
        LT_d = din("LT", [128, 128])
        SU_d = din("SU", [128, 128])
        win_d = din("w_in", [D, 6976])
        wa2_d = din("wa2", [16, 512])
        ba2_d = din("ba2", [1, 512])
        gng_d = din("gng_l", [128, 2])
        wba_d = din("w_branch_a", [D, D])
        LT = sb(G, "LT", [128, 128], F32)
        SU = sb(G, "SU", [128, 128], F32)
        ones_bf = sb(G, "ones_bf", [128, 128], BF16)
        ones_f = sb(G, "ones_f", [128, 128], F32)
        kb.dma("sp", LT[:], LT_d, writes=["LT"])
        kb.dma("sp", SU[:], SU_d, writes=["SU"])
        kb.op("dve", lambda e: e.memset(ones_bf[:], 1.0), writes=["ones_bf"])
        kb.op("dve", lambda e: e.memset(ones_f[:], 1.0), writes=["ones_f"])

        class Rot:
            def __init__(self, stack, name, shape, dt, n=2):
                self.t = [sb(stack, "%s_%d" % (name, i), shape, dt) for i in range(n)]
                self.k = ["%s_%d" % (name, i) for i in range(n)]
                self.i = 0

            def next(self):
                j = self.i
                self.i = (j + 1) % len(self.t)
                return self.t[j], self.k[j]

        def load_w(dst, key, src, ncols):
            kb.dma("pool", dst[:, :, 0:ncols], src.rearrange("(kc p) n -> p kc n", p=128), writes=[key])

        def proj_fm(w, wkey, c0, m, hT, hkeys, t0, nt, ps_ap, pskey):
            for kc in range(8):
                kb.op("pe", lambda e, kc=kc: e.matmul(ps_ap, lhsT=w[:, kc, c0:c0 + m], rhs=hT[:, kc, t0:t0 + nt],
                                                      start=(kc == 0), stop=(kc == 7)),
                      reads=[wkey] + hkeys, writes=[pskey])

        def proj_tm(w, wkey, c0, n, hT, hkeys, i, ps_ap, pskey):
            for kc in range(8):
                kb.op("pe", lambda e, kc=kc: e.matmul(ps_ap, lhsT=hT[:, kc, i * 128:(i + 1) * 128], rhs=w[:, kc, c0:c0 + n],
                                                      start=(kc == 0), stop=(kc == 7)),
                      reads=[wkey] + hkeys, writes=[pskey])

        mT = sb(G, "mT", [128, 8, S], BF16)
        H1K = ["h1T"]
        ev_rr = [0]

        def evac_copy(out_ap, in_ap, reads, writes):
            e = "act" if ev_rr[0] % 2 == 0 else "dve"
            ev_rr[0] += 1
            if e == "act":
                kb.op("act", lambda en: en.copy(out=out_ap, in_=in_ap), reads=reads, writes=writes)
            else:
                kb.op("dve", lambda en: en.tensor_copy(out=out_ap, in_=in_ap), reads=reads, writes=writes)

        with contextlib.ExitStack() as GS:
            yaT = sb(GS, "yaT", [128, 8, S], BF16)
            lrT = sb(GS, "lrT", [16, S], F32)
            wb = Rot(GS, "wb", [128, 8, 512], BF16, 2)
            wa2 = sb(GS, "wa2", [16, 512], F32)
            ba2 = sb(GS, "ba2", [1, 512], F32)
            gng = sb(GS, "gng", [128, 2], F32)
            kb.dma("sp", wa2[:], wa2_d, writes=["wa2"])
            kb.dma("sp", ba2[:], ba2_d, writes=["ba2"])
            kb.dma("sp", gng[:], gng_d, writes=["gng"])
            w, wk = wb.next()
            load_w(w, wk, win_d[:, 3072:3088], 16)
            for tc in range(4):
                p = ps_next()
                proj_fm(w, wk, 0, 16, h1T, H1K, tc * 512, 512, psb[p][0:16, :], ("ps", p))
                kb.op("act", lambda e, p=p, tc=tc: e.copy(out=lrT[:, tc * 512:(tc + 1) * 512], in_=psb[p][0:16, :]),
                      reads=[("ps", p)], writes=["lrT"])
            for blk in range(2):
                w, wk = wb.next()
                load_w(w, wk, win_d[:, 2048 + blk * 512:2048 + (blk + 1) * 512], 512)
                for mt in range(4):
                    for tc in range(4):
                        p = ps_next()
                        proj_fm(w, wk, mt * 128, 128, h1T, H1K, tc * 512, 512, psb[p][:, :], ("ps", p))
                        kb.op("act", lambda e, p=p, tc=tc, ti=blk * 4 + mt: e.activation(
                            out=yaT[:, ti, tc * 512:(tc + 1) * 512], in_=psb[p][:, :], func=AF.Silu),
                            reads=[("ps", p)], writes=[("yaT", blk * 4 + mt)])
            for hp in range(2):
                with contextlib.ExitStack() as HP:
                    qT = sb(HP, "qT", [128, 2, S], BF16)
                    kT = sb(HP, "kT", [128, 2, S], BF16)
                    ktm = sb(HP, "ktm", [128, NT, 256], BF16)
                    vtm = sb(HP, "vtm", [128, NT, 512], BF16)
                    state = sb(HP, "state", [128, 2, 256], F32)
                    state_bf = sb(HP, "state_bf", [128, 2, 256], BF16)
                    r_e1 = Rot(HP, "e1", [128, 256], F32)
                    r_l = Rot(HP, "l", [128, 256], F32)
                    r_er = Rot(HP, "er", [128, 256], F32)
                    r_kh = Rot(HP, "kh", [128, 256], BF16)
                    r_eq = Rot(HP, "eq", [128, 128], F32, 3)
                    r_ek = Rot(HP, "ek", [128, 128], F32, 3)
                    r_qt = Rot(HP, "qt", [128, 128], BF16, 3)
                    r_kt = Rot(HP, "kt", [128, 128], BF16, 3)
                    r_pT = Rot(HP, "pT", [128, 128], BF16, 3)
                    r_sq = Rot(HP, "sq", [128, 256], BF16, 3)
                    r_rt = Rot(HP, "rt", [128, 128], F32, 3)
                    r_rs = Rot(HP, "rs", [128, 128], F32, 3)
                    r_t1 = Rot(HP, "t1", [128, 256], F32, 3)
                    kb.op("dve", lambda e: e.memset(state[:], 0.0), writes=["state0", "state1"])
                    kb.op("dve", lambda e: e.memset(state_bf[:], 0.0), writes=["sbf0", "sbf1"])
                    for which, dst, cbase in (("q", qT, 0), ("k", kT, 512)):
                        w, wk = wb.next()
                        load_w(w, wk, win_d[:, cbase + hp * 256:cbase + (hp + 1) * 256], 256)
                        for hh in range(2):
                            for tc in range(4):
                                p = ps_next()
                                proj_fm(w, wk, hh * 128, 128, h1T, H1K, tc * 512, 512, psb[p][:, :], ("ps", p))
                                evac_copy(dst[:, hh, tc * 512:(tc + 1) * 512], psb[p][:, :], [("ps", p)], [(which + "T", hh)])
                        if which == "k":
                            for i in range(NT):
                                p = ps_next()
                                proj_tm(w, wk, 0, 256, h1T, H1K, i, psb[p][:, 0:256], ("ps", p))
                                evac_copy(ktm[:, i, :], psb[p][:, 0:256], [("ps", p)], [("ktm", i)])
                    w, wk = wb.next()
                    load_w(w, wk, win_d[:, 1024 + hp * 512:1024 + (hp + 1) * 512], 512)
                    for i in range(NT):
                        p = ps_next()
                        proj_tm(w, wk, 0, 512, h1T, H1K, i, psb[p][:, :], ("ps", p))
                        evac_copy(vtm[:, i, :], psb[p][:, :], [("ps", p)], [("vtm", i)])
                    for i in range(NT):
                        ts = slice(i * 128, (i + 1) * 128)
                        p = ps_next()
                        kb.op("pe", lambda e, p=p: e.matmul(psb[p][:, 0:256], lhsT=lrT[:, ts], rhs=wa2[:, hp * 256:(hp + 1) * 256],
                                                            start=True, stop=False), reads=["lrT", "wa2"], writes=[("ps", p)])
                        kb.op("pe", lambda e, p=p: e.matmul(psb[p][:, 0:256], lhsT=ones_f[0:1, 0:128], rhs=ba2[0:1, hp * 256:(hp + 1) * 256],
                                                            start=False, stop=True), reads=["ones_f", "ba2"], writes=[("ps", p)])
                        e1, e1k = r_e1.next()
                        lt, lk = r_l.next()
                        kb.op("act", lambda e, p=p, e1=e1: e.activation(out=e1[:], in_=psb[p][:, 0:256], func=AF.Exp, scale=-1.0),
                              reads=[("ps", p)], writes=[e1k])
                        kb.op("act", lambda e, e1=e1, lt=lt: e.activation(out=lt[:], in_=e1[:], func=AF.Ln, bias=1.0),
                              reads=[e1k], writes=[lk])
                        p = ps_next()
                        kb.op("pe", lambda e, p=p, lt=lt: e.matmul(psb[p][:, 0:256], lhsT=SU[:], rhs=lt[:], start=True, stop=True),
                              reads=["SU", lk], writes=[("ps", p)])
                        er, erk = r_er.next()
                        kb.op("act", lambda e, p=p, er=er: e.activation(out=er[:], in_=psb[p][:, 0:256], func=AF.Exp, scale=-1.0 / 16),
                              reads=[("ps", p)], writes=[erk])
                        kh, khk = r_kh.next()
                        kb.op("dve", lambda e, kh=kh, er=er: e.tensor_tensor(out=kh[:], in0=ktm[:, i, :], in1=er[:], op=ALU.mult),
                              reads=[("ktm", i), erk], writes=[khk])
                        for hh in range(2):
                            h = hp * 2 + hh
                            p = ps_next()
                            kb.op("pe", lambda e, p=p, lt=lt: e.matmul(psb[p][:, 0:128], lhsT=lt[:, hh * 128:(hh + 1) * 128], rhs=LT[:],
                                                                      start=True, stop=True), reads=[lk, "LT"], writes=[("ps", p)])
                            eq, eqk = r_eq.next()
                            ek, ekk = r_ek.next()
                            kb.op("act", lambda e, p=p, eq=eq: e.activation(out=eq[:], in_=psb[p][:, 0:128], func=AF.Exp, scale=-1.0 / 16),
                                  reads=[("ps", p)], writes=[eqk])
                            kb.op("act", lambda e, p=p, ek=ek: e.activation(out=ek[:], in_=psb[p][:, 0:128], func=AF.Exp, scale=1.0 / 16),
                                  reads=[("ps", p)], writes=[ekk])
                            qt, qtk = r_qt.next()
                            kt, ktk = r_kt.next()
                            kb.op("dve", lambda e, qt=qt, eq=eq: e.scalar_tensor_tensor(
                                out=qt[:], in0=qT[:, hh, ts], scalar=float(128 ** -0.5), in1=eq[:], op0=ALU.mult, op1=ALU.mult),
                                reads=[("qT", hh), eqk], writes=[qtk])
                            kb.op("dve", lambda e, kt=kt, ek=ek: e.tensor_tensor(out=kt[:], in0=kT[:, hh, ts], in1=ek[:], op=ALU.mult),
                                  reads=[("kT", hh), ekk], writes=[ktk])
                            p = ps_next()
                            kb.op("pe", lambda e, p=p, kt=kt, qt=qt: e.matmul(psb[p][:, 0:128], lhsT=kt[:], rhs=qt[:], start=True, stop=True),
                                  reads=[ktk, qtk], writes=[("ps", p)])
                            pT, pTk = r_pT.next()
                            kb.op("dve", lambda e, p=p, pT=pT: e.tensor_tensor(out=pT[:], in0=psb[p][:, 0:128], in1=LT[:], op=ALU.mult),
                                  reads=[("ps", p), "LT"], writes=[pTk])
                            po = ps_next()
                            for vh in range(2):
                                kb.op("pe", lambda e, po=po, vh=vh, pT=pT: e.matmul(
                                    psb[po][:, vh * 128:(vh + 1) * 128], lhsT=vtm[:, i, hh * 256 + vh * 128:hh * 256 + (vh + 1) * 128],
                                    rhs=pT[:], start=True, stop=False), reads=[("vtm", i), pTk], writes=[("ps", po)])
                                kb.op("pe", lambda e, po=po, vh=vh, qt=qt: e.matmul(
                                    psb[po][:, vh * 128:(vh + 1) * 128], lhsT=state_bf[:, hh, vh * 128:(vh + 1) * 128],
                                    rhs=qt[:], start=False, stop=True), reads=["sbf%d" % hh, qtk], writes=[("ps", po)])
                            sq, sqk = r_sq.next()
                            kb.op("act", lambda e, po=po, sq=sq: e.activation(out=sq[:], in_=psb[po][:, 0:256], func=AF.Square),
                                  reads=[("ps", po)], writes=[sqk])
                            p2 = ps_next()
                            for vh in range(2):
                                kb.op("pe", lambda e, p2=p2, vh=vh, sq=sq: e.matmul(
                                    psb[p2][:, 0:128], lhsT=ones_bf[:], rhs=sq[:, vh * 128:(vh + 1) * 128],
                                    start=(vh == 0), stop=(vh == 1)), reads=["ones_bf", sqk], writes=[("ps", p2)])
                            rt, rtk = r_rt.next()
                            rs, rsk = r_rs.next()
                            kb.op("act", lambda e, p2=p2, rt=rt: e.activation(out=rt[:], in_=psb[p2][:, 0:128], func=AF.Sqrt,
                                                                              scale=1.0 / 256, bias=EPS), reads=[("ps", p2)], writes=[rtk])
                            kb.op("dve", lambda e, rt=rt, rs=rs: e.reciprocal(out=rs[:], in_=rt[:]), reads=[rtk], writes=[rsk])
                            t1, t1k = r_t1.next()
                            for vh in range(2):
                                kb.op("dve", lambda e, po=po, vh=vh, t1=t1, rs=rs: e.scalar_tensor_tensor(
                                    out=t1[:, vh * 128:(vh + 1) * 128], in0=psb[po][:, vh * 128:(vh + 1) * 128],
                                    scalar=gng[:, vh:vh + 1], in1=rs[:], op0=ALU.mult, op1=ALU.mult),
                                    reads=[("ps", po), "gng", rsk], writes=[t1k])
                            for vh in range(2):
                                kb.op("pool", lambda e, vh=vh, t1=t1, h=h: e.tensor_tensor(
                                    out=yaT[:, h * 2 + vh, ts], in0=t1[:, vh * 128:(vh + 1) * 128], in1=yaT[:, h * 2 + vh, ts],
                                    op=ALU.mult), reads=[t1k, ("yaT", h * 2 + vh)], writes=[("yaT", h * 2 + vh)])
                            pk = ps_next()
                            kb.op("pe", lambda e, pk=pk, kh=kh: e.matmul(psb[pk][:, 0:256], lhsT=kh[:, hh * 128:(hh + 1) * 128],
                                                                        rhs=vtm[:, i, hh * 256:(hh + 1) * 256], start=True, stop=True),
                                  reads=[khk, ("vtm", i)], writes=[("ps", pk)])
                            kb.op("dve", lambda e, pk=pk, eq=eq: e.scalar_tensor_tensor(
                                out=state[:, hh, :], in0=state[:, hh, :], scalar=eq[:, 127:128], in1=psb[pk][:, 0:256],
                                op0=ALU.mult, op1=ALU.add), reads=["state%d" % hh, eqk, ("ps", pk)], writes=["state%d" % hh])
                            kb.op("act", lambda e: e.copy(out=state_bf[:, hh, :], in_=state[:, hh, :]),
                                  reads=["state%d" % hh], writes=["sbf%d" % hh])
                    kb.barrier()
            if "yaT" in dbg:
                kb.dma("sp", dbg_out("yaT", [128, 8, S], BF16), yaT[:], reads=[("yaT", i) for i in range(8)])
            for blk in range(2):
                wm, wmk = wb.next()
                load_w(wm, wmk, win_d[:, 4928 + blk * 512:4928 + (blk + 1) * 512], 512)
                wa, wak = wb.next()
                load_w(wa, wak, wba_d[:, blk * 512:(blk + 1) * 512], 512)
                with contextlib.ExitStack() as BS:
                    r_sg = Rot(BS, "sg", [128, 512], F32, 2)
                    for mt in range(4):
                        for tc in range(4):
                            p1 = ps_next()
                            proj_fm(wm, wmk, mt * 128, 128, h1T, H1K, tc * 512, 512, psb[p1][:, :], ("ps", p1))
                            sg, sgk = r_sg.next()
                            kb.op("act", lambda e, p1=p1, sg=sg: e.activation(out=sg[:], in_=psb[p1][:, :], func=AF.Sigmoid),
                                  reads=[("ps", p1)], writes=[sgk])
                            p2 = ps_next()
                            proj_fm(wa, wak, mt * 128, 128, yaT, [("yaT", j) for j in range(8)], tc * 512, 512, psb[p2][:, :], ("ps", p2))
                            kb.op("dve", lambda e, p2=p2, sg=sg, ti=blk * 4 + mt, tc=tc: e.tensor_tensor(
                                out=mT[:, ti, tc * 512:(tc + 1) * 512], in0=psb[p2][:, :], in1=sg[:], op=ALU.mult),
                                reads=[("ps", p2), sgk], writes=[("mT", blk * 4 + mt)])
                    kb.barrier()
            kb.barrier()
        pos_d = nc.dram_tensor("pos", [1, S], I32, kind="ExternalInput").ap()
        ropec_d = din("rope_c", [16, 4])
        wqrot_d = din("w_qrot", [D, 256])
        wkrot_d = din("w_krot", [D, 96])
        cw1_d = din("cmp_w1", [2, 64, 32 * 64])
        cw2_d = din("cmp_w2", [2, 64, 64])
        cpe_d = din("cmp_peT", [2, 64, 32])
        ovl_d = din("ovl", [127, 32])
        cmpb_d = din("cmpbias", [127, S])
        blk_d = din("blockind", [32, S])
        cb_d = din("causalbias", [128, 128])
        ab_d = din("antibias", [128, 128])
        Msel_d = din("Msel", [128, NT, 32])
        Fsel_d = din("Fsel", [128, NT, 32])
        wbb_d = din("w_branch_b", [D, D])
        NEG = 30000.0

        with contextlib.ExitStack() as NS:
            zeros_bf = sb(NS, "zeros_bf", [128, 512], BF16)
            kb.op("pool", lambda e: e.memset(zeros_bf[:], 0.0), writes=["zeros_bf"])
            cosT = sb(NS, "cosT", [16, S], BF16)
            sinT = sb(NS, "sinT", [16, S], BF16)
            gates = sb(NS, "gates", [128, NT, 48], F32)
            ovl_f = sb(NS, "ovl_f", [127, 32], F32)
            cmpb = sb(NS, "cmpb", [127, S], BF16)
            cb = sb(NS, "cb", [128, 128], BF16)
            ab = sb(NS, "ab", [128, 128], BF16)
            Msel = sb(NS, "Msel", [128, NT, 32], F32)
            Fsel = sb(NS, "Fsel", [128, NT, 32], F32)
            kb.dma("sp", ovl_f[:], ovl_d, writes=["ovl_f"])
            kb.dma("pool", cmpb[:], cmpb_d, writes=["cmpb"])
            kb.dma("pool", cb[:], cb_d, writes=["cb"])
            kb.dma("pool", ab[:], ab_d, writes=["ab"])
            kb.dma("sp", Msel[:], Msel_d, writes=["Msel"])
            kb.dma("sp", Fsel[:], Fsel_d, writes=["Fsel"])
            with contextlib.ExitStack() as RS:
                posi = sb(RS, "posi", [16, S], I32)
                posf = sb(RS, "posf", [16, S], F32)
                ang = sb(RS, "ang", [16, S], F32)
                ki = sb(RS, "ki", [16, S], I32)
                kf = sb(RS, "kf", [16, S], F32)
                rr = sb(RS, "rr", [16, S], F32)
                ropec = sb(RS, "ropec", [16, 4], F32)
                kb.dma("sp", posi[:], pos_d.broadcast_to([16, S]), writes=["posi"])
                kb.dma("sp", ropec[:], ropec_d, writes=["ropec"])
                kb.op("dve", lambda e: e.tensor_copy(out=posf[:], in_=posi[:]), reads=["posi"], writes=["posf"])
                for col, dst, dk in ((1, cosT, "cosT"), (2, sinT, "sinT")):
                    kb.op("dve", lambda e, col=col: e.tensor_scalar(out=ang[:], in0=posf[:], scalar1=ropec[:, 0:1], scalar2=ropec[:, col:col + 1],
                                                                    op0=ALU.mult, op1=ALU.add), reads=["posf", "ropec"], writes=["ang"])
                    kb.op("dve", lambda e: e.tensor_scalar(out=ki[:], in0=ang[:], scalar1=float(1.0 / (2 * np.pi)), scalar2=None,
                                                           op0=ALU.mult), reads=["ang"], writes=["ki"])
                    kb.op("dve", lambda e: e.tensor_copy(out=kf[:], in_=ki[:]), reads=["ki"], writes=["kf"])
                    kb.op("dve", lambda e: e.scalar_tensor_tensor(out=rr[:], in0=kf[:], scalar=float(-2 * np.pi), in1=ang[:],
                                                                  op0=ALU.mult, op1=ALU.add), reads=["kf", "ang"], writes=["rr"])
                    kb.op("dve", lambda e: e.tensor_scalar(out=kf[:], in0=rr[:], scalar1=float(np.pi), scalar2=float(2 * np.pi),
                                                           op0=ALU.is_gt, op1=ALU.mult), reads=["rr"], writes=["kf"])
                    kb.op("dve", lambda e: e.tensor_tensor(out=rr[:], in0=rr[:], in1=kf[:], op=ALU.subtract), reads=["rr", "kf"], writes=["rr"])
                    kb.op("dve", lambda e: e.tensor_scalar(out=kf[:], in0=rr[:], scalar1=float(-np.pi), scalar2=float(2 * np.pi),
                                                           op0=ALU.is_lt, op1=ALU.mult), reads=["rr"], writes=["kf"])
                    kb.op("dve", lambda e: e.tensor_tensor(out=rr[:], in0=rr[:], in1=kf[:], op=ALU.add), reads=["rr", "kf"], writes=["rr"])
                    kb.op("act", lambda e, dst=dst: e.activation(out=dst[:], in_=rr[:], func=AF.Sin), reads=["rr"], writes=[dk])
                kb.barrier()
            kstop("rope")
            wbn = Rot(NS, "wbn", [128, 8, 512], BF16, 2)
            w, wk = wbn.next()
            load_w(w, wk, win_d[:, 4880:4928], 48)
            for i in range(NT):
                p = ps_next()
                proj_tm(w, wk, 0, 48, h1T, H1K, i, psb[p][:, 0:48], ("ps", p))
                kb.op("act", lambda e, p=p, i=i: e.activation(out=gates[:, i, :], in_=psb[p][:, 0:48], func=AF.Sigmoid),
                      reads=[("ps", p)], writes=["gates"])

            kstop("gates")
            for g in range(2):
                with contextlib.ExitStack() as GR:
                    Qa = [sb(GR, "Qa%d" % hh, [96, S], BF16) for hh in range(8)]
                    kcT = sb(GR, "kcT", [64, S], BF16)
                    ksT = sb(GR, "ksT", [96, S], BF16)
                    kwT = sb(GR, "kwT", [64, S], BF16)
                    vs_a = sb(GR, "vs_a", [128, NT, 65], BF16)
                    vw_a = sb(GR, "vw_a", [128, NT, 65], BF16)
                    vcmp = sb(GR, "vcmp", [127, 97], BF16)
                    kcmpT = sb(GR, "kcmpT", [64, 127], BF16)
                    yb = sb(GR, "yb", [128, NT, 512], BF16)
                    score = sb(GR, "score", [128, NT, 32], F32)
                    PJ = contextlib.ExitStack()
                    vcT = sb(PJ, "vcT", [64, S], BF16)
                    cw1 = sb(PJ, "cw1", [64, 2, 32 * 64], BF16)
                    cw2 = sb(PJ, "cw2", [64, 2, 64], BF16)
                    cpe = sb(PJ, "cpe", [64, 2, 32], BF16)
                    peb = sb(PJ, "peb", [64, 2], F32)
                    gsb = sb(PJ, "gsb", [64, 2, 127], BF16)
                    r_t1 = Rot(PJ, "rt1", [16, 512], F32, 1)
                    r_t2 = Rot(PJ, "rt2", [16, 512], F32, 1)
                    wqr = sb(PJ, "wqr", [128, 8, 128], BF16)
                    wkr = sb(PJ, "wkr", [128, 8, 96], BF16)
                    wkv = sb(PJ, "wkv", [128, 8, 384], BF16)
                    for wi in range(6):
                        kb.dma("pool", wkv[:, :, wi * 64:(wi + 1) * 64],
                               win_d[:, 4112 + wi * 128 + g * 64:4112 + wi * 128 + (g + 1) * 64].rearrange("(kc p) n -> p kc n", p=128),
                               writes=["wkv"])
                    kb.op("dve", lambda e: e.memset(score[:], 0.0), writes=["score"])
                    kb.op("pool", lambda e: e.memset(vs_a[:, :, 64:65], 1.0), writes=["vs_a"])
                    kb.op("pool", lambda e: e.memset(vw_a[:, :, 64:65], 1.0), writes=["vw_a"])
                    kb.op("pool", lambda e: e.memset(vcmp[:, 64:65], 1.0), writes=["vcmp"])
                    kb.op("dve", lambda e: e.tensor_copy(out=vcmp[:, 65:97], in_=ovl_f[:]), reads=["ovl_f"], writes=["vcmp"])
                    for hh in range(8):
                        kb.op("pool", lambda e, hh=hh: e.memset(Qa[hh][64:96, :], 0.0), writes=[("Qa", hh)])
                    kb.dma("pool", ksT[64:96, :], blk_d, writes=["ksT"])
                    for kv in range(2):
                        kb.dma("pool", cw1[:, kv, :], cw1_d[kv], writes=["cw1"])
                    kb.dma("pool", cw2[:], cw2_d.rearrange("k d c -> d k c"), writes=["cw2"])
                    kb.dma("pool", cpe[:], cpe_d.rearrange("k d l -> d k l"), writes=["cpe"])
                    load_w(wqr, "wqr", wqrot_d[:, g * 128:(g + 1) * 128], 128)
                    load_w(wkr, "wkr", wkrot_d, 96)

                    def rope_evac(dst, dkey, pm, pr, tc):
                        cs = slice(tc * 512, (tc + 1) * 512)
                        kb.op("act", lambda e: e.copy(out=dst[0:64, cs], in_=psb[pm][0:64, :]), reads=[("ps", pm)], writes=[dkey])
                        import os
                        if os.environ.get("KNOROPE"):
                            return
                        t1, t1k = r_t1.next()
                        t2, t2k = r_t2.next()
                        kb.op("dve", lambda e: e.tensor_tensor(out=t1[:], in0=cosT[:, cs], in1=psb[pm][0:16, :], op=ALU.mult),
                              reads=[("ps", pm), "cosT"], writes=[t1k])
                        kb.op("dve", lambda e: e.tensor_tensor(out=t2[:], in0=sinT[:, cs], in1=psb[pr][0:16, :], op=ALU.mult),
                              reads=[("ps", pr), "sinT"], writes=[t2k])
                        if os.environ.get("KNOPOOL"):
                            return
                        kb.op("pool", lambda e: e.tensor_tensor(out=dst[0:16, cs], in0=t1[:], in1=t2[:], op=ALU.add),
                              reads=[t1k, t2k], writes=[dkey])

                    kstop("grsetup")
                    w, wk = wbn.next()
                    load_w(w, wk, win_d[:, 3088 + g * 512:3088 + (g + 1) * 512], 512)
                    for hh in range(8):
                        for tc in range(4):
                            pm = ps_next()
                            proj_fm(w, wk, hh * 64, 64, h1T, H1K, tc * 512, 512, psb[pm][0:64, :], ("ps", pm))
                            pr = ps_next()
                            proj_fm(wqr, "wqr", hh * 16, 16, h1T, H1K, tc * 512, 512, psb[pr][0:16, :], ("ps", pr))
                            rope_evac(Qa[hh], ("Qa", hh), pm, pr, tc)
                    kstop("qproj")
                    for wi, dst, dkey, ri_ in ((0, kcT, "kcT", 0), (2, ksT, "ksT", 1), (4, kwT, "kwT", 2)):
                        for tc in range(4):
                            pm = ps_next()
                            proj_fm(wkv, "wkv", wi * 64, 64, h1T, H1K, tc * 512, 512, psb[pm][0:64, :], ("ps", pm))
                            pr = ps_next()
                            proj_fm(wkr, "wkr", ri_ * 32 + g * 16, 16, h1T, H1K, tc * 512, 512, psb[pr][0:16, :], ("ps", pr))
                            rope_evac(dst, dkey, pm, pr, tc)
                    for tc in range(4):
                        pm = ps_next()
                        proj_fm(wkv, "wkv", 1 * 64, 64, h1T, H1K, tc * 512, 512, psb[pm][0:64, :], ("ps", pm))
                        kb.op("act", lambda e, pm=pm, tc=tc: e.copy(out=vcT[:, tc * 512:(tc + 1) * 512], in_=psb[pm][0:64, :]),
                              reads=[("ps", pm)], writes=["vcT"])
                    for wi, dst, dkey in ((3, vs_a, "vs_a"), (5, vw_a, "vw_a")):
                        for i in range(NT):
                            pm = ps_next()
                            proj_tm(wkv, "wkv", wi * 64, 64, h1T, H1K, i, psb[pm][:, 0:64], ("ps", pm))
                            evac_copy(dst[:, i, 0:64], psb[pm][:, 0:64], [("ps", pm)], [dkey])
                    kstop("kvproj")
                    for kv, srcT, skey in ((0, kcT, "kcT"), (1, vcT, "vcT")):
                        p = ps_next()
                        for l in range(32):
                            kb.op("pe", lambda e, p=p, l=l, kv=kv: e.matmul(psb[p][0:64, 0:1], lhsT=cw1[:, kv, l * 64:(l + 1) * 64],
                                                                            rhs=cpe[:, kv, l:l + 1], start=(l == 0), stop=(l == 31)),
                                  reads=["cw1", "cpe"], writes=[("ps", p)])
                        kb.op("act", lambda e, p=p, kv=kv: e.copy(out=peb[:, kv:kv + 1], in_=psb[p][0:64, 0:1]),
                              reads=[("ps", p)], writes=["peb"])
                        p = ps_next()
                        for l in range(32):
                            kb.op("pe", lambda e, p=p, l=l, kv=kv, srcT=srcT: e.matmul(
                                psb[p][0:64, 0:127], lhsT=cw1[:, kv, l * 64:(l + 1) * 64], rhs=srcT[0:64, l:l + 16 * 126 + 1:16],
                                start=(l == 0), stop=(l == 31)), reads=["cw1", skey], writes=[("ps", p)])
                        kb.op("act", lambda e, p=p, kv=kv: e.activation(out=gsb[:, kv, :], in_=psb[p][0:64, 0:127], func=AF.Gelu,
                                                                        bias=peb[:, kv:kv + 1]), reads=[("ps", p), "peb"], writes=["gsb"])
                    p = ps_next()
                    kb.op("pe", lambda e, p=p: e.matmul(psb[p][0:64, 0:127], lhsT=cw2[:, 0, :], rhs=gsb[:, 0, :], start=True, stop=True),
                          reads=["cw2", "gsb"], writes=[("ps", p)])
                    kb.op("act", lambda e, p=p: e.copy(out=kcmpT[:], in_=psb[p][0:64, 0:127]), reads=[("ps", p)], writes=["kcmpT"])
                    p = ps_next()
                    kb.op("pe", lambda e, p=p: e.matmul(psb[p][0:127, 0:64], lhsT=gsb[:, 1, :], rhs=cw2[:, 1, :], start=True, stop=True),
                          reads=["cw2", "gsb"], writes=[("ps", p)])
                    kb.op("act", lambda e, p=p: e.copy(out=vcmp[:, 0:64], in_=psb[p][0:127, 0:64]), reads=[("ps", p)], writes=["vcmp"])

                    kb.barrier()
                    PJ.close()
                    AT = contextlib.ExitStack()
                    r_PT = Rot(AT, "PT", [128, 512], BF16, 4)
                    r_ri = Rot(AT, "ri", [128, 8], F32, 3)
                    r_tm = Rot(AT, "otmp", [128, 4, 64], F32, 2)
                    r_sc = Rot(AT, "sct", [128, 4, 32], F32, 2)
                    def attn_chunk(hh, tc, K, kT, kkey, kt_list, rhs_of, vkey, vn, pv_map, br):
                        po = ps_next_acc()
                        kb.op("pe", lambda e: e.matmul(psb[po][:, 0:4 * vn], lhsT=zeros_bf[:, 0:128], rhs=zeros_bf[:, 0:4 * vn],
                                                       start=True, stop=True), reads=["zeros_bf"], writes=[("ps", po)])
                        total = sum(len(x) for x in pv_map)
                        done = [0]

                        def emit_pv(entry):
                            j, PT, PTk, nk = entry
                            for qi in range(4):
                                if j in pv_map[qi]:
                                    done[0] += 1
                                    last = (done[0] == total)
                                    kb.op("pe", lambda e, qi=qi: e.matmul(
                                        psb[po][:, qi * vn:(qi + 1) * vn], lhsT=PT[0:nk, qi * 128:(qi + 1) * 128], rhs=rhs_of(j),
                                        start=False, stop=True, skip_group_check=True), reads=[PTk, vkey], writes=[("ps", po)])
                        pend = []
                        for (j, nk, c0, c1, masks) in kt_list:
                            p = ps_next()
                            kb.op("pe", lambda e, p=p, j=j, nk=nk, c0=c0, c1=c1: e.matmul(
                                psb[p][0:nk, c0:c1], lhsT=kT[0:K, j * 128:j * 128 + nk], rhs=Qa[hh][0:K, tc * 512 + c0:tc * 512 + c1],
                                start=True, stop=(len(masks) == 0)), reads=[kkey, ("Qa", hh)], writes=[("ps", p)])
                            for mi, (col0, width, bt, bkey) in enumerate(masks):
                                kb.op("pe", lambda e, p=p, nk=nk, col0=col0, width=width, bt=bt, mi=mi: e.matmul(
                                    psb[p][0:nk, col0:col0 + width], lhsT=ident_b[0:nk, 0:nk], rhs=bt,
                                    start=False, stop=(mi == len(masks) - 1)), reads=["ident_b", bkey], writes=[("ps", p)])
                            PT, PTk = r_PT.next()
                            kb.op("act", lambda e, p=p, nk=nk, c0=c0, c1=c1, PT=PT: e.activation(
                                out=PT[0:nk, c0:c1], in_=psb[p][0:nk, c0:c1], func=AF.Exp, scale=0.125),
                                reads=[("ps", p)], writes=[PTk])
                            pend.append((j, PT, PTk, nk))
                            if len(pend) > 2:
                                emit_pv(pend.pop(0))
                        while pend:
                            emit_pv(pend.pop(0))
                        O = psb[po][:, 0:4 * vn].rearrange("p (a c) -> p a c", c=vn)
                        ri, rik = r_ri.next()
                        kb.op("dve", lambda e: e.tensor_scalar(out=ri[:, 0:4], in0=O[:, :, 64], scalar1=1e-30, scalar2=None, op0=ALU.add),
                              reads=[("ps", po)], writes=[rik])
                        kb.op("dve", lambda e: e.reciprocal(out=ri[:, 4:8], in_=ri[:, 0:4]), reads=[rik], writes=[rik])
                        if br == 0:
                            sct, sck = r_sc.next()
                            kb.op("dve", lambda e: e.tensor_tensor(out=sct[:], in0=O[:, :, 65:97],
                                                                   in1=ri[:, 4:8].unsqueeze(2).broadcast_to([128, 4, 32]), op=ALU.mult),
                                  reads=[("ps", po), rik], writes=[sck])
                            kb.op("pool", lambda e: e.tensor_tensor(out=score[:, 4 * tc:4 * tc + 4, :], in0=score[:, 4 * tc:4 * tc + 4, :],
                                                                    in1=sct[:], op=ALU.add), reads=[sck, "score"], writes=["score"])
                        gcol = br * 16 + g * 8 + hh
                        kb.op("dve", lambda e: e.tensor_tensor(out=ri[:, 0:4], in0=ri[:, 4:8], in1=gates[:, 4 * tc:4 * tc + 4, gcol], op=ALU.mult),
                              reads=[rik, "gates"], writes=[rik])
                        ydst = yb[:, 4 * tc:4 * tc + 4, hh * 64:(hh + 1) * 64]
                        rgb = ri[:, 0:4].unsqueeze(2).broadcast_to([128, 4, 64])
                        if br == 0:
                            kb.op("dve", lambda e: e.tensor_tensor(out=ydst, in0=O[:, :, 0:64], in1=rgb, op=ALU.mult),
                                  reads=[("ps", po), rik], writes=[("yb", hh, tc)])
                        else:
                            ot, otk = r_tm.next()
                            kb.op("dve", lambda e: e.tensor_tensor(out=ot[:], in0=O[:, :, 0:64], in1=rgb, op=ALU.mult),
                                  reads=[("ps", po), rik], writes=[otk])
                            kb.op("pool", lambda e: e.tensor_tensor(out=ydst, in0=ydst, in1=ot[:], op=ALU.add),
                                  reads=[otk, ("yb", hh, tc)], writes=[("yb", hh, tc)])

                    kstop("cmp")
                    for hh in range(8):
                        for tc in range(4):
                            attn_chunk(hh, tc, 64, kcmpT, "kcmpT",
                                       [(0, 127, 0, 512, [(0, 512, cmpb[:, tc * 512:(tc + 1) * 512], "cmpb")])],
                                       lambda j: vcmp[0:127, 0:97], "vcmp", 97, [[0], [0], [0], [0]], 0)
                    kstop("passA")
                    r_scp = Rot(AT, "scp", [128, 32], F32, 2)
                    r_sc2 = Rot(AT, "sc2", [128, 32], F32, 2)
                    r_m8 = Rot(AT, "m8", [128, 16], F32, 2)
                    r_bt = Rot(AT, "biast", [128, 96], BF16, 2)
                    for bt_, btk_ in zip(r_bt.t, r_bt.k):
                        kb.op("dve", lambda e, bt_=bt_: e.memset(bt_[:], 0.0), writes=[btk_])
                    for i in range(8, NT):
                        scp, scpk = r_scp.next()
                        sc2, sc2k = r_sc2.next()
                        m8, m8k = r_m8.next()
                        bt_, btk_ = r_bt.next()
                        kb.op("dve", lambda e: e.scalar_tensor_tensor(out=scp[:], in0=score[:, i, :], scalar=1.0, in1=Msel[:, i, :],
                                                                      op0=ALU.add, op1=ALU.mult), reads=["score", "Msel"], writes=[scpk])
                        kb.op("dve", lambda e: e.max(out=m8[:, 0:8], in_=scp[:]), reads=[scpk], writes=[m8k])
                        kb.op("dve", lambda e: e.match_replace(out=sc2[:], in_to_replace=m8[:, 0:8], in_values=scp[:], imm_value=-1.0),
                              reads=[scpk, m8k], writes=[sc2k])
                        kb.op("dve", lambda e: e.max(out=m8[:, 8:16], in_=sc2[:]), reads=[sc2k], writes=[m8k])
                        kb.op("dve", lambda e: e.tensor_scalar(out=sc2[:], in0=scp[:], scalar1=m8[:, 12:13], scalar2=None, op0=ALU.is_ge),
                              reads=[scpk, m8k], writes=[sc2k])
                        kb.op("dve", lambda e: e.tensor_tensor(out=sc2[:], in0=sc2[:], in1=Fsel[:, i, :], op=ALU.max),
                              reads=[sc2k, "Fsel"], writes=[sc2k])
                        kb.op("dve", lambda e: e.tensor_scalar(out=bt_[:, 64:96], in0=sc2[:], scalar1=NEG, scalar2=-NEG,
                                                               op0=ALU.mult, op1=ALU.add), reads=[sc2k], writes=[btk_])
                        p = ps_next()
                        pv = psb[p][:].bitcast(BF16)
                        kb.op("pe", lambda e, pv=pv: e.transpose(out=pv[0:96, 0:128], in_=bt_[:, 0:96], identity=ident_b[:]),
                              reads=[btk_, "ident_b"], writes=[("ps", p)])
                        for hh in range(8):
                            evac_copy(Qa[hh][64:96, i * 128:(i + 1) * 128], pv[64:96, 0:128], [("ps", p)], [("Qa", hh)])
                    if "score" in dbg and g == 0:
                        kb.dma("sp", dbg_out("score", [128, NT, 32]), score[:], reads=["score"])
                    if "ycmp" in dbg and g == 0:
                        kb.dma("sp", dbg_out("ycmp", [128, NT, 512], BF16), yb[:], reads=[("yb", hh, tc) for hh in range(8) for tc in range(4)])
                    kstop("passB")
                    for hh in range(8):
                        for tc in range(4):
                            ktl = []
                            for j in range(4 * tc + 4):
                                if j < 4 * tc:
                                    ktl.append((j, 128, 0, 512, []))
                                else:
                                    r = j - 4 * tc
                                    ktl.append((j, 128, 128 * r, 512, [(128 * r, 128, cb[:], "cb")]))
                            import os
                            KBR = os.environ.get("KBR", "12")
                            if "1" in KBR:
                                attn_chunk(hh, tc, 96, ksT, "ksT", ktl, lambda j: vs_a[:, j, 0:65], "vs_a", 65,
                                           [list(range(0, 4 * tc + qi + 1)) for qi in range(4)], 1)
                            ktl = []
                            for j in range(max(0, 4 * tc - 4), 4 * tc + 4):
                                jj = j - 4 * tc
                                c0 = 128 * max(jj, 0)
                                c1 = 128 * (min(jj + 4, 3) + 1)
                                masks = []
                                if jj >= 0:
                                    masks.append((128 * jj, 128, cb[:], "cb"))
                                if jj + 4 <= 3:
                                    masks.append((128 * (jj + 4), 128, ab[:], "ab"))
                                ktl.append((j, 128, c0, c1, masks))
                            if "2" in KBR:
                                attn_chunk(hh, tc, 64, kwT, "kwT", ktl, lambda j: vw_a[:, j, 0:65], "vw_a", 65,
                                           [list(range(max(0, 4 * tc + qi - 4), 4 * tc + qi + 1)) for qi in range(4)], 2)
                    kb.barrier()
                    AT.close()
                    FB = contextlib.ExitStack()
                    ybT = sb(FB, "ybT", [128, 4, S], BF16)
                    r_sg = Rot(FB, "sgb", [128, 512], F32, 2)
                    r_tb = Rot(FB, "tb", [128, 512], F32, 2)
                    kstop("passC")
                    for i in range(NT):
                        p = ps_next()
                        pv = psb[p][:].bitcast(BF16)
                        for m in range(4):
                            kb.op("pe", lambda e, pv=pv, m=m, i=i: e.transpose(out=pv[:, m * 128:(m + 1) * 128], in_=yb[:, i, m * 128:(m + 1) * 128],
                                                                              identity=ident_b[:]),
                                  reads=[("yb", hh, i // 4) for hh in range(8)] + ["ident_b"], writes=[("ps", p)])
                        evac_copy(ybT[:, :, i * 128:(i + 1) * 128], pv[:, 0:512].rearrange("p (m t) -> p m t", t=128),
                                  [("ps", p)], ["ybT"])
                    kstop("fbT")
                    if "ybT" in dbg:
                        kb.dma("sp", dbg_out("ybT%d" % g, [128, 4, S], BF16), ybT[:], reads=["ybT"])
                    for blk in range(2):
                        wm, wmk = wbn.next()
                        load_w(wm, wmk, win_d[:, 5952 + blk * 512:5952 + (blk + 1) * 512], 512)
                        wa, wak = wbn.next()
                        kb.dma("pool", wa[:, 0:4, :], wbb_d[g * 512:(g + 1) * 512, blk * 512:(blk + 1) * 512].rearrange("(kc p) n -> p kc n", p=128),
                               writes=[wak])
                        for mt in range(4):
                            for tc in range(4):
                                p1 = ps_next()
                                proj_fm(wm, wmk, mt * 128, 128, h1T, H1K, tc * 512, 512, psb[p1][:, :], ("ps", p1))
                                sg, sgk = r_sg.next()
                                kb.op("act", lambda e, p1=p1, sg=sg: e.activation(out=sg[:], in_=psb[p1][:, :], func=AF.Sigmoid),
                                      reads=[("ps", p1)], writes=[sgk])
                                p2 = ps_next()
                                for kc in range(4):
                                    kb.op("pe", lambda e, kc=kc, p2=p2: e.matmul(psb[p2][:, :], lhsT=wa[:, kc, mt * 128:(mt + 1) * 128],
                                                                                 rhs=ybT[:, kc, tc * 512:(tc + 1) * 512],
                                                                                 start=(kc == 0), stop=(kc == 3)),
                                          reads=[wak, "ybT"], writes=[("ps", p2)])
                                tb, tbk = r_tb.next()
                                kb.op("dve", lambda e, p2=p2, sg=sg, tb=tb: e.tensor_tensor(out=tb[:], in0=psb[p2][:, :], in1=sg[:], op=ALU.mult),
                                      reads=[("ps", p2), sgk], writes=[tbk])
                                ti = blk * 4 + mt
                                kb.op("pool", lambda e, tb=tb, ti=ti, tc=tc: e.tensor_tensor(
                                    out=mT[:, ti, tc * 512:(tc + 1) * 512], in0=mT[:, ti, tc * 512:(tc + 1) * 512], in1=tb[:], op=ALU.add),
                                    reads=[tbk, ("mT", ti)], writes=[("mT", ti)])
                    kb.barrier()
                    FB.close()
                    kstop("g0")
            kb.barrier()
        if "mT" in dbg:
            kb.dma("sp", dbg_out("mT", [128, 8, S], BF16), mT[:], reads=[("mT", i) for i in range(8)])
        wout_d = din("w_out", [D, D])
        wq_d = din("peer_wq", [D, 2048])
        pk_d = din("peer_k", [2, 8, 128, 128])
        pu_d = din("peer_u", [16384, D])
        pvv_d = din("peer_v", [16384, D])
        mix = h1T[:].rearrange("p a (b c) -> p (a b) c", c=1024)
        h2T = mT
        with contextlib.ExitStack() as PS_:
            G1S = contextlib.ExitStack()
            g1_bc = sb(G1S, "g1_bc", [128, D], F32)

            def build_bc(stack, items):
                r_dg = Rot(stack, "diag", [128, 128], F32, 2)
                for dst, src, dk in items:
                    for dc in range(8):
                        dg, dgk = r_dg.next()
                        kb.op("dve", lambda e, dg=dg, src=src, dc=dc: e.tensor_scalar(out=dg[:], in0=ident_f[:], scalar1=src[:, dc:dc + 1],
                                                                                     scalar2=None, op0=ALU.mult),
                              reads=["ident_f", "mod", "gf"], writes=[dgk])
                        p = ps_next()
                        kb.op("pe", lambda e, p=p, dg=dg: e.matmul(psb[p][:, 0:128], lhsT=ones_f[:, 0:128], rhs=dg[:], start=True, stop=True),
                              reads=["ones_f", dgk], writes=[("ps", p)])
                        evac_copy(dst[:, dc * 128:(dc + 1) * 128], psb[p][:, 0:128], [("ps", p)], [dk])

            with contextlib.ExitStack() as MS:
                build_bc(MS, [(g1_bc, mod[:, 16:24], "g1_bc")])
                wout = sb(MS, "wout", [128, 8, D], BF16)
                for hf in range(2):
                    kb.dma("pool", wout[:, :, hf * 512:(hf + 1) * 512],
                           wout_d[:, hf * 512:(hf + 1) * 512].rearrange("(kc p) n -> p kc n", p=128), writes=["wout"])
                for i in range(NT):
                    for dd in range(2):
                        p = ps_next()
                        for kc in range(8):
                            kb.op("pe", lambda e, p=p, kc=kc, i=i, dd=dd: e.matmul(
                                psb[p][:, :], lhsT=mT[:, kc, i * 128:(i + 1) * 128], rhs=wout[:, kc, dd * 512:(dd + 1) * 512],
                                start=(kc == 0), stop=(kc == 7)), reads=["wout"] + [("mT", j) for j in range(8)], writes=[("ps", p)])
                        evac_copy(mix[:, i, dd * 512:(dd + 1) * 512], psb[p][:, :], [("ps", p)], ["mix"])
                kb.barrier()
            if "mix" in dbg:
                kb.dma("sp", dbg_out("mix", [128, NT, D], BF16), mix, reads=["mix"])
            kstop("mix")
            with contextlib.ExitStack() as N2:
                r_x1 = Rot(N2, "x1tmp", [128, D], F32, 2)

                def load_x1(i, dst, key):
                    kb.dma("sp", dst[:], x_d[i * 128:(i + 1) * 128, :], writes=[key])
                    tmp, tk = r_x1.next()
                    kb.op("dve", lambda e: e.tensor_tensor(out=tmp[:], in0=mix[:, i, :], in1=g1_bc[:], op=ALU.mult),
                          reads=["mix", "g1_bc"], writes=[tk])
                    kb.op("pool", lambda e: e.tensor_tensor(out=dst[:], in0=dst[:], in1=tmp[:], op=ALU.add),
                          reads=[key, tk], writes=[key])

                norm_mod_T("n2", load_x1, gs2, 24, h2T, "h2T")
            if "h2T" in dbg:
                kb.dma("sp", dbg_out("h2T", [128, 8, S], BF16), h2T[:], reads=["h2T"])
            kstop("h2T")
            kb.barrier()
            G1S.close()
            with contextlib.ExitStack() as PE_:
                k1T = sb(PE_, "k1T", [128, 8, 128], BF16)
                k2T = sb(PE_, "k2T", [128, 8, 128], BF16)
                with contextlib.ExitStack() as KS:
                    kst = sb(KS, "kst", [128, 2, 8, 128], BF16)
                    kb.dma("pool", kst[:], pk_d.rearrange("w h n d -> n w h d"), writes=["kst"])
                    for wi, dst, dk in ((0, k1T, "k1T"), (1, k2T, "k2T")):
                        p = ps_next()
                        pv = psb[p][:].bitcast(BF16)
                        for h in range(8):
                            kb.op("pe", lambda e, pv=pv, h=h, wi=wi: e.transpose(out=pv[:, h * 128:(h + 1) * 128], in_=kst[:, wi, h, :],
                                                                              identity=ident_b[:]), reads=["kst", "ident_b"], writes=[("ps", p)])
                        evac_copy(dst[:], pv[:, 0:1024].rearrange("p (h n) -> p h n", n=128), [("ps", p)], [dk])
                    kb.barrier()
                peer_acc = sb(PE_, "peer_acc", [128, 8, D], F32)
                qT = sb(PE_, "qT", [128, 16, 1024], BF16)
                thr_all = sb(PE_, "thr_all", [128, 8, 8], F32)
                nb_all = sb(PE_, "nb_all", [128, 8, 8], F32)
                Dk = sb(PE_, "Dk", [128, 8, 8, 128], BF16)
                kap = sb(PE_, "kap", [128, 8, 8], F32)
                for hf in range(2):
                    T0 = hf * 8
                    tok0 = T0 * 128
                    QS = contextlib.ExitStack()
                    wqb = Rot(QS, "wqb", [128, 8, 512], BF16, 2)
                    for blk in range(4):
                        w, wk = wqb.next()
                        load_w(w, wk, wq_d[:, blk * 512:(blk + 1) * 512], 512)
                        for mt in range(4):
                            for tc in range(2):
                                p = ps_next()
                                proj_fm(w, wk, mt * 128, 128, h2T, ["h2T"], tok0 + tc * 512, 512, psb[p][:, :], ("ps", p))
                                evac_copy(qT[:, blk * 4 + mt, tc * 512:(tc + 1) * 512], psb[p][:, :], [("ps", p)], [("qT", blk * 4 + mt)])
                    kb.barrier()
                    QS.close()
                    with contextlib.ExitStack() as PR:
                        v12 = sb(PR, "v12", [128, 2, 8, 16], F32)
                        cand = sb(PR, "cand", [128, 8, 16, 16], F32)
                        ctop = sb(PR, "ctop", [128, 8, 24], F32)
                        r_mr = Rot(PR, "mr", [128, 256], F32, 2)
                        sm = sb(PR, "sm", [128, 8, 16], F32)
                        zz = sb(PR, "zz", [128, 4, 8], F32)
                        for il in range(8):
                            ts = slice(il * 128, (il + 1) * 128)
                            for wi, kT_ in ((0, k1T), (1, k2T)):
                                for h in range(8):
                                    p = ps_next()
                                    kb.op("pe", lambda e, p=p, h=h, wi=wi, kT_=kT_: e.matmul(
                                        psb[p][:, 0:128], lhsT=qT[:, h * 2 + wi, ts], rhs=kT_[:, h, :], start=True, stop=True),
                                        reads=[("qT", h * 2 + wi), "k1T", "k2T"], writes=[("ps", p)])
                                    mr, mrk = r_mr.next()
                                    kb.op("dve", lambda e, p=p, h=h, wi=wi: e.max(out=v12[:, wi, h, 0:8], in_=psb[p][:, 0:128]),
                                          reads=[("ps", p)], writes=["v12"])
                                    kb.op("dve", lambda e, p=p, h=h, wi=wi, mr=mr: e.match_replace(
                                        out=mr[:, 0:128], in_to_replace=v12[:, wi, h, 0:8], in_values=psb[p][:, 0:128], imm_value=-1e30),
                                        reads=[("ps", p), "v12"], writes=[mrk])
                                    kb.op("dve", lambda e, h=h, wi=wi, mr=mr: e.max(out=v12[:, wi, h, 8:16], in_=mr[:, 0:128]),
                                          reads=[mrk], writes=["v12"])
                            kb.op("dve", lambda e: e.tensor_tensor(
                                out=cand[:], in0=v12[:, 0, :, :].unsqueeze(3).broadcast_to([128, 8, 16, 16]),
                                in1=v12[:, 1, :, :].unsqueeze(2).broadcast_to([128, 8, 16, 16]), op=ALU.add),
                                reads=["v12"], writes=["cand"])
                            for h in range(8):
                                ch = cand[:, h, :, :].rearrange("p a b -> p (a b)")
                                mr, mrk = r_mr.next()
                                mr2, mr2k = r_mr.next()
                                kb.op("dve", lambda e, h=h, ch=ch: e.max(out=ctop[:, h, 0:8], in_=ch), reads=["cand"], writes=["ctop"])
                                kb.op("dve", lambda e, h=h, ch=ch, mr=mr: e.match_replace(out=mr[:], in_to_replace=ctop[:, h, 0:8], in_values=ch,
                                                                                        imm_value=-1e30), reads=["cand", "ctop"], writes=[mrk])
                                kb.op("dve", lambda e, h=h, mr=mr: e.max(out=ctop[:, h, 8:16], in_=mr[:]), reads=[mrk], writes=["ctop"])
                                kb.op("dve", lambda e, h=h, mr=mr, mr2=mr2: e.match_replace(out=mr2[:], in_to_replace=ctop[:, h, 8:16], in_values=mr[:],
                                                                                          imm_value=-1e30), reads=[mrk, "ctop"], writes=[mr2k])
                                kb.op("dve", lambda e, h=h, mr2=mr2: e.max(out=ctop[:, h, 16:24], in_=mr2[:]), reads=[mr2k], writes=["ctop"])
                            kb.op("dve", lambda e: e.tensor_tensor(out=zz[:, 0, :], in0=ctop[:, :, 15], in1=ctop[:, :, 16], op=ALU.add),
                                  reads=["ctop"], writes=["zz"])
                            kb.op("dve", lambda e, il=il: e.tensor_scalar(out=thr_all[:, il, :], in0=zz[:, 0, :], scalar1=0.5, scalar2=None, op0=ALU.mult),
                                  reads=["zz"], writes=["thr_all"])
                            kb.op("dve", lambda e: e.tensor_tensor(out=sm[:], in0=ctop[:, :, 0:16],
                                                                   in1=ctop[:, :, 0:1].broadcast_to([128, 8, 16]), op=ALU.subtract),
                                  reads=["ctop"], writes=["sm"])
                            kb.op("act", lambda e: e.activation(out=sm[:], in_=sm[:], func=AF.Exp), reads=["sm"], writes=["sm"])
                            kb.op("dve", lambda e: e.tensor_reduce(out=zz[:, 1, :], in_=sm[:], axis=mybir.AxisListType.X, op=ALU.add),
                                  reads=["sm"], writes=["zz"])
                            kb.op("act", lambda e: e.activation(out=zz[:, 2, :], in_=zz[:, 1, :], func=AF.Ln), reads=["zz"], writes=["zz"])
                            kb.op("dve", lambda e: e.tensor_tensor(out=zz[:, 3, :], in0=zz[:, 2, :], in1=ctop[:, :, 0], op=ALU.add),
                                  reads=["zz", "ctop"], writes=["zz"])
                            kb.op("dve", lambda e, il=il: e.tensor_scalar(out=nb_all[:, il, :], in0=zz[:, 3, :], scalar1=-1.0, scalar2=None, op0=ALU.mult),
                                  reads=["zz"], writes=["nb_all"])
                            kb.op("dve", lambda e, il=il: e.tensor_tensor(out=kap[:, il, :], in0=thr_all[:, il, :], in1=nb_all[:, il, :], op=ALU.add),
                                  reads=["thr_all", "nb_all"], writes=["kap"])
                            kb.op("act", lambda e, il=il: e.activation(out=kap[:, il, :], in_=kap[:, il, :], func=AF.Exp), reads=["kap"], writes=["kap"])
                            for h in range(8):
                                kb.op("dve", lambda e, il=il, h=h: e.tensor_scalar(out=Dk[:, il, h, :], in0=ident_b[:], scalar1=kap[:, il, h:h + 1],
                                                                                   scalar2=None, op0=ALU.mult),
                                      reads=["ident_b", "kap"], writes=["Dk"])
                        kb.barrier()
                    EL = contextlib.ExitStack()
                    uT = Rot(EL, "uT", [128, 8, 512], BF16, 1)
                    ust = Rot(EL, "ust", [128, D], BF16, 2)
                    vbk = Rot(EL, "vbk", [128, 4, D], BF16, 1)
                    r_s1 = Rot(EL, "s1sb", [128, 8, 4], F32, 2)
                    r_aT4 = Rot(EL, "aT4", [128, 4, 512], BF16, 2)
                    sh_rr = [0]

                    def sh_next():
                        i = sh_rr[0]
                        sh_rr[0] = (i + 1) % 2
                        return 4 + i
                    r_sum = Rot(EL, "sum", [128, 4, 4, 128], BF16, 2)
                    r_E = Rot(EL, "E", [128, 4, 4, 128], BF16, 2)
                    r_G = Rot(EL, "G", [128, 4, 4, 128], BF16, 2)
                    r_wT = Rot(EL, "wT", [128, 4, 128], BF16, 2)
                    kb.op("pool", lambda e: e.memset(peer_acc[:], 0.0), writes=["peer_acc"])
                    import os
                    NBLK = int(os.environ.get("KNBLK", "32"))
                    ut, utk = uT.next()

                    def emit_u_dma(b, c):
                        e0 = b * 512
                        us, usk = ust.next()
                        kb.dma("pool", us[:], pu_d[e0 + c * 128:e0 + (c + 1) * 128, :], writes=[usk])
                        return us, usk

                    def emit_u_tr(c, us, usk):
                        p = sh_next()
                        pv = psb[p][:].bitcast(BF16)
                        for kc in range(8):
                            kb.op("pe", lambda e, pv=pv, kc=kc: e.transpose(out=pv[:, kc * 128:(kc + 1) * 128],
                                                                          in_=us[:, kc * 128:(kc + 1) * 128], identity=ident_b[:]),
                                  reads=[usk, "ident_b"], writes=[("ps", p)])
                        kb.op("act", lambda e: e.copy(out=ut[:, :, c * 128:(c + 1) * 128], in_=pv[:, 0:1024].rearrange("p (k e) -> p k e", e=128)),
                              reads=[("ps", p)], writes=[(utk, c)])

                    def emit_v(b):
                        e0 = b * 512
                        vb, vbk_ = vbk.next()
                        kb.dma("pool", vb[:], pvv_d[e0:e0 + 512, :].rearrange("(c p) d -> p c d", p=128), writes=[vbk_])
                        return vb, vbk_

                    def emit_aT4_chunk_pe(aT, g4, c, bank):
                        aT4, aT4k = aT
                        for kc in range(8):
                            kb.op("pe", lambda e, kc=kc: e.matmul(
                                psb[bank][:, :], lhsT=ut[:, kc, c * 128:(c + 1) * 128],
                                rhs=h2T[:, kc, tok0 + g4 * 512:tok0 + (g4 + 1) * 512], start=(kc == 0), stop=(kc == 7)),
                                reads=[(utk, c), "h2T"], writes=[("ps", bank)])

                        def gelu():
                            kb.op("act", lambda e: e.activation(out=aT4[:, c, :], in_=psb[bank][:, :], func=AF.Gelu),
                                  reads=[("ps", bank)], writes=[(aT4k, c)])
                        return gelu

                    class Job:
                        pass

                    def make_job(b, il, aT4, aT4k, vb, vbk_):
                        J = Job()
                        ts = slice(il * 128, (il + 1) * 128)
                        q = il % 4
                        st = {}

                        def A_pe():
                            p1 = sh_next()
                            st["p1"] = p1
                            for h in range(8):
                                kb.op("pe", lambda e, h=h: e.matmul(psb[p1][:, h * 4:(h + 1) * 4], lhsT=qT[:, h * 2, ts],
                                                                    rhs=k1T[:, h, 4 * b:4 * b + 4], start=True, stop=True),
                                      reads=[("qT", h * 2), "k1T"], writes=[("ps", p1)])
                            p2s = [(il % 2) * 2, (il % 2) * 2 + 1]
                            st["p2s"] = p2s
                            for hg in range(2):
                                for hh in range(4):
                                    h = hg * 4 + hh
                                    kb.op("pe", lambda e, h=h, hh=hh, hg=hg: e.matmul(psb[p2s[hg]][:, hh * 128:(hh + 1) * 128], lhsT=qT[:, h * 2 + 1, ts],
                                                                                    rhs=k2T[:, h, :], start=True, stop=True),
                                          reads=[("qT", h * 2 + 1), "k2T"], writes=[("ps", p2s[hg])])

                        def A_dve():
                            p1 = st["p1"]
                            s1, s1k = r_s1.next()
                            st["s1"] = (s1, s1k)
                            kb.op("dve", lambda e: e.tensor_tensor(
                                out=s1[:], in0=psb[p1][:, 0:32].rearrange("p (h i) -> p h i", i=4),
                                in1=thr_all[:, il, :].unsqueeze(2).broadcast_to([128, 8, 4]), op=ALU.subtract),
                                reads=[("ps", p1), "thr_all"], writes=[s1k])

                        def B1_sums():
                            s1, s1k = st["s1"]
                            sms = []
                            for hg in range(2):
                                p2 = st["p2s"][hg]
                                sm_, smk = r_sum.next()
                                kb.op("dve", lambda e: e.tensor_tensor(
                                    out=sm_[:], in0=s1[:, hg * 4:(hg + 1) * 4, :].unsqueeze(3).broadcast_to([128, 4, 4, 128]),
                                    in1=psb[p2][:, :].rearrange("p (h j) -> p h j", j=128).unsqueeze(2).broadcast_to([128, 4, 4, 128]), op=ALU.add),
                                    reads=[s1k, ("ps", p2)], writes=[smk])
                                sms.append((sm_, smk))
                            Es = []
                            for hg in range(2):
                                sm_, smk = sms[hg]
                                E_, Ek = r_E.next()
                                kb.op("act", lambda e: e.activation(out=E_[:], in_=sm_[:], func=AF.Exp), reads=[smk], writes=[Ek])
                                Es.append((E_, Ek))
                            st["sms"] = sms
                            st["Es"] = Es

                        def B1_stt():
                            Gs = []
                            for hg in range(2):
                                sm_, smk = st["sms"][hg]
                                E_, Ek = st["Es"][hg]
                                G_, Gk = r_G.next()
                                kb.op("dve", lambda e: e.scalar_tensor_tensor(out=G_[:].rearrange("p a b c -> p (a b) c"),
                                                                              in0=sm_[:].rearrange("p a b c -> p (a b) c"), scalar=0.0,
                                                                              in1=E_[:].rearrange("p a b c -> p (a b) c"),
                                                                              op0=ALU.is_gt, op1=ALU.mult),
                                      reads=[smk, Ek], writes=[Gk])
                                Gs.append((G_, Gk))
                            st["Gs"] = Gs

                        def B1_gt():
                            pg = ps_next_acc()
                            st["pg"] = pg
                            for hg in range(2):
                                G_, Gk = st["Gs"][hg]
                                for hh in range(4):
                                    h = hg * 4 + hh
                                    for c in range(4):
                                        kb.op("pe", lambda e, c=c, h=h, hh=hh: e.matmul(psb[pg][:, c * 128:(c + 1) * 128], lhsT=G_[:, hh, c, :],
                                                                                      rhs=Dk[:, il, h, :], start=(h == 0 and c == 0),
                                                                                      stop=True, skip_group_check=True),
                                              reads=[Gk, "Dk"], writes=[("ps", pg)])

                        def B2_wT():
                            pg = st["pg"]
                            wT, wTk = r_wT.next()
                            st["wT"] = (wT, wTk)
                            kb.op("dve", lambda e: e.tensor_tensor(out=wT[:], in0=aT4[:, :, q * 128:(q + 1) * 128],
                                                                   in1=psb[pg][:, :].rearrange("p (c t) -> p c t", t=128),
                                                                   op=ALU.mult), reads=[(aT4k, cc) for cc in range(4)] + [("ps", pg)], writes=[wTk])

                        def B2_peer_acc():
                            wT, wTk = st["wT"]
                            pps = []
                            for dd in range(2):
                                pp = sh_next()
                                pps.append(pp)
                                for c in range(4):
                                    kb.op("pe", lambda e, c=c, dd=dd, pp=pp: e.matmul(psb[pp][:, :], lhsT=wT[:, c, :], rhs=vb[:, c, dd * 512:(dd + 1) * 512],
                                                                                    start=(c == 0), stop=(c == 3)),
                                          reads=[wTk, vbk_], writes=[("ps", pp)])
                            for dd in range(2):
                                pp = pps[dd]
                                kb.op("dve", lambda e, dd=dd, pp=pp: e.tensor_tensor(out=peer_acc[:, il, dd * 512:(dd + 1) * 512],
                                                                                     in0=peer_acc[:, il, dd * 512:(dd + 1) * 512], in1=psb[pp][:, :],
                                                                                     op=ALU.add), reads=[("pacc", il), ("ps", pp), "peer_acc"],
                                      writes=[("pacc", il)])

                        J.A_pe, J.A_dve, J.B1_sums, J.B1_stt, J.B1_gt, J.B2_wT, J.B2_peer_acc = A_pe, A_dve, B1_sums, B1_stt, B1_gt, B2_wT, B2_peer_acc
                        return J

                    for c in range(4):
                        us, usk = emit_u_dma(0, c)
                        emit_u_tr(c, us, usk)
                    vb, vbk_ = emit_v(0)
                    aT_cur = [r_aT4.next(), None]
                    for c in range(4):
                        emit_aT4_chunk_pe(aT_cur[0], 0, c, sh_next())()
                    aT_next = None
                    jobs = {}
                    NSTEP = NBLK * 8
                    u_pend = {}
                    pend_gelu = []
                    for n in range(NSTEP + 3):
                        b, q8 = divmod(n, 8)
                        if n < NSTEP:
                            if q8 == 1:
                                aT_cur[1] = r_aT4.next()
                            if q8 == 0 and b > 0:
                                aT_cur[0] = aT_next
                            aT4, aT4k = aT_cur[q8 // 4]
                            jobs[n] = make_job(b, q8, aT4, aT4k, vb, vbk_)
                        if (n - 1) in jobs:
                            jobs[n - 1].B1_sums()
                        while pend_gelu:
                            pend_gelu.pop(0)()
                        if (n - 3) in jobs:
                            jobs[n - 3].B2_peer_acc()
                            del jobs[n - 3]
                        if n < NSTEP and q8 == 2 and b > 0:
                            vb, vbk_ = emit_v(b)
                        if n < NSTEP:
                            jobs[n].A_pe()
                            jobs[n].A_dve()
                        if (n - 2) in jobs:
                            jobs[n - 2].B2_wT()
                        if n < NSTEP:
                            bank = 6 + (1 - ps_acc[0]) if (n - 1) in jobs else 6 + ps_acc[0]
                            if 1 <= q8 <= 4:
                                pend_gelu.append(emit_aT4_chunk_pe(aT_cur[1], 1, q8 - 1, bank))
                            elif b + 1 < NBLK and q8 >= 5:
                                if q8 == 5:
                                    aT_next = r_aT4.next()
                                pend_gelu.append(emit_aT4_chunk_pe(aT_next, 0, q8 - 5, bank))
                            elif b > 0 and q8 == 0:
                                pend_gelu.append(emit_aT4_chunk_pe(aT_cur[0], 0, 3, bank))
                            if b + 1 < NBLK:
                                if 3 <= q8 <= 6:
                                    emit_u_tr(q8 - 3, *u_pend[q8 - 3])
                                if 1 <= q8 <= 4:
                                    u_pend[q8 - 1] = emit_u_dma(b + 1, q8 - 1)
                        if (n - 1) in jobs:
                            jobs[n - 1].B1_stt()
                            jobs[n - 1].B1_gt()
                    while pend_gelu:
                        pend_gelu.pop(0)()
                    if "peer" in dbg:
                        kb.dma("sp", dbg_out("peer%d" % hf, [128, 8, D]), peer_acc[:], reads=[("pacc", il) for il in range(8)] + ["peer_acc"])
                    kb.barrier()
                    EL.close()
                    with contextlib.ExitStack() as EP:
                        g1_bc = sb(EP, "g1_bc", [128, D], F32)
                        g2_bc = sb(EP, "g2_bc", [128, D], F32)
                        gf_bc = sb(EP, "gf_bc", [128, D], F32)
                        build_bc(EP, [(g1_bc, mod[:, 16:24], "g1_bc"), (g2_bc, mod[:, 40:48], "g2_bc"), (gf_bc, gf[:, 0:8], "gf_bc")])
                        r_x2 = Rot(EP, "x2", [128, D], F32, 2)
                        r_t3 = Rot(EP, "x2u", [128, D], F32, 2)
                        r_t2 = Rot(EP, "x2t", [128, D], F32, 2)
                        r_jk = Rot(EP, "jk", [128, D], BF16, 1)
                        r_st = Rot(EP, "st", [128, 4], F32, 2)
                        for il in range(8):
                            i = T0 + il
                            x2, x2k = r_x2.next()
                            t2, t2k = r_t2.next()
                            st, stk = r_st.next()
                            jk, jkk = r_jk.next()
                            kb.dma("sp", x2[:], x_d[i * 128:(i + 1) * 128, :], writes=[x2k])
                            kb.op("dve", lambda e: e.tensor_tensor(out=t2[:], in0=mix[:, i, :], in1=g1_bc[:], op=ALU.mult),
                                  reads=["mix", "g1_bc"], writes=[t2k])
                            t3, t3k = r_t3.next()
                            kb.op("pool", lambda e: e.tensor_tensor(out=t3[:], in0=peer_acc[:, il, :], in1=g2_bc[:], op=ALU.mult),
                                  reads=[("pacc", il), "peer_acc", "g2_bc"], writes=[t3k])
                            kb.op("pool", lambda e: e.tensor_tensor(out=t2[:], in0=t2[:], in1=t3[:], op=ALU.add),
                                  reads=[t2k, t3k], writes=[t2k])
                            kb.op("pool", lambda e: e.tensor_tensor(out=x2[:], in0=x2[:], in1=t2[:], op=ALU.add), reads=[x2k, t2k], writes=[x2k])
                            kb.op("act", lambda e: e.activation(out=jk[:], in_=x2[:], func=AF.Square, accum_out=st[:, 0:1]),
                                  reads=[x2k], writes=[stk, jkk])
                            kb.op("dve", lambda e: e.tensor_scalar(out=st[:, 1:2], in0=st[:, 0:1], scalar1=1.0 / D, scalar2=EPS,
                                                                   op0=ALU.mult, op1=ALU.add), reads=[stk], writes=[stk])
                            kb.op("act", lambda e: e.activation(out=st[:, 2:3], in_=st[:, 1:2], func=AF.Sqrt), reads=[stk], writes=[stk])
                            kb.op("dve", lambda e: e.reciprocal(out=st[:, 3:4], in_=st[:, 2:3]), reads=[stk], writes=[stk])
                            kb.op("dve", lambda e: e.scalar_tensor_tensor(out=t2[:], in0=x2[:], scalar=st[:, 3:4], in1=gf_bc[:],
                                                                          op0=ALU.mult, op1=ALU.mult), reads=[x2k, stk, "gf_bc"], writes=[t2k])
                            OUT_TOKS.append(kb.dma("sp", out_d[i * 128:(i + 1) * 128, :], t2[:], reads=[t2k]))
                        kb.barrier()
                kb.barrier()

    try:
        stages()
    except StopBuild:
        pass
    toks = []
    if "mod" in dbg:
        toks.append(kb.dma("sp", dbg_out("mod", [128, 48]), mod[:], reads=["mod"]))
    if "h1T" in dbg:
        o = dbg_out("h1T", [128, 8, S], BF16)
        toks.append(kb.dma("sp", o, h1T[:], reads=["h1T"]))
    kb.barrier()
    for e in ("sp",):
        for q in kb.dsem:
            for i in range(kb.nslot):
                if kb.dval[q][i] > 0:
                    kb._wait(e, (kb.dsem[q][i], kb.dval[q][i], "dma"))
    return kb


_CONST_CACHE = {}


def nsa_consts(inputs, b):
    f = np.float32
    m = {}
    m["pos"] = np.ascontiguousarray(np.asarray(inputs["positions"][b], dtype=np.int32).reshape(1, S))
    w_in = np.asarray(inputs["w_in"][0], dtype=f)
    rot = (np.arange(16) + 8) % 16
    m["w_qrot"] = np.ascontiguousarray(np.concatenate([w_in[:, 3088 + hd * 64 + rot] for hd in range(16)], axis=1))
    cols = []
    for base in (4112, 4368, 4624):
        for g in range(2):
            cols.append(w_in[:, base + g * 64 + rot])
    m["w_krot"] = np.ascontiguousarray(np.concatenate(cols, axis=1))
    w1 = np.stack([np.asarray(inputs["nsa_ck_w1"][0], dtype=f), np.asarray(inputs["nsa_cv_w1"][0], dtype=f)])
    m["cmp_w1"] = np.ascontiguousarray(w1.reshape(2, 32, 64, 64).transpose(0, 2, 1, 3).reshape(2, 64, 2048))
    m["cmp_w2"] = np.ascontiguousarray(np.stack([np.asarray(inputs["nsa_ck_w2"][0], dtype=f), np.asarray(inputs["nsa_cv_w2"][0], dtype=f)]))
    pe = np.stack([np.asarray(inputs["nsa_pe_k"][0], dtype=f), np.asarray(inputs["nsa_pe_v"][0], dtype=f)])
    m["cmp_peT"] = np.ascontiguousarray(pe.transpose(0, 2, 1))
    m["w_branch_b"] = np.ascontiguousarray(inputs["w_branch_b"][0], dtype=f)
    if "c" not in _CONST_CACHE:
        c = {}
        inv = (500000.0 ** (-np.arange(8, dtype=np.float32) * 2.0 / 16)).astype(f)
        rc = np.zeros((16, 4), dtype=f)
        rc[:, 0] = np.concatenate([inv, inv])
        rc[:, 1] = np.pi / 2
        rc[0:8, 2] = np.pi
        rc[8:16, 2] = 0.0
        c["rope_c"] = rc
        n = np.arange(127)[:, None]
        j = np.arange(32)[None, :]
        c["ovl"] = ((n * 16 < (j + 1) * 64) & (n * 16 + 32 > j * 64)).astype(f)
        t = np.arange(S)[None, :]
        c["cmpbias"] = np.where(n * 16 + 31 <= t, 0.0, -30000.0).astype(f)
        k = np.arange(S)[None, :]
        c["blockind"] = (k // 64 == np.arange(32)[:, None]).astype(f)
        kk = np.arange(128)[:, None]
        tt = np.arange(128)[None, :]
        c["causalbias"] = np.where(kk > tt, -30000.0, 0.0).astype(f)
        c["antibias"] = np.where(kk <= tt, -30000.0, 0.0).astype(f)
        tok = (np.arange(NT)[None, :, None] * 128 + np.arange(128)[:, None, None])
        cur = tok // 64
        jj = np.arange(32)[None, None, :]
        c["Msel"] = ((jj >= 1) & (jj <= cur - 2)).astype(f)
        c["Fsel"] = ((jj == 0) | (jj == cur) | (jj == cur - 1)).astype(f)
        _CONST_CACHE["c"] = c
    m.update(_CONST_CACHE["c"])
    return m


def host_layout(inputs, b):
    f = np.float32
    m = {}
    m["x"] = np.ascontiguousarray(inputs["x"][b], dtype=f)
    m["c_l"] = np.ascontiguousarray(np.asarray(inputs["c"][b], dtype=f).reshape(8, 128).T)
    m["ada_w"] = np.ascontiguousarray(inputs["ada_w"][0], dtype=f)
    m["ada_b_l"] = np.ascontiguousarray(np.asarray(inputs["ada_b"][0], dtype=f).reshape(48, 128).T)
    m["g1_l"] = np.ascontiguousarray(np.asarray(inputs["norm1_g"][0], dtype=f).reshape(8, 128).T)
    m["g2_l"] = np.ascontiguousarray(np.asarray(inputs["norm2_g"][0], dtype=f).reshape(8, 128).T)
    m["gf_l"] = np.ascontiguousarray(np.asarray(inputs["final_g"], dtype=f).reshape(8, 128).T)
    m["ident"] = np.eye(128, dtype=f)
    m["LT"] = np.triu(np.ones((128, 128), dtype=f))
    m["SU"] = np.tril(np.ones((128, 128), dtype=f), -1)
    m["w_in"] = np.ascontiguousarray(inputs["w_in"][0], dtype=f)
    m["wa2"] = np.ascontiguousarray(inputs["gla_wa2"][0], dtype=f)
    m["ba2"] = np.ascontiguousarray(np.asarray(inputs["gla_ba2"][0], dtype=f).reshape(1, 512))
    m["gng_l"] = np.ascontiguousarray(np.asarray(inputs["gla_norm_g"][0], dtype=f).reshape(2, 128).T)
    m.update(nsa_consts(inputs, b))
    m["w_out"] = np.ascontiguousarray(inputs["w_out"][0], dtype=f)
    m["peer_wq"] = np.ascontiguousarray(inputs["peer_wq"][0], dtype=f)
    m["peer_k"] = np.ascontiguousarray(np.stack([np.asarray(inputs["peer_k1"][0], dtype=f), np.asarray(inputs["peer_k2"][0], dtype=f)]))
    m["peer_u"] = np.ascontiguousarray(inputs["peer_u"][0], dtype=f)
    m["peer_v"] = np.ascontiguousarray(inputs["peer_v"][0], dtype=f)
    m["w_branch_a"] = np.ascontiguousarray(inputs["w_branch_a"][0], dtype=f)
    return m


def run(inputs, dbg=(), ncores=8):
    kb = build(dbg)
    in_maps = [host_layout(inputs, b) for b in range(ncores)]
    res = run_bass_kernel_spmd(kb.nc, in_maps, core_ids=list(range(ncores)))
    return res.results


def kernel(**inputs):
    res = run(inputs)
    return np.stack([np.asarray(r["out"], dtype=np.float32) for r in res], axis=0)
```

```python
import contextlib
import numpy as np
import concourse.bass as bass
import concourse.mybir as mybir
from concourse.alu_op_type import AluOpType as ALU
from concourse.bass_utils import run_bass_kernel_spmd

AF = mybir.ActivationFunctionType
F32 = mybir.dt.float32
BF16 = mybir.dt.bfloat16
I32 = mybir.dt.int32
U32 = mybir.dt.uint32

S = 2048
D = 1024
NT = S // 128
EPS = 1e-6


class StopBuild(Exception):
    pass


def kstop(name):
    import os
    if os.environ.get('KSTOP', '') == name:
        raise StopBuild()


class KB:
    def __init__(self):
        self.nc = bass.Bass("TRN2", target_bir_lowering=False)
        nc = self.nc
        self.es = contextlib.ExitStack()
        self.eng = dict(pe=nc.tensor, act=nc.scalar, dve=nc.vector, pool=nc.gpsimd, sp=nc.sync)
        self.sem = {e: self.es.enter_context(nc.semaphore("s_" + e)) for e in self.eng}
        self.cnt = {e: 0 for e in self.eng}
        self.nslot = 10
        self.dsem = {q: [self.es.enter_context(nc.semaphore("d_%s%d" % (q, i))) for i in range(self.nslot)]
                     for q in ("sp", "pool", "act")}
        self.dval = {q: [0] * self.nslot for q in self.dsem}
        self.dnext = {q: 0 for q in self.dsem}
        self.waited = {}
        self.lastw = {}
        self.readers = {}
        self.ninst = 0

    def _wait(self, e, tok):
        sem, val, prod = tok
        if prod == "pe" and e == "pe":
            return
        key = (e, id(sem))
        if self.waited.get(key, 0) >= val:
            return
        self.eng[e].wait_ge(sem, val)
        self.waited[key] = val

    def _deps(self, e, reads, writes):
        for k in list(reads) + list(writes):
            t = self.lastw.get(k)
            if t is not None:
                self._wait(e, t)
        for k in reads:
            if isinstance(k, tuple) and k[0] == "ps":
                for t in self.readers.get(k, {}).values():
                    if t[2] != e:
                        self._wait(e, t)
        for k in writes:
            for t in self.readers.get(k, {}).values():
                self._wait(e, t)

    def _record(self, tok, reads, writes):
        for k in writes:
            self.lastw[k] = tok
            self.readers[k] = {}
        for k in reads:
            self.readers.setdefault(k, {})[id(tok[0])] = tok

    def op(self, e, fn, reads=(), writes=()):
        self._deps(e, reads, writes)
        inst = fn(self.eng[e])
        self.cnt[e] += 1
        inst.then_inc(self.sem[e], 1)
        tok = (self.sem[e], self.cnt[e], e)
        self._record(tok, reads, writes)
        self.ninst += 1
        return inst

    def dma(self, q, out, in_, reads=(), writes=(), **kw):
        slot = self.dnext[q]
        self.dnext[q] = (slot + 1) % self.nslot
        sem = self.dsem[q][slot]
        if self.dval[q][slot] > 0:
            self._wait(q, (sem, self.dval[q][slot], "dma"))
        self._deps(q, reads, writes)
        inst = self.eng[q].dma_start(out=out, in_=in_, **kw)
        self.dval[q][slot] += 16
        inst.then_inc(sem, 16)
        tok = (sem, self.dval[q][slot], "dma")
        self._record(tok, reads, writes)
        self.ninst += 1
        return tok

    def barrier(self):
        toks = [(self.sem[e], self.cnt[e], e) for e in self.eng if self.cnt[e] > 0]
        for q in self.dsem:
            for i in range(self.nslot):
                if self.dval[q][i] > 0:
                    toks.append((self.dsem[q][i], self.dval[q][i], "dma"))
        for e in self.eng:
            for t in toks:
                self._wait(e, t)
        self.lastw.clear()
        self.readers.clear()

    def final_wait(self, toks):
        for t in toks:
            self._wait("sp", t)


def build(dbg=()):
    kb = KB()
    nc = kb.nc
    es = kb.es
    dbg = set(dbg)

    def din(name, shape, dt=F32):
        return nc.dram_tensor(name, list(shape), dt, kind="ExternalInput").ap()

    def dout(name, shape, dt=F32):
        return nc.dram_tensor(name, list(shape), dt, kind="ExternalOutput").ap()

    uid = [0]

    def sb(stack, name, shape, dt):
        uid[0] += 1
        return stack.enter_context(nc.sbuf_tensor("sb%d_%s" % (uid[0], name), list(shape), dt))

    x_d = din("x", [S, D])
    c_d = din("c_l", [128, 8])
    adaw_d = din("ada_w", [D, 6 * D])
    adab_d = din("ada_b_l", [128, 48])
    g1_d = din("g1_l", [128, 8])
    g2_d = din("g2_l", [128, 8])
    gf_d = din("gf_l", [128, 8])
    ident_d = din("ident", [128, 128])
    out_d = dout("out", [S, D])
    dbg_t = {}
    OUT_TOKS = []

    def dbg_out(name, shape, dt=F32):
        dbg_t[name] = dout("dbg_" + name, shape, dt)
        return dbg_t[name]

    psb = [es.enter_context(nc.psum_tensor("ps%d" % i, [128, 512], F32)) for i in range(8)]
    ps_rr = [0]

    def ps_next():
        i = ps_rr[0]
        ps_rr[0] = (i + 1) % 6
        return i

    ps_acc = [0]

    def ps_next_acc():
        i = ps_acc[0]
        ps_acc[0] = (i + 1) % 2
        return 6 + i

    G = es
    ident_f = sb(G, "ident_f", [128, 128], F32)
    ident_b = sb(G, "ident_b", [128, 128], BF16)
    c_sb = sb(G, "c_sb", [128, 8], F32)
    sc_sb = sb(G, "sc_sb", [128, 8], F32)
    adab_sb = sb(G, "adab_sb", [128, 48], F32)
    mod = sb(G, "mod", [128, 48], F32)
    g1 = sb(G, "g1", [128, 8], F32)
    g2 = sb(G, "g2", [128, 8], F32)
    gf = sb(G, "gf", [128, 8], F32)
    gs1 = sb(G, "gs1", [128, 8], F32)
    gs2 = sb(G, "gs2", [128, 8], F32)
    h1T = sb(G, "h1T", [128, 8, S], BF16)

    kb.dma("sp", ident_f[:], ident_d, writes=["ident_f"])
    kb.dma("sp", c_sb[:], c_d, writes=["c"])
    kb.dma("sp", adab_sb[:], adab_d, writes=["adab"])
    kb.dma("sp", g1[:], g1_d, writes=["g1"])
    kb.dma("sp", g2[:], g2_d, writes=["g2"])
    kb.dma("sp", gf[:], gf_d, writes=["gf"])
    kb.op("dve", lambda e: e.tensor_copy(out=ident_b[:], in_=ident_f[:]), reads=["ident_f"], writes=["ident_b"])

    kb.op("act", lambda e: e.activation(out=sc_sb[:], in_=c_sb[:], func=AF.Silu), reads=["c"], writes=["sc"])
    with contextlib.ExitStack() as P1:
        wbuf = [sb(P1, "adaw%d" % i, [128, 8, 512], F32) for i in range(2)]
        pm = ps_next()
        adaw_v = adaw_d.rearrange("(kc p) n -> p kc n", p=128)
        for blk in range(12):
            wb = wbuf[blk % 2]
            wk = "adaw%d" % (blk % 2)
            kb.dma("sp", wb[:], adaw_v[:, :, blk * 512:(blk + 1) * 512], writes=[wk])
            for mm in range(4):
                m = blk * 4 + mm
                for kc in range(8):
                    kb.op("pe", lambda e, m=m, mm=mm, kc=kc, wb=wb: e.matmul(
                        psb[pm][:, m:m + 1], lhsT=wb[:, kc, mm * 128:(mm + 1) * 128], rhs=sc_sb[:, kc:kc + 1],
                        start=(kc == 0), stop=(kc == 7)), reads=[wk, "sc"], writes=[("ps", pm)])
        kb.op("dve", lambda e: e.tensor_tensor(out=mod[:], in0=psb[pm][:, 0:48], in1=adab_sb[:], op=ALU.add),
              reads=[("ps", pm), "adab"], writes=["mod"])
        kb.op("dve", lambda e: e.scalar_tensor_tensor(out=gs1[:], in0=mod[:, 8:16], scalar=1.0, in1=g1[:],
                                                      op0=ALU.add, op1=ALU.mult), reads=["mod", "g1"], writes=["gs1"])
        kb.op("dve", lambda e: e.scalar_tensor_tensor(out=gs2[:], in0=mod[:, 32:40], scalar=1.0, in1=g2[:],
                                                      op0=ALU.add, op1=ALU.mult), reads=["mod", "g2"], writes=["gs2"])
        kb.barrier()

    def norm_mod_T(stack_name, src_tile_loader, gs, shift_col0, dstT, dst_key):
        with contextlib.ExitStack() as P2:
            xt = [sb(P2, stack_name + "x%d" % i, [128, D], F32) for i in range(2)]
            xn = [sb(P2, stack_name + "xn%d" % i, [128, D], BF16) for i in range(2)]
            junk = sb(P2, stack_name + "junk", [128, D], BF16)
            ss = [sb(P2, stack_name + "ss%d" % i, [128, 4], F32) for i in range(2)]
            for i in range(NT):
                b = i % 2
                kx, kn, ks = stack_name + "x%d" % b, stack_name + "xn%d" % b, stack_name + "ss%d" % b
                src_tile_loader(i, xt[b], kx)
                kb.op("act", lambda e, b=b: e.activation(out=junk[:], in_=xt[b][:], func=AF.Square,
                                                         accum_out=ss[b][:, 0:1]), reads=[kx], writes=[ks, stack_name + "junk"])
                kb.op("dve", lambda e, b=b: e.tensor_scalar(out=ss[b][:, 1:2], in0=ss[b][:, 0:1], scalar1=1.0 / D,
                                                            scalar2=EPS, op0=ALU.mult, op1=ALU.add), reads=[ks], writes=[ks])
                kb.op("act", lambda e, b=b: e.activation(out=ss[b][:, 2:3], in_=ss[b][:, 1:2], func=AF.Sqrt),
                      reads=[ks], writes=[ks])
                kb.op("dve", lambda e, b=b: e.reciprocal(out=ss[b][:, 3:4], in_=ss[b][:, 2:3]), reads=[ks], writes=[ks])
                kb.op("act", lambda e, b=b: e.activation(out=xn[b][:], in_=xt[b][:], func=AF.Identity,
                                                         scale=ss[b][:, 3:4]), reads=[kx, ks], writes=[kn])
                p = ps_next()
                pv = psb[p][:].bitcast(BF16)
                for dc in range(8):
                    kb.op("pe", lambda e, b=b, dc=dc, pv=pv: e.transpose(
                        out=pv[:, dc * 128:(dc + 1) * 128], in_=xn[b][:, dc * 128:(dc + 1) * 128], identity=ident_b[:]),
                        reads=[kn, "ident_b"], writes=[("ps", p)])
                for dc in range(8):
                    eng = "dve" if dc % 2 == 0 else "pool"
                    eng = "dve"
                    kb.op(eng, lambda e, dc=dc, pv=pv, i=i: e.tensor_scalar(
                        out=dstT[:, dc, i * 128:(i + 1) * 128], in0=pv[:, dc * 128:(dc + 1) * 128],
                        scalar1=gs[:, dc:dc + 1], scalar2=mod[:, shift_col0 + dc:shift_col0 + dc + 1],
                        op0=ALU.mult, op1=ALU.add), reads=[("ps", p), "gs1", "gs2", "mod"], writes=[dst_key])
            kb.barrier()

    def load_x(i, dst, key):
        kb.dma("sp", dst[:], x_d[i * 128:(i + 1) * 128, :], writes=[key])

    norm_mod_T("n1", load_x, gs1, 0, h1T, "h1T")

    def stages():
        LT_d = din("LT", [128, 128])
        SU_d = din("SU", [128, 128])
        win_d = din("w_in", [D, 6976])
        wa2_d = din("wa2", [16, 512])
        ba2_d = din("ba2", [1, 512])
        gng_d = din("gng_l", [128, 2])
        wba_d = din("w_branch_a", [D, D])
        LT = sb(G, "LT", [128, 128], F32)
        SU = sb(G, "SU", [128, 128], F32)
        ones_bf = sb(G, "ones_bf", [128, 128], BF16)
        ones_f = sb(G, "ones_f", [128, 128], F32)
        kb.dma("sp", LT[:], LT_d, writes=["LT"])
        kb.dma("sp", SU[:], SU_d, writes=["SU"])
        kb.op("dve", lambda e: e.memset(ones_bf[:], 1.0), writes=["ones_bf"])
        kb.op("dve", lambda e: e.memset(ones_f[:], 1.0), writes=["ones_f"])

        class Rot:
            def __init__(self, stack, name, shape, dt, n=2):
                self.t = [sb(stack, "%s_%d" % (name, i), shape, dt) for i in range(n)]
                self.k = ["%s_%d" % (name, i) for i in range(n)]
                self.i = 0

            def next(self):
                j = self.i
                self.i = (j + 1) % len(self.t)
                return self.t[j], self.k[j]

        def load_w(dst, key, src, ncols):
            kb.dma("pool", dst[:, :, 0:ncols], src.rearrange("(kc p) n -> p kc n", p=128), writes=[key])

        def proj_fm(w, wkey, c0, m, hT, hkeys, t0, nt, ps_ap, pskey):
            for kc in range(8):
                kb.op("pe", lambda e, kc=kc: e.matmul(ps_ap, lhsT=w[:, kc, c0:c0 + m], rhs=hT[:, kc, t0:t0 + nt],
                                                      start=(kc == 0), stop=(kc == 7)),
                      reads=[wkey] + hkeys, writes=[pskey])

        def proj_tm(w, wkey, c0, n, hT, hkeys, i, ps_ap, pskey):
            for kc in range(8):
                kb.op("pe", lambda e, kc=kc: e.matmul(ps_ap, lhsT=hT[:, kc, i * 128:(i + 1) * 128], rhs=w[:, kc, c0:c0 + n],
                                                      start=(kc == 0), stop=(kc == 7)),
                      reads=[wkey] + hkeys, writes=[pskey])

        mT = sb(G, "mT", [128, 8, S], BF16)
        H1K = ["h1T"]
        ev_rr = [0]

        def evac_copy(out_ap, in_ap, reads, writes):
            e = "act" if ev_rr[0] % 2 == 0 else "dve"
            ev_rr[0] += 1
            if e == "act":
                kb.op("act", lambda en: en.copy(out=out_ap, in_=in_ap), reads=reads, writes=writes)
            else:
                kb.op("dve", lambda en: en.tensor_copy(out=out_ap, in_=in_ap), reads=reads, writes=writes)

        with contextlib.ExitStack() as GS:
            yaT = sb(GS, "yaT", [128, 8, S], BF16)
            lrT = sb(GS, "lrT", [16, S], F32)
            wb = Rot(GS, "wb", [128, 8, 512], BF16, 2)
            wa2 = sb(GS, "wa2", [16, 512], F32)
            ba2 = sb(GS, "ba2", [1, 512], F32)
            gng = sb(GS, "gng", [128, 2], F32)
            kb.dma("sp", wa2[:], wa2_d, writes=["wa2"])
            kb.dma("sp", ba2[:], ba2_d, writes=["ba2"])
            kb.dma("sp", gng[:], gng_d, writes=["gng"])
            w, wk = wb.next()
            load_w(w, wk, win_d[:, 3072:3088], 16)
            for tc in range(4):
                p = ps_next()
                proj_fm(w, wk, 0, 16, h1T, H1K, tc * 512, 512, psb[p][0:16, :], ("ps", p))
                kb.op("act", lambda e, p=p, tc=tc: e.copy(out=lrT[:, tc * 512:(tc + 1) * 512], in_=psb[p][0:16, :]),
                      reads=[("ps", p)], writes=["lrT"])
            for blk in range(2):
                w, wk = wb.next()
                load_w(w, wk, win_d[:, 2048 + blk * 512:2048 + (blk + 1) * 512], 512)
                for mt in range(4):
                    for tc in range(4):
                        p = ps_next()
                        proj_fm(w, wk, mt * 128, 128, h1T, H1K, tc * 512, 512, psb[p][:, :], ("ps", p))
                        kb.op("act", lambda e, p=p, tc=tc, ti=blk * 4 + mt: e.activation(
                            out=yaT[:, ti, tc * 512:(tc + 1) * 512], in_=psb[p][:, :], func=AF.Silu),
                            reads=[("ps", p)], writes=[("yaT", blk * 4 + mt)])
            for hp in range(2):
                with contextlib.ExitStack() as HP:
                    qT = sb(HP, "qT", [128, 2, S], BF16)
                    kT = sb(HP, "kT", [128, 2, S], BF16)
                    ktm = sb(HP, "ktm", [128, NT, 256], BF16)
                    vtm = sb(HP, "vtm", [128, NT, 512], BF16)
                    state = sb(HP, "state", [128, 2, 256], F32)
                    state_bf = sb(HP, "state_bf", [128, 2, 256], BF16)
                    r_e1 = Rot(HP, "e1", [128, 256], F32)
                    r_l = Rot(HP, "l", [128, 256], F32)
                    r_er = Rot(HP, "er", [128, 256], F32)
                    r_kh = Rot(HP, "kh", [128, 256], BF16)
                    r_eq = Rot(HP, "eq", [128, 128], F32, 3)
                    r_ek = Rot(HP, "ek", [128, 128], F32, 3)
                    r_qt = Rot(HP, "qt", [128, 128], BF16, 3)
                    r_kt = Rot(HP, "kt", [128, 128], BF16, 3)
                    r_pT = Rot(HP, "pT", [128, 128], BF16, 3)
                    r_sq = Rot(HP, "sq", [128, 256], BF16, 3)
                    r_rt = Rot(HP, "rt", [128, 128], F32, 3)
                    r_rs = Rot(HP, "rs", [128, 128], F32, 3)
                    r_t1 = Rot(HP, "t1", [128, 256], F32, 3)
                    kb.op("dve", lambda e: e.memset(state[:], 0.0), writes=["state0", "state1"])
                    kb.op("dve", lambda e: e.memset(state_bf[:], 0.0), writes=["sbf0", "sbf1"])
                    for which, dst, cbase in (("q", qT, 0), ("k", kT, 512)):
                        w, wk = wb.next()
                        load_w(w, wk, win_d[:, cbase + hp * 256:cbase + (hp + 1) * 256], 256)
                        for hh in range(2):
                            for tc in range(4):
                                p = ps_next()
                                proj_fm(w, wk, hh * 128, 128, h1T, H1K, tc * 512, 512, psb[p][:, :], ("ps", p))
                                evac_copy(dst[:, hh, tc * 512:(tc + 1) * 512], psb[p][:, :], [("ps", p)], [(which + "T", hh)])
                        if which == "k":
                            for i in range(NT):
                                p = ps_next()
                                proj_tm(w, wk, 0, 256, h1T, H1K, i, psb[p][:, 0:256], ("ps", p))
                                evac_copy(ktm[:, i, :], psb[p][:, 0:256], [("ps", p)], [("ktm", i)])
                    w, wk = wb.next()
                    load_w(w, wk, win_d[:, 1024 + hp * 512:1024 + (hp + 1) * 512], 512)
                    for i in range(NT):
                        p = ps_next()
                        proj_tm(w, wk, 0, 512, h1T, H1K, i, psb[p][:, :], ("ps", p))
                        evac_copy(vtm[:, i, :], psb[p][:, :], [("ps", p)], [("vtm", i)])
                    for i in range(NT):
                        ts = slice(i * 128, (i + 1) * 128)
                        p = ps_next()
                        kb.op("pe", lambda e, p=p: e.matmul(psb[p][:, 0:256], lhsT=lrT[:, ts], rhs=wa2[:, hp * 256:(hp + 1) * 256],
                                                            start=True, stop=False), reads=["lrT", "wa2"], writes=[("ps", p)])
                        kb.op("pe", lambda e, p=p: e.matmul(psb[p][:, 0:256], lhsT=ones_f[0:1, 0:128], rhs=ba2[0:1, hp * 256:(hp + 1) * 256],
                                                            start=False, stop=True), reads=["ones_f", "ba2"], writes=[("ps", p)])
                        e1, e1k = r_e1.next()
                        lt, lk = r_l.next()
                        kb.op("act", lambda e, p=p, e1=e1: e.activation(out=e1[:], in_=psb[p][:, 0:256], func=AF.Exp, scale=-1.0),
                              reads=[("ps", p)], writes=[e1k])
                        kb.op("act", lambda e, e1=e1, lt=lt: e.activation(out=lt[:], in_=e1[:], func=AF.Ln, bias=1.0),
                              reads=[e1k], writes=[lk])
                        p = ps_next()
                        kb.op("pe", lambda e, p=p, lt=lt: e.matmul(psb[p][:, 0:256], lhsT=SU[:], rhs=lt[:], start=True, stop=True),
                              reads=["SU", lk], writes=[("ps", p)])
                        er, erk = r_er.next()
                        kb.op("act", lambda e, p=p, er=er: e.activation(out=er[:], in_=psb[p][:, 0:256], func=AF.Exp, scale=-1.0 / 16),
                              reads=[("ps", p)], writes=[erk])
                        kh, khk = r_kh.next()
                        kb.op("dve", lambda e, kh=kh, er=er: e.tensor_tensor(out=kh[:], in0=ktm[:, i, :], in1=er[:], op=ALU.mult),
                              reads=[("ktm", i), erk], writes=[khk])
                        for hh in range(2):
                            h = hp * 2 + hh
                            p = ps_next()
                            kb.op("pe", lambda e, p=p, lt=lt: e.matmul(psb[p][:, 0:128], lhsT=lt[:, hh * 128:(hh + 1) * 128], rhs=LT[:],
                                                                      start=True, stop=True), reads=[lk, "LT"], writes=[("ps", p)])
                            eq, eqk = r_eq.next()
                            ek, ekk = r_ek.next()
                            kb.op("act", lambda e, p=p, eq=eq: e.activation(out=eq[:], in_=psb[p][:, 0:128], func=AF.Exp, scale=-1.0 / 16),
                                  reads=[("ps", p)], writes=[eqk])
                            kb.op("act", lambda e, p=p, ek=ek: e.activation(out=ek[:], in_=psb[p][:, 0:128], func=AF.Exp, scale=1.0 / 16),
                                  reads=[("ps", p)], writes=[ekk])
                            qt, qtk = r_qt.next()
                            kt, ktk = r_kt.next()
                            kb.op("dve", lambda e, qt=qt, eq=eq: e.scalar_tensor_tensor(
                                out=qt[:], in0=qT[:, hh, ts], scalar=float(128 ** -0.5), in1=eq[:], op0=ALU.mult, op1=ALU.mult),
                                reads=[("qT", hh), eqk], writes=[qtk])
                            kb.op("dve", lambda e, kt=kt, ek=ek: e.tensor_tensor(out=kt[:], in0=kT[:, hh, ts], in1=ek[:], op=ALU.mult),
                                  reads=[("kT", hh), ekk], writes=[ktk])
                            p = ps_next()
                            kb.op("pe", lambda e, p=p, kt=kt, qt=qt: e.matmul(psb[p][:, 0:128], lhsT=kt[:], rhs=qt[:], start=True, stop=True),
                                  reads=[ktk, qtk], writes=[("ps", p)])
                            pT, pTk = r_pT.next()
                            kb.op("dve", lambda e, p=p, pT=pT: e.tensor_tensor(out=pT[:], in0=psb[p][:, 0:128], in1=LT[:], op=ALU.mult),
                                  reads=[("ps", p), "LT"], writes=[pTk])
                            po = ps_next()
                            for vh in range(2):
                                kb.op("pe", lambda e, po=po, vh=vh, pT=pT: e.matmul(
                                    psb[po][:, vh * 128:(vh + 1) * 128], lhsT=vtm[:, i, hh * 256 + vh * 128:hh * 256 + (vh + 1) * 128],
                                    rhs=pT[:], start=True, stop=False), reads=[("vtm", i), pTk], writes=[("ps", po)])
                                kb.op("pe", lambda e, po=po, vh=vh, qt=qt: e.matmul(
                                    psb[po][:, vh * 128:(vh + 1) * 128], lhsT=state_bf[:, hh, vh * 128:(vh + 1) * 128],
                                    rhs=qt[:], start=False, stop=True), reads=["sbf%d" % hh, qtk], writes=[("ps", po)])
                            sq, sqk = r_sq.next()
                            kb.op("act", lambda e, po=po, sq=sq: e.activation(out=sq[:], in_=psb[po][:, 0:256], func=AF.Square),
                                  reads=[("ps", po)], writes=[sqk])
                            p2 = ps_next()
                            for vh in range(2):
                                kb.op("pe", lambda e, p2=p2, vh=vh, sq=sq: e.matmul(
                                    psb[p2][:, 0:128], lhsT=ones_bf[:], rhs=sq[:, vh * 128:(vh + 1) * 128],
                                    start=(vh == 0), stop=(vh == 1)), reads=["ones_bf", sqk], writes=[("ps", p2)])
                            rt, rtk = r_rt.next()
                            rs, rsk = r_rs.next()
                            kb.op("act", lambda e, p2=p2, rt=rt: e.activation(out=rt[:], in_=psb[p2][:, 0:128], func=AF.Sqrt,
                                                                              scale=1.0 / 256, bias=EPS), reads=[("ps", p2)], writes=[rtk])
                            kb.op("dve", lambda e, rt=rt, rs=rs: e.reciprocal(out=rs[:], in_=rt[:]), reads=[rtk], writes=[rsk])
                            t1, t1k = r_t1.next()
                            for vh in range(2):
                                kb.op("dve", lambda e, po=po, vh=vh, t1=t1, rs=rs: e.scalar_tensor_tensor(
                                    out=t1[:, vh * 128:(vh + 1) * 128], in0=psb[po][:, vh * 128:(vh + 1) * 128],
                                    scalar=gng[:, vh:vh + 1], in1=rs[:], op0=ALU.mult, op1=ALU.mult),
                                    reads=[("ps", po), "gng", rsk], writes=[t1k])
                            for vh in range(2):
                                kb.op("pool", lambda e, vh=vh, t1=t1, h=h: e.tensor_tensor(
                                    out=yaT[:, h * 2 + vh, ts], in0=t1[:, vh * 128:(vh + 1) * 128], in1=yaT[:, h * 2 + vh, ts],
                                    op=ALU.mult), reads=[t1k, ("yaT", h * 2 + vh)], writes=[("yaT", h * 2 + vh)])
                            pk = ps_next()
                            kb.op("pe", lambda e, pk=pk, kh=kh: e.matmul(psb[pk][:, 0:256], lhsT=kh[:, hh * 128:(hh + 1) * 128],
                                                                        rhs=vtm[:, i, hh * 256:(hh + 1) * 256], start=True, stop=True),
                                  reads=[khk, ("vtm", i)], writes=[("ps", pk)])
                            kb.op("dve", lambda e, pk=pk, eq=eq: e.scalar_tensor_tensor(
                                out=state[:, hh, :], in0=state[:, hh, :], scalar=eq[:, 127:128], in1=psb[pk][:, 0:256],
                                op0=ALU.mult, op1=ALU.add), reads=["state%d" % hh, eqk, ("ps", pk)], writes=["state%d" % hh])
                            kb.op("act", lambda e: e.copy(out=state_bf[:, hh, :], in_=state[:, hh, :]),
                                  reads=["state%d" % hh], writes=["sbf%d" % hh])
                    kb.barrier()
            if "yaT" in dbg:
                kb.dma("sp", dbg_out("yaT", [128, 8, S], BF16), yaT[:], reads=[("yaT", i) for i in range(8)])
            for blk in range(2):
                wm, wmk = wb.next()
                load_w(wm, wmk, win_d[:, 4928 + blk * 512:4928 + (blk + 1) * 512], 512)
                wa, wak = wb.next()
                load_w(wa, wak, wba_d[:, blk * 512:(blk + 1) * 512], 512)
                with contextlib.ExitStack() as BS:
                    r_sg = Rot(BS, "sg", [128, 512], F32, 2)
                    for mt in range(4):
                        for tc in range(4):
                            p1 = ps_next()
                            proj_fm(wm, wmk, mt * 128, 128, h1T, H1K, tc * 512, 512, psb[p1][:, :], ("ps", p1))
                            sg, sgk = r_sg.next()
                            kb.op("act", lambda e, p1=p1, sg=sg: e.activation(out=sg[:], in_=psb[p1][:, :], func=AF.Sigmoid),
                                  reads=[("ps", p1)], writes=[sgk])
                            p2 = ps_next()
                            proj_fm(wa, wak, mt * 128, 128, yaT, [("yaT", j) for j in range(8)], tc * 512, 512, psb[p2][:, :], ("ps", p2))
                            kb.op("dve", lambda e, p2=p2, sg=sg, ti=blk * 4 + mt, tc=tc: e.tensor_tensor(
                                out=mT[:, ti, tc * 512:(tc + 1) * 512], in0=psb[p2][:, :], in1=sg[:], op=ALU.mult),
                                reads=[("ps", p2), sgk], writes=[("mT", blk * 4 + mt)])
                    kb.barrier()
            kb.barrier()
        pos_d = nc.dram_tensor("pos", [1, S], I32, kind="ExternalInput").ap()
        ropec_d = din("rope_c", [16, 4])
        wqrot_d = din("w_qrot", [D, 256])
        wkrot_d = din("w_krot", [D, 96])
        cw1_d = din("cmp_w1", [2, 64, 32 * 64])
        cw2_d = din("cmp_w2", [2, 64, 64])
        cpe_d = din("cmp_peT", [2, 64, 32])
        ovl_d = din("ovl", [127, 32])
        cmpb_d = din("cmpbias", [127, S])
        blk_d = din("blockind", [32, S])
        cb_d = din("causalbias", [128, 128])
        ab_d = din("antibias", [128, 128])
        Msel_d = din("Msel", [128, NT, 32])
        Fsel_d = din("Fsel", [128, NT, 32])
        wbb_d = din("w_branch_b", [D, D])
        NEG = 30000.0

        with contextlib.ExitStack() as NS:
            zeros_bf = sb(NS, "zeros_bf", [128, 512], BF16)
            kb.op("pool", lambda e: e.memset(zeros_bf[:], 0.0), writes=["zeros_bf"])
            cosT = sb(NS, "cosT", [16, S], BF16)
            sinT = sb(NS, "sinT", [16, S], BF16)
            gates = sb(NS, "gates", [128, NT, 48], F32)
            ovl_f = sb(NS, "ovl_f", [127, 32], F32)
            cmpb = sb(NS, "cmpb", [127, S], BF16)
            cb = sb(NS, "cb", [128, 128], BF16)
            ab = sb(NS, "ab", [128, 128], BF16)
            Msel = sb(NS, "Msel", [128, NT, 32], F32)
            Fsel = sb(NS, "Fsel", [128, NT, 32], F32)
            kb.dma("sp", ovl_f[:], ovl_d, writes=["ovl_f"])
            kb.dma("pool", cmpb[:], cmpb_d, writes=["cmpb"])
            kb.dma("pool", cb[:], cb_d, writes=["cb"])
            kb.dma("pool", ab[:], ab_d, writes=["ab"])
            kb.dma("sp", Msel[:], Msel_d, writes=["Msel"])
            kb.dma("sp", Fsel[:], Fsel_d, writes=["Fsel"])
            with contextlib.ExitStack() as RS:
                posi = sb(RS, "posi", [16, S], I32)
                posf = sb(RS, "posf", [16, S], F32)
                ang = sb(RS, "ang", [16, S], F32)
                ki = sb(RS, "ki", [16, S], I32)
                kf = sb(RS, "kf", [16, S], F32)
                rr = sb(RS, "rr", [16, S], F32)
                ropec = sb(RS, "ropec", [16, 4], F32)
                kb.dma("sp", posi[:], pos_d.broadcast_to([16, S]), writes=["posi"])
                kb.dma("sp", ropec[:], ropec_d, writes=["ropec"])
                kb.op("dve", lambda e: e.tensor_copy(out=posf[:], in_=posi[:]), reads=["posi"], writes=["posf"])
                for col, dst, dk in ((1, cosT, "cosT"), (2, sinT, "sinT")):
                    kb.op("dve", lambda e, col=col: e.tensor_scalar(out=ang[:], in0=posf[:], scalar1=ropec[:, 0:1], scalar2=ropec[:, col:col + 1],
                                                                    op0=ALU.mult, op1=ALU.add), reads=["posf", "ropec"], writes=["ang"])
                    kb.op("dve", lambda e: e.tensor_scalar(out=ki[:], in0=ang[:], scalar1=float(1.0 / (2 * np.pi)), scalar2=None,
                                                           op0=ALU.mult), reads=["ang"], writes=["ki"])
                    kb.op("dve", lambda e: e.tensor_copy(out=kf[:], in_=ki[:]), reads=["ki"], writes=["kf"])
                    kb.op("dve", lambda e: e.scalar_tensor_tensor(out=rr[:], in0=kf[:], scalar=float(-2 * np.pi), in1=ang[:],
                                                                  op0=ALU.mult, op1=ALU.add), reads=["kf", "ang"], writes=["rr"])
                    kb.op("dve", lambda e: e.tensor_scalar(out=kf[:], in0=rr[:], scalar1=float(np.pi), scalar2=float(2 * np.pi),
                                                           op0=ALU.is_gt, op1=ALU.mult), reads=["rr"], writes=["kf"])
                    kb.op("dve", lambda e: e.tensor_tensor(out=rr[:], in0=rr[:], in1=kf[:], op=ALU.subtract), reads=["rr", "kf"], writes=["rr"])
                    kb.op("dve", lambda e: e.tensor_scalar(out=kf[:], in0=rr[:], scalar1=float(-np.pi), scalar2=float(2 * np.pi),
                                                           op0=ALU.is_lt, op1=ALU.mult), reads=["rr"], writes=["kf"])
                    kb.op("dve", lambda e: e.tensor_tensor(out=rr[:], in0=rr[:], in1=kf[:], op=ALU.add), reads=["rr", "kf"], writes=["rr"])
                    kb.op("act", lambda e, dst=dst: e.activation(out=dst[:], in_=rr[:], func=AF.Sin), reads=["rr"], writes=[dk])
                kb.barrier()
            kstop("rope")
            wbn = Rot(NS, "wbn", [128, 8, 512], BF16, 2)
            w, wk = wbn.next()
            load_w(w, wk, win_d[:, 4880:4928], 48)
            for i in range(NT):
                p = ps_next()
                proj_tm(w, wk, 0, 48, h1T, H1K, i, psb[p][:, 0:48], ("ps", p))
                kb.op("act", lambda e, p=p, i=i: e.activation(out=gates[:, i, :], in_=psb[p][:, 0:48], func=AF.Sigmoid),
                      reads=[("ps", p)], writes=["gates"])

            kstop("gates")
            for g in range(2):
                with contextlib.ExitStack() as GR:
                    Qa = [sb(GR, "Qa%d" % hh, [96, S], BF16) for hh in range(8)]
                    kcT = sb(GR, "kcT", [64, S], BF16)
                    ksT = sb(GR, "ksT", [96, S], BF16)
                    kwT = sb(GR, "kwT", [64, S], BF16)
                    vs_a = sb(GR, "vs_a", [128, NT, 65], BF16)
                    vw_a = sb(GR, "vw_a", [128, NT, 65], BF16)
                    vcmp = sb(GR, "vcmp", [127, 97], BF16)
                    kcmpT = sb(GR, "kcmpT", [64, 127], BF16)
                    yb = sb(GR, "yb", [128, NT, 512], BF16)
                    score = sb(GR, "score", [128, NT, 32], F32)
                    PJ = contextlib.ExitStack()
                    vcT = sb(PJ, "vcT", [64, S], BF16)
                    cw1 = sb(PJ, "cw1", [64, 2, 32 * 64], BF16)
                    cw2 = sb(PJ, "cw2", [64, 2, 64], BF16)
                    cpe = sb(PJ, "cpe", [64, 2, 32], BF16)
                    peb = sb(PJ, "peb", [64, 2], F32)
                    gsb = sb(PJ, "gsb", [64, 2, 127], BF16)
                    r_t1 = Rot(PJ, "rt1", [16, 512], F32, 1)
                    r_t2 = Rot(PJ, "rt2", [16, 512], F32, 1)
                    wqr = sb(PJ, "wqr", [128, 8, 128], BF16)
                    wkr = sb(PJ, "wkr", [128, 8, 96], BF16)
                    wkv = sb(PJ, "wkv", [128, 8, 384], BF16)
                    for wi in range(6):
                        kb.dma("pool", wkv[:, :, wi * 64:(wi + 1) * 64],
                               win_d[:, 4112 + wi * 128 + g * 64:4112 + wi * 128 + (g + 1) * 64].rearrange("(kc p) n -> p kc n", p=128),
                               writes=["wkv"])
                    kb.op("dve", lambda e: e.memset(score[:], 0.0), writes=["score"])
                    kb.op("pool", lambda e: e.memset(vs_a[:, :, 64:65], 1.0), writes=["vs_a"])
                    kb.op("pool", lambda e: e.memset(vw_a[:, :, 64:65], 1.0), writes=["vw_a"])
                    kb.op("pool", lambda e: e.memset(vcmp[:, 64:65], 1.0), writes=["vcmp"])
                    kb.op("dve", lambda e: e.tensor_copy(out=vcmp[:, 65:97], in_=ovl_f[:]), reads=["ovl_f"], writes=["vcmp"])
                    for hh in range(8):
                        kb.op("pool", lambda e, hh=hh: e.memset(Qa[hh][64:96, :], 0.0), writes=[("Qa", hh)])
                    kb.dma("pool", ksT[64:96, :], blk_d, writes=["ksT"])
                    for kv in range(2):
                        kb.dma("pool", cw1[:, kv, :], cw1_d[kv], writes=["cw1"])
                    kb.dma("pool", cw2[:], cw2_d.rearrange("k d c -> d k c"), writes=["cw2"])
                    kb.dma("pool", cpe[:], cpe_d.rearrange("k d l -> d k l"), writes=["cpe"])
                    load_w(wqr, "wqr", wqrot_d[:, g * 128:(g + 1) * 128], 128)
                    load_w(wkr, "wkr", wkrot_d, 96)

                    def rope_evac(dst, dkey, pm, pr, tc):
                        cs = slice(tc * 512, (tc + 1) * 512)
                        kb.op("act", lambda e: e.copy(out=dst[0:64, cs], in_=psb[pm][0:64, :]), reads=[("ps", pm)], writes=[dkey])
                        import os
                        if os.environ.get("KNOROPE"):
                            return
                        t1, t1k = r_t1.next()
                        t2, t2k = r_t2.next()
                        kb.op("dve", lambda e: e.tensor_tensor(out=t1[:], in0=cosT[:, cs], in1=psb[pm][0:16, :], op=ALU.mult),
                              reads=[("ps", pm), "cosT"], writes=[t1k])
                        kb.op("dve", lambda e: e.tensor_tensor(out=t2[:], in0=sinT[:, cs], in1=psb[pr][0:16, :], op=ALU.mult),
                              reads=[("ps", pr), "sinT"], writes=[t2k])
                        if os.environ.get("KNOPOOL"):
                            return
                        kb.op("pool", lambda e: e.tensor_tensor(out=dst[0:16, cs], in0=t1[:], in1=t2[:], op=ALU.add),
                              reads=[t1k, t2k], writes=[dkey])

                    kstop("grsetup")
                    w, wk = wbn.next()
                    load_w(w, wk, win_d[:, 3088 + g * 512:3088 + (g + 1) * 512], 512)
                    for hh in range(8):
                        for tc in range(4):
                            pm = ps_next()
                            proj_fm(w, wk, hh * 64, 64, h1T, H1K, tc * 512, 512, psb[pm][0:64, :], ("ps", pm))
                            pr = ps_next()
                            proj_fm(wqr, "wqr", hh * 16, 16, h1T, H1K, tc * 512, 512, psb[pr][0:16, :], ("ps", pr))
                            rope_evac(Qa[hh], ("Qa", hh), pm, pr, tc)
                    kstop("qproj")
                    for wi, dst, dkey, ri_ in ((0, kcT, "kcT", 0), (2, ksT, "ksT", 1), (4, kwT, "kwT", 2)):
                        for tc in range(4):
                            pm = ps_next()
                            proj_fm(wkv, "wkv", wi * 64, 64, h1T, H1K, tc * 512, 512, psb[pm][0:64, :], ("ps", pm))
                            pr = ps_next()
                            proj_fm(wkr, "wkr", ri_ * 32 + g * 16, 16, h1T, H1K, tc * 512, 512, psb[pr][0:16, :], ("ps", pr))
                            rope_evac(dst, dkey, pm, pr, tc)
                    for tc in range(4):
                        pm = ps_next()
                        proj_fm(wkv, "wkv", 1 * 64, 64, h1T, H1K, tc * 512, 512, psb[pm][0:64, :], ("ps", pm))
                        kb.op("act", lambda e, pm=pm, tc=tc: e.copy(out=vcT[:, tc * 512:(tc + 1) * 512], in_=psb[pm][0:64, :]),
                              reads=[("ps", pm)], writes=["vcT"])
                    for wi, dst, dkey in ((3, vs_a, "vs_a"), (5, vw_a, "vw_a")):
                        for i in range(NT):
                            pm = ps_next()
                            proj_tm(wkv, "wkv", wi * 64, 64, h1T, H1K, i, psb[pm][:, 0:64], ("ps", pm))
                            evac_copy(dst[:, i, 0:64], psb[pm][:, 0:64], [("ps", pm)], [dkey])
                    kstop("kvproj")
                    for kv, srcT, skey in ((0, kcT, "kcT"), (1, vcT, "vcT")):
                        p = ps_next()
                        for l in range(32):
                            kb.op("pe", lambda e, p=p, l=l, kv=kv: e.matmul(psb[p][0:64, 0:1], lhsT=cw1[:, kv, l * 64:(l + 1) * 64],
                                                                            rhs=cpe[:, kv, l:l + 1], start=(l == 0), stop=(l == 31)),
                                  reads=["cw1", "cpe"], writes=[("ps", p)])
                        kb.op("act", lambda e, p=p, kv=kv: e.copy(out=peb[:, kv:kv + 1], in_=psb[p][0:64, 0:1]),
                              reads=[("ps", p)], writes=["peb"])
                        p = ps_next()
                        for l in range(32):
                            kb.op("pe", lambda e, p=p, l=l, kv=kv, srcT=srcT: e.matmul(
                                psb[p][0:64, 0:127], lhsT=cw1[:, kv, l * 64:(l + 1) * 64], rhs=srcT[0:64, l:l + 16 * 126 + 1:16],
                                start=(l == 0), stop=(l == 31)), reads=["cw1", skey], writes=[("ps", p)])
                        kb.op("act", lambda e, p=p, kv=kv: e.activation(out=gsb[:, kv, :], in_=psb[p][0:64, 0:127], func=AF.Gelu,
                                                                        bias=peb[:, kv:kv + 1]), reads=[("ps", p), "peb"], writes=["gsb"])
                    p = ps_next()
                    kb.op("pe", lambda e, p=p: e.matmul(psb[p][0:64, 0:127], lhsT=cw2[:, 0, :], rhs=gsb[:, 0, :], start=True, stop=True),
                          reads=["cw2", "gsb"], writes=[("ps", p)])
                    kb.op("act", lambda e, p=p: e.copy(out=kcmpT[:], in_=psb[p][0:64, 0:127]), reads=[("ps", p)], writes=["kcmpT"])
                    p = ps_next()
                    kb.op("pe", lambda e, p=p: e.matmul(psb[p][0:127, 0:64], lhsT=gsb[:, 1, :], rhs=cw2[:, 1, :], start=True, stop=True),
                          reads=["cw2", "gsb"], writes=[("ps", p)])
                    kb.op("act", lambda e, p=p: e.copy(out=vcmp[:, 0:64], in_=psb[p][0:127, 0:64]), reads=[("ps", p)], writes=["vcmp"])

                    kb.barrier()
                    PJ.close()
                    AT = contextlib.ExitStack()
                    r_PT = Rot(AT, "PT", [128, 512], BF16, 4)
                    r_ri = Rot(AT, "ri", [128, 8], F32, 3)
                    r_tm = Rot(AT, "otmp", [128, 4, 64], F32, 2)
                    r_sc = Rot(AT, "sct", [128, 4, 32], F32, 2)
                    def attn_chunk(hh, tc, K, kT, kkey, kt_list, rhs_of, vkey, vn, pv_map, br):
                        po = ps_next_acc()
                        total = sum(len(x) for x in pv_map)
                        done = [0]

                        def emit_pv(entry):
                            j, PT, PTk, nk = entry
                            for qi in range(4):
                                if j in pv_map[qi]:
                                    done[0] += 1
                                    first = (done[0] == 1)
                                    kb.op("pe", lambda e, qi=qi: e.matmul(
                                        psb[po][:, qi * vn:(qi + 1) * vn], lhsT=PT[0:nk, qi * 128:(qi + 1) * 128], rhs=rhs_of(j),
                                        start=first, stop=True, skip_group_check=True), reads=[PTk, vkey], writes=[("ps", po)])
                        pend = []
                        for (j, nk, c0, c1, masks) in kt_list:
                            p = ps_next()
                            kb.op("pe", lambda e, p=p, j=j, nk=nk, c0=c0, c1=c1: e.matmul(
                                psb[p][0:nk, c0:c1], lhsT=kT[0:K, j * 128:j * 128 + nk], rhs=Qa[hh][0:K, tc * 512 + c0:tc * 512 + c1],
                                start=True, stop=(len(masks) == 0)), reads=[kkey, ("Qa", hh)], writes=[("ps", p)])
                            for mi, (col0, width, bt, bkey) in enumerate(masks):
                                kb.op("pe", lambda e, p=p, nk=nk, col0=col0, width=width, bt=bt, mi=mi: e.matmul(
                                    psb[p][0:nk, col0:col0 + width], lhsT=ident_b[0:nk, 0:nk], rhs=bt,
                                    start=False, stop=(mi == len(masks) - 1)), reads=["ident_b", bkey], writes=[("ps", p)])
                            PT, PTk = r_PT.next()
                            kb.op("act", lambda e, p=p, nk=nk, c0=c0, c1=c1, PT=PT: e.activation(
                                out=PT[0:nk, c0:c1], in_=psb[p][0:nk, c0:c1], func=AF.Exp, scale=0.125),
                                reads=[("ps", p)], writes=[PTk])
                            pend.append((j, PT, PTk, nk))
                            if len(pend) > 2:
                                emit_pv(pend.pop(0))
                        while pend:
                            emit_pv(pend.pop(0))
                        O = psb[po][:, 0:4 * vn].rearrange("p (a c) -> p a c", c=vn)
                        ri, rik = r_ri.next()
                        kb.op("dve", lambda e: e.tensor_scalar(out=ri[:, 0:4], in0=O[:, :, 64], scalar1=1e-30, scalar2=None, op0=ALU.add),
                              reads=[("ps", po)], writes=[rik])
                        kb.op("dve", lambda e: e.reciprocal(out=ri[:, 4:8], in_=ri[:, 0:4]), reads=[rik], writes=[rik])
                        if br == 0:
                            sct, sck = r_sc.next()
                            kb.op("dve", lambda e: e.tensor_tensor(out=sct[:], in0=O[:, :, 65:97],
                                                                   in1=ri[:, 4:8].unsqueeze(2).broadcast_to([128, 4, 32]), op=ALU.mult),
                                  reads=[("ps", po), rik], writes=[sck])
                            kb.op("pool", lambda e: e.tensor_tensor(out=score[:, 4 * tc:4 * tc + 4, :], in0=score[:, 4 * tc:4 * tc + 4, :],
                                                                    in1=sct[:], op=ALU.add), reads=[sck, "score"], writes=["score"])
                        gcol = br * 16 + g * 8 + hh
                        kb.op("dve", lambda e: e.tensor_tensor(out=ri[:, 0:4], in0=ri[:, 4:8], in1=gates[:, 4 * tc:4 * tc + 4, gcol], op=ALU.mult),
                              reads=[rik, "gates"], writes=[rik])
                        ydst = yb[:, 4 * tc:4 * tc + 4, hh * 64:(hh + 1) * 64]
                        rgb = ri[:, 0:4].unsqueeze(2).broadcast_to([128, 4, 64])
                        if br == 0:
                            kb.op("dve", lambda e: e.tensor_tensor(out=ydst, in0=O[:, :, 0:64], in1=rgb, op=ALU.mult),
                                  reads=[("ps", po), rik], writes=[("yb", hh, tc)])
                        else:
                            ot, otk = r_tm.next()
                            kb.op("dve", lambda e: e.tensor_tensor(out=ot[:], in0=O[:, :, 0:64], in1=rgb, op=ALU.mult),
                                  reads=[("ps", po), rik], writes=[otk])
                            kb.op("pool", lambda e: e.tensor_tensor(out=ydst, in0=ydst, in1=ot[:], op=ALU.add),
                                  reads=[otk, ("yb", hh, tc)], writes=[("yb", hh, tc)])

                    kstop("cmp")
                    for hh in range(8):
                        for tc in range(4):
                            attn_chunk(hh, tc, 64, kcmpT, "kcmpT",
                                       [(0, 127, 0, 512, [(0, 512, cmpb[:, tc * 512:(tc + 1) * 512], "cmpb")])],
                                       lambda j: vcmp[0:127, 0:97], "vcmp", 97, [[0], [0], [0], [0]], 0)
                    kstop("passA")
                    r_scp = Rot(AT, "scp", [128, 32], F32, 2)
                    r_sc2 = Rot(AT, "sc2", [128, 32], F32, 2)
                    r_m8 = Rot(AT, "m8", [128, 16], F32, 2)
                    r_bt = Rot(AT, "biast", [128, 96], BF16, 2)
                    for bt_, btk_ in zip(r_bt.t, r_bt.k):
                        kb.op("dve", lambda e, bt_=bt_: e.memset(bt_[:], 0.0), writes=[btk_])
                    for i in range(8, NT):
                        scp, scpk = r_scp.next()
                        sc2, sc2k = r_sc2.next()
                        m8, m8k = r_m8.next()
                        bt_, btk_ = r_bt.next()
                        kb.op("dve", lambda e: e.scalar_tensor_tensor(out=scp[:], in0=score[:, i, :], scalar=1.0, in1=Msel[:, i, :],
                                                                      op0=ALU.add, op1=ALU.mult), reads=["score", "Msel"], writes=[scpk])
                        kb.op("dve", lambda e: e.max(out=m8[:, 0:8], in_=scp[:]), reads=[scpk], writes=[m8k])
                        kb.op("dve", lambda e: e.match_replace(out=sc2[:], in_to_replace=m8[:, 0:8], in_values=scp[:], imm_value=-1.0),
                              reads=[scpk, m8k], writes=[sc2k])
                        kb.op("dve", lambda e: e.max(out=m8[:, 8:16], in_=sc2[:]), reads=[sc2k], writes=[m8k])
                        kb.op("dve", lambda e: e.tensor_scalar(out=sc2[:], in0=scp[:], scalar1=m8[:, 12:13], scalar2=None, op0=ALU.is_ge),
                              reads=[scpk, m8k], writes=[sc2k])
                        kb.op("dve", lambda e: e.tensor_tensor(out=sc2[:], in0=sc2[:], in1=Fsel[:, i, :], op=ALU.max),
                              reads=[sc2k, "Fsel"], writes=[sc2k])
                        kb.op("dve", lambda e: e.tensor_scalar(out=bt_[:, 64:96], in0=sc2[:], scalar1=NEG, scalar2=-NEG,
                                                               op0=ALU.mult, op1=ALU.add), reads=[sc2k], writes=[btk_])
                        p = ps_next()
                        pv = psb[p][:].bitcast(BF16)
                        kb.op("pe", lambda e, pv=pv: e.transpose(out=pv[0:96, 0:128], in_=bt_[:, 0:96], identity=ident_b[:]),
                              reads=[btk_, "ident_b"], writes=[("ps", p)])
                        for hh in range(8):
                            evac_copy(Qa[hh][64:96, i * 128:(i + 1) * 128], pv[64:96, 0:128], [("ps", p)], [("Qa", hh)])
                    if "score" in dbg and g == 0:
                        kb.dma("sp", dbg_out("score", [128, NT, 32]), score[:], reads=["score"])
                    if "ycmp" in dbg and g == 0:
                        kb.dma("sp", dbg_out("ycmp", [128, NT, 512], BF16), yb[:], reads=[("yb", hh, tc) for hh in range(8) for tc in range(4)])
                    kstop("passB")
                    for hh in range(8):
                        for tc in range(4):
                            ktl = []
                            for j in range(4 * tc + 4):
                                if j < 4 * tc:
                                    ktl.append((j, 128, 0, 512, []))
                                else:
                                    r = j - 4 * tc
                                    ktl.append((j, 128, 128 * r, 512, [(128 * r, 128, cb[:], "cb")]))
                            import os
                            KBR = os.environ.get("KBR", "12")
                            if "1" in KBR:
                                attn_chunk(hh, tc, 96, ksT, "ksT", ktl, lambda j: vs_a[:, j, 0:65], "vs_a", 65,
                                           [list(range(0, 4 * tc + qi + 1)) for qi in range(4)], 1)
                            ktl = []
                            for j in range(max(0, 4 * tc - 4), 4 * tc + 4):
                                jj = j - 4 * tc
                                c0 = 128 * max(jj, 0)
                                c1 = 128 * (min(jj + 4, 3) + 1)
                                masks = []
                                if jj >= 0:
                                    masks.append((128 * jj, 128, cb[:], "cb"))
                                if jj + 4 <= 3:
                                    masks.append((128 * (jj + 4), 128, ab[:], "ab"))
                                ktl.append((j, 128, c0, c1, masks))
                            if "2" in KBR:
                                attn_chunk(hh, tc, 64, kwT, "kwT", ktl, lambda j: vw_a[:, j, 0:65], "vw_a", 65,
                                           [list(range(max(0, 4 * tc + qi - 4), 4 * tc + qi + 1)) for qi in range(4)], 2)
                    kb.barrier()
                    AT.close()
                    FB = contextlib.ExitStack()
                    ybT = sb(FB, "ybT", [128, 4, S], BF16)
                    r_sg = Rot(FB, "sgb", [128, 512], F32, 2)
                    r_tb = Rot(FB, "tb", [128, 512], F32, 2)
                    kstop("passC")
                    for i in range(NT):
                        p = ps_next()
                        pv = psb[p][:].bitcast(BF16)
                        for m in range(4):
                            kb.op("pe", lambda e, pv=pv, m=m, i=i: e.transpose(out=pv[:, m * 128:(m + 1) * 128], in_=yb[:, i, m * 128:(m + 1) * 128],
                                                                              identity=ident_b[:]),
                                  reads=[("yb", hh, i // 4) for hh in range(8)] + ["ident_b"], writes=[("ps", p)])
                        evac_copy(ybT[:, :, i * 128:(i + 1) * 128], pv[:, 0:512].rearrange("p (m t) -> p m t", t=128),
                                  [("ps", p)], ["ybT"])
                    kstop("fbT")
                    if "ybT" in dbg:
                        kb.dma("sp", dbg_out("ybT%d" % g, [128, 4, S], BF16), ybT[:], reads=["ybT"])
                    for blk in range(2):
                        wm, wmk = wbn.next()
                        load_w(wm, wmk, win_d[:, 5952 + blk * 512:5952 + (blk + 1) * 512], 512)
                        wa, wak = wbn.next()
                        kb.dma("pool", wa[:, 0:4, :], wbb_d[g * 512:(g + 1) * 512, blk * 512:(blk + 1) * 512].rearrange("(kc p) n -> p kc n", p=128),
                               writes=[wak])
                        for mt in range(4):
                            for tc in range(4):
                                p1 = ps_next()
                                proj_fm(wm, wmk, mt * 128, 128, h1T, H1K, tc * 512, 512, psb[p1][:, :], ("ps", p1))
                                sg, sgk = r_sg.next()
                                kb.op("act", lambda e, p1=p1, sg=sg: e.activation(out=sg[:], in_=psb[p1][:, :], func=AF.Sigmoid),
                                      reads=[("ps", p1)], writes=[sgk])
                                p2 = ps_next()
                                for kc in range(4):
                                    kb.op("pe", lambda e, kc=kc, p2=p2: e.matmul(psb[p2][:, :], lhsT=wa[:, kc, mt * 128:(mt + 1) * 128],
                                                                                 rhs=ybT[:, kc, tc * 512:(tc + 1) * 512],
                                                                                 start=(kc == 0), stop=(kc == 3)),
                                          reads=[wak, "ybT"], writes=[("ps", p2)])
                                tb, tbk = r_tb.next()
                                kb.op("dve", lambda e, p2=p2, sg=sg, tb=tb: e.tensor_tensor(out=tb[:], in0=psb[p2][:, :], in1=sg[:], op=ALU.mult),
                                      reads=[("ps", p2), sgk], writes=[tbk])
                                ti = blk * 4 + mt
                                kb.op("pool", lambda e, tb=tb, ti=ti, tc=tc: e.tensor_tensor(
                                    out=mT[:, ti, tc * 512:(tc + 1) * 512], in0=mT[:, ti, tc * 512:(tc + 1) * 512], in1=tb[:], op=ALU.add),
                                    reads=[tbk, ("mT", ti)], writes=[("mT", ti)])
                    kb.barrier()
                    FB.close()
                    kstop("g0")
            kb.barrier()
        if "mT" in dbg:
            kb.dma("sp", dbg_out("mT", [128, 8, S], BF16), mT[:], reads=[("mT", i) for i in range(8)])
        wout_d = din("w_out", [D, D])
        wq_d = din("peer_wq", [D, 2048])
        pk_d = din("peer_k", [2, 8, 128, 128])
        pu_d = din("peer_u", [16384, D])
        pvv_d = din("peer_v", [16384, D])
        mix = h1T[:].rearrange("p a (b c) -> p (a b) c", c=1024)
        h2T = mT
        with contextlib.ExitStack() as PS_:
            G1S = contextlib.ExitStack()
            g1_bc = sb(G1S, "g1_bc", [128, D], F32)

            def build_bc(stack, items):
                r_dg = Rot(stack, "diag", [128, 128], F32, 2)
                for dst, src, dk in items:
                    for dc in range(8):
                        dg, dgk = r_dg.next()
                        kb.op("dve", lambda e, dg=dg, src=src, dc=dc: e.tensor_scalar(out=dg[:], in0=ident_f[:], scalar1=src[:, dc:dc + 1],
                                                                                     scalar2=None, op0=ALU.mult),
                              reads=["ident_f", "mod", "gf"], writes=[dgk])
                        p = ps_next()
                        kb.op("pe", lambda e, p=p, dg=dg: e.matmul(psb[p][:, 0:128], lhsT=ones_f[:, 0:128], rhs=dg[:], start=True, stop=True),
                              reads=["ones_f", dgk], writes=[("ps", p)])
                        evac_copy(dst[:, dc * 128:(dc + 1) * 128], psb[p][:, 0:128], [("ps", p)], [dk])

            with contextlib.ExitStack() as MS:
                build_bc(MS, [(g1_bc, mod[:, 16:24], "g1_bc")])
                wout = sb(MS, "wout", [128, 8, D], BF16)
                for hf in range(2):
                    kb.dma("pool", wout[:, :, hf * 512:(hf + 1) * 512],
                           wout_d[:, hf * 512:(hf + 1) * 512].rearrange("(kc p) n -> p kc n", p=128), writes=["wout"])
                for i in range(NT):
                    for dd in range(2):
                        p = ps_next()
                        for kc in range(8):
                            kb.op("pe", lambda e, p=p, kc=kc, i=i, dd=dd: e.matmul(
                                psb[p][:, :], lhsT=mT[:, kc, i * 128:(i + 1) * 128], rhs=wout[:, kc, dd * 512:(dd + 1) * 512],
                                start=(kc == 0), stop=(kc == 7)), reads=["wout"] + [("mT", j) for j in range(8)], writes=[("ps", p)])
                        evac_copy(mix[:, i, dd * 512:(dd + 1) * 512], psb[p][:, :], [("ps", p)], ["mix"])
                kb.barrier()
            if "mix" in dbg:
                kb.dma("sp", dbg_out("mix", [128, NT, D], BF16), mix, reads=["mix"])
            kstop("mix")
            with contextlib.ExitStack() as N2:
                r_x1 = Rot(N2, "x1tmp", [128, D], F32, 2)

                def load_x1(i, dst, key):
                    kb.dma("sp", dst[:], x_d[i * 128:(i + 1) * 128, :], writes=[key])
                    tmp, tk = r_x1.next()
                    kb.op("dve", lambda e: e.tensor_tensor(out=tmp[:], in0=mix[:, i, :], in1=g1_bc[:], op=ALU.mult),
                          reads=["mix", "g1_bc"], writes=[tk])
                    kb.op("pool", lambda e: e.tensor_tensor(out=dst[:], in0=dst[:], in1=tmp[:], op=ALU.add),
                          reads=[key, tk], writes=[key])

                norm_mod_T("n2", load_x1, gs2, 24, h2T, "h2T")
            if "h2T" in dbg:
                kb.dma("sp", dbg_out("h2T", [128, 8, S], BF16), h2T[:], reads=["h2T"])
            kstop("h2T")
            kb.barrier()
            G1S.close()
            with contextlib.ExitStack() as PE_:
                k1T = sb(PE_, "k1T", [128, 8, 128], BF16)
                k2T = sb(PE_, "k2T", [128, 8, 128], BF16)
                with contextlib.ExitStack() as KS:
                    kst = sb(KS, "kst", [128, 2, 8, 128], BF16)
                    kb.dma("pool", kst[:], pk_d.rearrange("w h n d -> n w h d"), writes=["kst"])
                    for wi, dst, dk in ((0, k1T, "k1T"), (1, k2T, "k2T")):
                        p = ps_next()
                        pv = psb[p][:].bitcast(BF16)
                        for h in range(8):
                            kb.op("pe", lambda e, pv=pv, h=h, wi=wi: e.transpose(out=pv[:, h * 128:(h + 1) * 128], in_=kst[:, wi, h, :],
                                                                              identity=ident_b[:]), reads=["kst", "ident_b"], writes=[("ps", p)])
                        evac_copy(dst[:], pv[:, 0:1024].rearrange("p (h n) -> p h n", n=128), [("ps", p)], [dk])
                    kb.barrier()
                peer_acc = sb(PE_, "peer_acc", [128, 8, D], F32)
                qT = sb(PE_, "qT", [128, 16, 1024], BF16)
                thr_all = sb(PE_, "thr_all", [128, 8, 8], F32)
                nb_all = sb(PE_, "nb_all", [128, 8, 8], F32)
                Dk = sb(PE_, "Dk", [128, 8, 8, 128], BF16)
                kap = sb(PE_, "kap", [128, 8, 8], F32)
                for hf in range(2):
                    T0 = hf * 8
                    tok0 = T0 * 128
                    QS = contextlib.ExitStack()
                    wqb = Rot(QS, "wqb", [128, 8, 512], BF16, 2)
                    for blk in range(4):
                        w, wk = wqb.next()
                        load_w(w, wk, wq_d[:, blk * 512:(blk + 1) * 512], 512)
                        for mt in range(4):
                            for tc in range(2):
                                p = ps_next()
                                proj_fm(w, wk, mt * 128, 128, h2T, ["h2T"], tok0 + tc * 512, 512, psb[p][:, :], ("ps", p))
                                evac_copy(qT[:, blk * 4 + mt, tc * 512:(tc + 1) * 512], psb[p][:, :], [("ps", p)], [("qT", blk * 4 + mt)])
                    kb.barrier()
                    QS.close()
                    with contextlib.ExitStack() as PR:
                        v12 = sb(PR, "v12", [128, 2, 8, 16], F32)
                        cand = sb(PR, "cand", [128, 8, 16, 16], F32)
                        ctop = sb(PR, "ctop", [128, 8, 24], F32)
                        r_mr = Rot(PR, "mr", [128, 256], F32, 2)
                        sm = sb(PR, "sm", [128, 8, 16], F32)
                        zz = sb(PR, "zz", [128, 4, 8], F32)
                        for il in range(8):
                            ts = slice(il * 128, (il + 1) * 128)
                            for wi, kT_ in ((0, k1T), (1, k2T)):
                                for h in range(8):
                                    p = ps_next()
                                    kb.op("pe", lambda e, p=p, h=h, wi=wi, kT_=kT_: e.matmul(
                                        psb[p][:, 0:128], lhsT=qT[:, h * 2 + wi, ts], rhs=kT_[:, h, :], start=True, stop=True),
                                        reads=[("qT", h * 2 + wi), "k1T", "k2T"], writes=[("ps", p)])
                                    mr, mrk = r_mr.next()
                                    kb.op("dve", lambda e, p=p, h=h, wi=wi: e.max(out=v12[:, wi, h, 0:8], in_=psb[p][:, 0:128]),
                                          reads=[("ps", p)], writes=["v12"])
                                    kb.op("dve", lambda e, p=p, h=h, wi=wi, mr=mr: e.match_replace(
                                        out=mr[:, 0:128], in_to_replace=v12[:, wi, h, 0:8], in_values=psb[p][:, 0:128], imm_value=-1e30),
                                        reads=[("ps", p), "v12"], writes=[mrk])
                                    kb.op("dve", lambda e, h=h, wi=wi, mr=mr: e.max(out=v12[:, wi, h, 8:16], in_=mr[:, 0:128]),
                                          reads=[mrk], writes=["v12"])
                            kb.op("dve", lambda e: e.tensor_tensor(
                                out=cand[:], in0=v12[:, 0, :, :].unsqueeze(3).broadcast_to([128, 8, 16, 16]),
                                in1=v12[:, 1, :, :].unsqueeze(2).broadcast_to([128, 8, 16, 16]), op=ALU.add),
                                reads=["v12"], writes=["cand"])
                            for h in range(8):
                                ch = cand[:, h, :, :].rearrange("p a b -> p (a b)")
                                mr, mrk = r_mr.next()
                                mr2, mr2k = r_mr.next()
                                kb.op("dve", lambda e, h=h, ch=ch: e.max(out=ctop[:, h, 0:8], in_=ch), reads=["cand"], writes=["ctop"])
                                kb.op("dve", lambda e, h=h, ch=ch, mr=mr: e.match_replace(out=mr[:], in_to_replace=ctop[:, h, 0:8], in_values=ch,
                                                                                        imm_value=-1e30), reads=["cand", "ctop"], writes=[mrk])
                                kb.op("dve", lambda e, h=h, mr=mr: e.max(out=ctop[:, h, 8:16], in_=mr[:]), reads=[mrk], writes=["ctop"])
                                kb.op("dve", lambda e, h=h, mr=mr, mr2=mr2: e.match_replace(out=mr2[:], in_to_replace=ctop[:, h, 8:16], in_values=mr[:],
                                                                                          imm_value=-1e30), reads=[mrk, "ctop"], writes=[mr2k])
                                kb.op("dve", lambda e, h=h, mr2=mr2: e.max(out=ctop[:, h, 16:24], in_=mr2[:]), reads=[mr2k], writes=["ctop"])
                            kb.op("dve", lambda e: e.tensor_tensor(out=zz[:, 0, :], in0=ctop[:, :, 15], in1=ctop[:, :, 16], op=ALU.add),
                                  reads=["ctop"], writes=["zz"])
                            kb.op("dve", lambda e, il=il: e.tensor_scalar(out=thr_all[:, il, :], in0=zz[:, 0, :], scalar1=0.5, scalar2=None, op0=ALU.mult),
                                  reads=["zz"], writes=["thr_all"])
                            kb.op("dve", lambda e: e.tensor_tensor(out=sm[:], in0=ctop[:, :, 0:16],
                                                                   in1=ctop[:, :, 0:1].broadcast_to([128, 8, 16]), op=ALU.subtract),
                                  reads=["ctop"], writes=["sm"])
                            kb.op("act", lambda e: e.activation(out=sm[:], in_=sm[:], func=AF.Exp), reads=["sm"], writes=["sm"])
                            kb.op("dve", lambda e: e.tensor_reduce(out=zz[:, 1, :], in_=sm[:], axis=mybir.AxisListType.X, op=ALU.add),
                                  reads=["sm"], writes=["zz"])
                            kb.op("act", lambda e: e.activation(out=zz[:, 2, :], in_=zz[:, 1, :], func=AF.Ln), reads=["zz"], writes=["zz"])
                            kb.op("dve", lambda e: e.tensor_tensor(out=zz[:, 3, :], in0=zz[:, 2, :], in1=ctop[:, :, 0], op=ALU.add),
                                  reads=["zz", "ctop"], writes=["zz"])
                            kb.op("dve", lambda e, il=il: e.tensor_scalar(out=nb_all[:, il, :], in0=zz[:, 3, :], scalar1=-1.0, scalar2=None, op0=ALU.mult),
                                  reads=["zz"], writes=["nb_all"])
                            kb.op("dve", lambda e, il=il: e.tensor_tensor(out=kap[:, il, :], in0=thr_all[:, il, :], in1=nb_all[:, il, :], op=ALU.add),
                                  reads=["thr_all", "nb_all"], writes=["kap"])
                            kb.op("act", lambda e, il=il: e.activation(out=kap[:, il, :], in_=kap[:, il, :], func=AF.Exp), reads=["kap"], writes=["kap"])
                            for h in range(8):
                                kb.op("dve", lambda e, il=il, h=h: e.tensor_scalar(out=Dk[:, il, h, :], in0=ident_b[:], scalar1=kap[:, il, h:h + 1],
                                                                                   scalar2=None, op0=ALU.mult),
                                      reads=["ident_b", "kap"], writes=["Dk"])
                        kb.barrier()
                    EL = contextlib.ExitStack()
                    uT = Rot(EL, "uT", [128, 8, 512], BF16, 1)
                    ust = Rot(EL, "ust", [128, D], BF16, 2)
                    vbk = Rot(EL, "vbk", [128, 4, D], BF16, 1)
                    r_s1 = Rot(EL, "s1sb", [128, 8, 4], F32, 2)
                    r_aT4 = Rot(EL, "aT4", [128, 4, 512], BF16, 2)
                    sh_rr = [0]

                    def sh_next():
                        i = sh_rr[0]
                        sh_rr[0] = (i + 1) % 2
                        return 4 + i
                    r_sum = Rot(EL, "sum", [128, 4, 4, 128], BF16, 2)
                    r_E = Rot(EL, "E", [128, 4, 4, 128], BF16, 2)
                    r_G = Rot(EL, "G", [128, 4, 4, 128], BF16, 2)
                    r_wT = Rot(EL, "wT", [128, 4, 128], BF16, 2)
                    kb.op("pool", lambda e: e.memset(peer_acc[:], 0.0), writes=["peer_acc"])
                    import os
                    NBLK = int(os.environ.get("KNBLK", "32"))
                    ut, utk = uT.next()

                    def emit_u_dma(b, c):
                        e0 = b * 512
                        us, usk = ust.next()
                        kb.dma("pool", us[:], pu_d[e0 + c * 128:e0 + (c + 1) * 128, :], writes=[usk])
                        return us, usk

                    def emit_u_tr(c, us, usk):
                        p = sh_next()
                        pv = psb[p][:].bitcast(BF16)
                        for kc in range(8):
                            kb.op("pe", lambda e, pv=pv, kc=kc: e.transpose(out=pv[:, kc * 128:(kc + 1) * 128],
                                                                          in_=us[:, kc * 128:(kc + 1) * 128], identity=ident_b[:]),
                                  reads=[usk, "ident_b"], writes=[("ps", p)])
                        kb.op("act", lambda e: e.copy(out=ut[:, :, c * 128:(c + 1) * 128], in_=pv[:, 0:1024].rearrange("p (k e) -> p k e", e=128)),
                              reads=[("ps", p)], writes=[(utk, c)])

                    def emit_v(b):
                        e0 = b * 512
                        vb, vbk_ = vbk.next()
                        kb.dma("pool", vb[:], pvv_d[e0:e0 + 512, :].rearrange("(c p) d -> p c d", p=128), writes=[vbk_])
                        return vb, vbk_

                    def emit_aT4_chunk_pe(aT, g4, c, bank):
                        aT4, aT4k = aT
                        for kc in range(8):
                            kb.op("pe", lambda e, kc=kc: e.matmul(
                                psb[bank][:, :], lhsT=ut[:, kc, c * 128:(c + 1) * 128],
                                rhs=h2T[:, kc, tok0 + g4 * 512:tok0 + (g4 + 1) * 512], start=(kc == 0), stop=(kc == 7)),
                                reads=[(utk, c), "h2T"], writes=[("ps", bank)])

                        def gelu():
                            kb.op("act", lambda e: e.activation(out=aT4[:, c, :], in_=psb[bank][:, :], func=AF.Gelu),
                                  reads=[("ps", bank)], writes=[(aT4k, c)])
                        return gelu

                    class Job:
                        pass

                    def make_job(b, il, aT4, aT4k, vb, vbk_):
                        J = Job()
                        ts = slice(il * 128, (il + 1) * 128)
                        q = il % 4
                        st = {}

                        def A_pe():
                            p1 = sh_next()
                            st["p1"] = p1
                            for h in range(8):
                                kb.op("pe", lambda e, h=h: e.matmul(psb[p1][:, h * 4:(h + 1) * 4], lhsT=qT[:, h * 2, ts],
                                                                    rhs=k1T[:, h, 4 * b:4 * b + 4], start=True, stop=True),
                                      reads=[("qT", h * 2), "k1T"], writes=[("ps", p1)])
                            p2s = [(il % 2) * 2, (il % 2) * 2 + 1]
                            st["p2s"] = p2s
                            for hg in range(2):
                                for hh in range(4):
                                    h = hg * 4 + hh
                                    kb.op("pe", lambda e, h=h, hh=hh, hg=hg: e.matmul(psb[p2s[hg]][:, hh * 128:(hh + 1) * 128], lhsT=qT[:, h * 2 + 1, ts],
                                                                                    rhs=k2T[:, h, :], start=True, stop=True),
                                          reads=[("qT", h * 2 + 1), "k2T"], writes=[("ps", p2s[hg])])

                        def A_dve():
                            p1 = st["p1"]
                            s1, s1k = r_s1.next()
                            st["s1"] = (s1, s1k)
                            kb.op("dve", lambda e: e.tensor_tensor(
                                out=s1[:], in0=psb[p1][:, 0:32].rearrange("p (h i) -> p h i", i=4),
                                in1=thr_all[:, il, :].unsqueeze(2).broadcast_to([128, 8, 4]), op=ALU.subtract),
                                reads=[("ps", p1), "thr_all"], writes=[s1k])

                        def B1_sums():
                            s1, s1k = st["s1"]
                            sms = []
                            for hg in range(2):
                                p2 = st["p2s"][hg]
                                sm_, smk = r_sum.next()
                                kb.op("dve", lambda e: e.tensor_tensor(
                                    out=sm_[:], in0=s1[:, hg * 4:(hg + 1) * 4, :].unsqueeze(3).broadcast_to([128, 4, 4, 128]),
                                    in1=psb[p2][:, :].rearrange("p (h j) -> p h j", j=128).unsqueeze(2).broadcast_to([128, 4, 4, 128]), op=ALU.add),
                                    reads=[s1k, ("ps", p2)], writes=[smk])
                                sms.append((sm_, smk))
                            Es = []
                            for hg in range(2):
                                sm_, smk = sms[hg]
                                E_, Ek = r_E.next()
                                kb.op("act", lambda e: e.activation(out=E_[:], in_=sm_[:], func=AF.Exp), reads=[smk], writes=[Ek])
                                Es.append((E_, Ek))
                            st["sms"] = sms
                            st["Es"] = Es

                        def B1_stt():
                            Gs = []
                            for hg in range(2):
                                sm_, smk = st["sms"][hg]
                                E_, Ek = st["Es"][hg]
                                G_, Gk = r_G.next()
                                kb.op("dve", lambda e: e.scalar_tensor_tensor(out=G_[:].rearrange("p a b c -> p (a b) c"),
                                                                              in0=sm_[:].rearrange("p a b c -> p (a b) c"), scalar=0.0,
                                                                              in1=E_[:].rearrange("p a b c -> p (a b) c"),
                                                                              op0=ALU.is_gt, op1=ALU.mult),
                                      reads=[smk, Ek], writes=[Gk])
                                Gs.append((G_, Gk))
                            st["Gs"] = Gs

                        def B1_gt():
                            pg = ps_next_acc()
                            st["pg"] = pg
                            for hg in range(2):
                                G_, Gk = st["Gs"][hg]
                                for hh in range(4):
                                    h = hg * 4 + hh
                                    for c in range(4):
                                        kb.op("pe", lambda e, c=c, h=h, hh=hh: e.matmul(psb[pg][:, c * 128:(c + 1) * 128], lhsT=G_[:, hh, c, :],
                                                                                      rhs=Dk[:, il, h, :], start=(h == 0 and c == 0),
                                                                                      stop=True, skip_group_check=True),
                                              reads=[Gk, "Dk"], writes=[("ps", pg)])

                        def B2_wT():
                            pg = st["pg"]
                            wT, wTk = r_wT.next()
                            st["wT"] = (wT, wTk)
                            kb.op("dve", lambda e: e.tensor_tensor(out=wT[:], in0=aT4[:, :, q * 128:(q + 1) * 128],
                                                                   in1=psb[pg][:, :].rearrange("p (c t) -> p c t", t=128),
                                                                   op=ALU.mult), reads=[(aT4k, cc) for cc in range(4)] + [("ps", pg)], writes=[wTk])

                        def B2_peer_acc():
                            wT, wTk = st["wT"]
                            pps = []
                            for dd in range(2):
                                pp = sh_next()
                                pps.append(pp)
                                for c in range(4):
                                    kb.op("pe", lambda e, c=c, dd=dd, pp=pp: e.matmul(psb[pp][:, :], lhsT=wT[:, c, :], rhs=vb[:, c, dd * 512:(dd + 1) * 512],
                                                                                    start=(c == 0), stop=(c == 3)),
                                          reads=[wTk, vbk_], writes=[("ps", pp)])
                            for dd in range(2):
                                pp = pps[dd]
                                kb.op("dve", lambda e, dd=dd, pp=pp: e.tensor_tensor(out=peer_acc[:, il, dd * 512:(dd + 1) * 512],
                                                                                     in0=peer_acc[:, il, dd * 512:(dd + 1) * 512], in1=psb[pp][:, :],
                                                                                     op=ALU.add), reads=[("pacc", il), ("ps", pp), "peer_acc"],
                                      writes=[("pacc", il)])

                        J.A_pe, J.A_dve, J.B1_sums, J.B1_stt, J.B1_gt, J.B2_wT, J.B2_peer_acc = A_pe, A_dve, B1_sums, B1_stt, B1_gt, B2_wT, B2_peer_acc
                        return J

                    for c in range(4):
                        us, usk = emit_u_dma(0, c)
                        emit_u_tr(c, us, usk)
                    vb, vbk_ = emit_v(0)
                    aT_cur = [r_aT4.next(), None]
                    for c in range(4):
                        emit_aT4_chunk_pe(aT_cur[0], 0, c, sh_next())()
                    aT_next = None
                    jobs = {}
                    NSTEP = NBLK * 8
                    u_pend = {}
                    pend_gelu = []
                    for n in range(NSTEP + 3):
                        b, q8 = divmod(n, 8)
                        if n < NSTEP:
                            if q8 == 1:
                                aT_cur[1] = r_aT4.next()
                            if q8 == 0 and b > 0:
                                aT_cur[0] = aT_next
                            aT4, aT4k = aT_cur[q8 // 4]
                            jobs[n] = make_job(b, q8, aT4, aT4k, vb, vbk_)
                        if (n - 1) in jobs:
                            jobs[n - 1].B1_sums()
                        while pend_gelu:
                            pend_gelu.pop(0)()
                        if (n - 3) in jobs:
                            jobs[n - 3].B2_peer_acc()
                            del jobs[n - 3]
                        if n < NSTEP and q8 == 2 and b > 0:
                            vb, vbk_ = emit_v(b)
                        if n < NSTEP:
                            jobs[n].A_pe()
                            jobs[n].A_dve()
                        if (n - 2) in jobs:
                            jobs[n - 2].B2_wT()
                        if n < NSTEP:
                            bank = 6 + (1 - ps_acc[0]) if (n - 1) in jobs else 6 + ps_acc[0]
                            if 1 <= q8 <= 4:
                                pend_gelu.append(emit_aT4_chunk_pe(aT_cur[1], 1, q8 - 1, bank))
                            elif b + 1 < NBLK and q8 >= 5:
                                if q8 == 5:
                                    aT_next = r_aT4.next()
                                pend_gelu.append(emit_aT4_chunk_pe(aT_next, 0, q8 - 5, bank))
                            elif b > 0 and q8 == 0:
                                pend_gelu.append(emit_aT4_chunk_pe(aT_cur[0], 0, 3, bank))
                            if b + 1 < NBLK:
                                if 3 <= q8 <= 6:
                                    emit_u_tr(q8 - 3, *u_pend[q8 - 3])
                                if 1 <= q8 <= 4:
                                    u_pend[q8 - 1] = emit_u_dma(b + 1, q8 - 1)
                        if (n - 1) in jobs:
                            jobs[n - 1].B1_stt()
                            jobs[n - 1].B1_gt()
                    while pend_gelu:
                        pend_gelu.pop(0)()
                    if "peer" in dbg:
                        kb.dma("sp", dbg_out("peer%d" % hf, [128, 8, D]), peer_acc[:], reads=[("pacc", il) for il in range(8)] + ["peer_acc"])
                    kb.barrier()
                    EL.close()
                    with contextlib.ExitStack() as EP:
                        g1_bc = sb(EP, "g1_bc", [128, D], F32)
                        g2_bc = sb(EP, "g2_bc", [128, D], F32)
                        gf_bc = sb(EP, "gf_bc", [128, D], F32)
                        build_bc(EP, [(g1_bc, mod[:, 16:24], "g1_bc"), (g2_bc, mod[:, 40:48], "g2_bc"), (gf_bc, gf[:, 0:8], "gf_bc")])
                        r_x2 = Rot(EP, "x2", [128, D], F32, 2)
                        r_t3 = Rot(EP, "x2u", [128, D], F32, 2)
                        r_t2 = Rot(EP, "x2t", [128, D], F32, 2)
                        r_jk = Rot(EP, "jk", [128, D], BF16, 1)
                        r_st = Rot(EP, "st", [128, 4], F32, 2)
                        for il in range(8):
                            i = T0 + il
                            x2, x2k = r_x2.next()
                            t2, t2k = r_t2.next()
                            st, stk = r_st.next()
                            jk, jkk = r_jk.next()
                            kb.dma("sp", x2[:], x_d[i * 128:(i + 1) * 128, :], writes=[x2k])
                            kb.op("dve", lambda e: e.tensor_tensor(out=t2[:], in0=mix[:, i, :], in1=g1_bc[:], op=ALU.mult),
                                  reads=["mix", "g1_bc"], writes=[t2k])
                            t3, t3k = r_t3.next()
                            kb.op("pool", lambda e: e.tensor_tensor(out=t3[:], in0=peer_acc[:, il, :], in1=g2_bc[:], op=ALU.mult),
                                  reads=[("pacc", il), "peer_acc", "g2_bc"], writes=[t3k])
                            kb.op("pool", lambda e: e.tensor_tensor(out=t2[:], in0=t2[:], in1=t3[:], op=ALU.add),
                                  reads=[t2k, t3k], writes=[t2k])
                            kb.op("pool", lambda e: e.tensor_tensor(out=x2[:], in0=x2[:], in1=t2[:], op=ALU.add), reads=[x2k, t2k], writes=[x2k])
                            kb.op("act", lambda e: e.activation(out=jk[:], in_=x2[:], func=AF.Square, accum_out=st[:, 0:1]),
                                  reads=[x2k], writes=[stk, jkk])
                            kb.op("dve", lambda e: e.tensor_scalar(out=st[:, 1:2], in0=st[:, 0:1], scalar1=1.0 / D, scalar2=EPS,
                                                                   op0=ALU.mult, op1=ALU.add), reads=[stk], writes=[stk])
                            kb.op("act", lambda e: e.activation(out=st[:, 2:3], in_=st[:, 1:2], func=AF.Sqrt), reads=[stk], writes=[stk])
                            kb.op("dve", lambda e: e.reciprocal(out=st[:, 3:4], in_=st[:, 2:3]), reads=[stk], writes=[stk])
                            kb.op("dve", lambda e: e.scalar_tensor_tensor(out=t2[:], in0=x2[:], scalar=st[:, 3:4], in1=gf_bc[:],
                                                                          op0=ALU.mult, op1=ALU.mult), reads=[x2k, stk, "gf_bc"], writes=[t2k])
                            OUT_TOKS.append(kb.dma("sp", out_d[i * 128:(i + 1) * 128, :], t2[:], reads=[t2k]))
                        kb.barrier()
                kb.barrier()

    try:
        stages()
    except StopBuild:
        pass
    toks = []
    if "mod" in dbg:
        toks.append(kb.dma("sp", dbg_out("mod", [128, 48]), mod[:], reads=["mod"]))
    if "h1T" in dbg:
        o = dbg_out("h1T", [128, 8, S], BF16)
        toks.append(kb.dma("sp", o, h1T[:], reads=["h1T"]))
    kb.barrier()
    for e in ("sp",):
        for q in kb.dsem:
            for i in range(kb.nslot):
                if kb.dval[q][i] > 0:
                    kb._wait(e, (kb.dsem[q][i], kb.dval[q][i], "dma"))
    return kb


_CONST_CACHE = {}


def nsa_consts(inputs, b):
    f = np.float32
    m = {}
    m["pos"] = np.ascontiguousarray(np.asarray(inputs["positions"][b], dtype=np.int32).reshape(1, S))
    w_in = np.asarray(inputs["w_in"][0], dtype=f)
    rot = (np.arange(16) + 8) % 16
    m["w_qrot"] = np.ascontiguousarray(np.concatenate([w_in[:, 3088 + hd * 64 + rot] for hd in range(16)], axis=1))
    cols = []
    for base in (4112, 4368, 4624):
        for g in range(2):
            cols.append(w_in[:, base + g * 64 + rot])
    m["w_krot"] = np.ascontiguousarray(np.concatenate(cols, axis=1))
    w1 = np.stack([np.asarray(inputs["nsa_ck_w1"][0], dtype=f), np.asarray(inputs["nsa_cv_w1"][0], dtype=f)])
    m["cmp_w1"] = np.ascontiguousarray(w1.reshape(2, 32, 64, 64).transpose(0, 2, 1, 3).reshape(2, 64, 2048))
    m["cmp_w2"] = np.ascontiguousarray(np.stack([np.asarray(inputs["nsa_ck_w2"][0], dtype=f), np.asarray(inputs["nsa_cv_w2"][0], dtype=f)]))
    pe = np.stack([np.asarray(inputs["nsa_pe_k"][0], dtype=f), np.asarray(inputs["nsa_pe_v"][0], dtype=f)])
    m["cmp_peT"] = np.ascontiguousarray(pe.transpose(0, 2, 1))
    m["w_branch_b"] = np.ascontiguousarray(inputs["w_branch_b"][0], dtype=f)
    if "c" not in _CONST_CACHE:
        c = {}
        inv = (500000.0 ** (-np.arange(8, dtype=np.float32) * 2.0 / 16)).astype(f)
        rc = np.zeros((16, 4), dtype=f)
        rc[:, 0] = np.concatenate([inv, inv])
        rc[:, 1] = np.pi / 2
        rc[0:8, 2] = np.pi
        rc[8:16, 2] = 0.0
        c["rope_c"] = rc
        n = np.arange(127)[:, None]
        j = np.arange(32)[None, :]
        c["ovl"] = ((n * 16 < (j + 1) * 64) & (n * 16 + 32 > j * 64)).astype(f)
        t = np.arange(S)[None, :]
        c["cmpbias"] = np.where(n * 16 + 31 <= t, 0.0, -30000.0).astype(f)
        k = np.arange(S)[None, :]
        c["blockind"] = (k // 64 == np.arange(32)[:, None]).astype(f)
        kk = np.arange(128)[:, None]
        tt = np.arange(128)[None, :]
        c["causalbias"] = np.where(kk > tt, -30000.0, 0.0).astype(f)
        c["antibias"] = np.where(kk <= tt, -30000.0, 0.0).astype(f)
        tok = (np.arange(NT)[None, :, None] * 128 + np.arange(128)[:, None, None])
        cur = tok // 64
        jj = np.arange(32)[None, None, :]
        c["Msel"] = ((jj >= 1) & (jj <= cur - 2)).astype(f)
        c["Fsel"] = ((jj == 0) | (jj == cur) | (jj == cur - 1)).astype(f)
        _CONST_CACHE["c"] = c
    m.update(_CONST_CACHE["c"])
    return m


def host_layout(inputs, b):
    f = np.float32
    m = {}
    m["x"] = np.ascontiguousarray(inputs["x"][b], dtype=f)
    m["c_l"] = np.ascontiguousarray(np.asarray(inputs["c"][b], dtype=f).reshape(8, 128).T)
    m["ada_w"] = np.ascontiguousarray(inputs["ada_w"][0], dtype=f)
    m["ada_b_l"] = np.ascontiguousarray(np.asarray(inputs["ada_b"][0], dtype=f).reshape(48, 128).T)
    m["g1_l"] = np.ascontiguousarray(np.asarray(inputs["norm1_g"][0], dtype=f).reshape(8, 128).T)
    m["g2_l"] = np.ascontiguousarray(np.asarray(inputs["norm2_g"][0], dtype=f).reshape(8, 128).T)
    m["gf_l"] = np.ascontiguousarray(np.asarray(inputs["final_g"], dtype=f).reshape(8, 128).T)
    m["ident"] = np.eye(128, dtype=f)
    m["LT"] = np.triu(np.ones((128, 128), dtype=f))
    m["SU"] = np.tril(np.ones((128, 128), dtype=f), -1)
    m["w_in"] = np.ascontiguousarray(inputs["w_in"][0], dtype=f)
    m["wa2"] = np.ascontiguousarray(inputs["gla_wa2"][0], dtype=f)
    m["ba2"] = np.ascontiguousarray(np.asarray(inputs["gla_ba2"][0], dtype=f).reshape(1, 512))
    m["gng_l"] = np.ascontiguousarray(np.asarray(inputs["gla_norm_g"][0], dtype=f).reshape(2, 128).T)
    m.update(nsa_consts(inputs, b))
    m["w_out"] = np.ascontiguousarray(inputs["w_out"][0], dtype=f)
    m["peer_wq"] = np.ascontiguousarray(inputs["peer_wq"][0], dtype=f)
    m["peer_k"] = np.ascontiguousarray(np.stack([np.asarray(inputs["peer_k1"][0], dtype=f), np.asarray(inputs["peer_k2"][0], dtype=f)]))
    m["peer_u"] = np.ascontiguousarray(inputs["peer_u"][0], dtype=f)
    m["peer_v"] = np.ascontiguousarray(inputs["peer_v"][0], dtype=f)
    m["w_branch_a"] = np.ascontiguousarray(inputs["w_branch_a"][0], dtype=f)
    return m


def run(inputs, dbg=(), ncores=8):
    kb = build(dbg)
    in_maps = [host_layout(inputs, b) for b in range(ncores)]
    res = run_bass_kernel_spmd(kb.nc, in_maps, core_ids=list(range(ncores)))
    return res.results


def kernel(**inputs):
    res = run(inputs)
    return np.stack([np.asarray(r["out"], dtype=np.float32) for r in res], axis=0)
```
